# Optimizing a Trainium2 kernel written in Bass

```python
import jax, jax.numpy as jnp
from jax import lax
import numpy as np

D_MODEL = 1024
BATCH = 4
SEQ = 8192
DEPTH = 2

N_META = 16
GLA_HEADS = 4
GLA_DK = D_MODEL // 2
GLA_DV = D_MODEL
GLA_HK = GLA_DK // GLA_HEADS
GLA_HV = GLA_DV // GLA_HEADS
GLA_RANK = 16
GLA_TAU = 16.0
CHUNK = 64
POOL_WINDOWS = (2, 4, 8, 16)
POOL_GROUPS = 4
POOL_DIM = D_MODEL
POOL_GDIM = POOL_DIM // POOL_GROUPS
D_FF = 2816
CONV_W = 3
EPS = 1e-6
IN_WIDTH = 2 * GLA_DK + 2 * GLA_DV + GLA_RANK + POOL_DIM + 2 * D_MODEL

kernel_name = "hybrid_gla_pool_gated_block"


def rmsnorm(x, g):
    xf = x.astype(jnp.float32)
    y = xf * lax.rsqrt(jnp.mean(xf * xf, axis=-1, keepdims=True) + EPS)
    return (y * g.astype(jnp.float32)).astype(x.dtype)


def gla_chunked(q, k, v, log_a):
    B, L, H, DKH = q.shape
    DVH = v.shape[-1]
    pad = CHUNK - N_META
    n_chunks = (L + pad) // CHUNK

    def to_chunks(t):
        t = jnp.pad(t.astype(jnp.float32), ((0, 0), (pad, 0), (0, 0), (0, 0)))
        return t.reshape(B, n_chunks, CHUNK, H, t.shape[-1]).transpose(1, 0, 3, 2, 4)

    q, k, v, g = (to_chunks(t) for t in (q, k, v, log_a))
    b = jnp.cumsum(g, axis=3)
    b_last = b[:, :, :, -1:, :]
    q_dec = q * jnp.exp(b)
    k_inv = k * jnp.exp(-b)
    k_end = k * jnp.exp(b_last - b)
    causal = jnp.tril(jnp.ones((CHUNK, CHUNK), dtype=bool))
    att = jnp.where(causal, jnp.einsum('nbhcd,nbhsd->nbhcs', q_dec, k_inv), 0.0)
    o_intra = jnp.einsum('nbhcs,nbhse->nbhce', att, v)

    def step(state, inp):
        q_c, k_c, v_c, dec_c = inp
        o_c = jnp.einsum('bhcd,bhde->bhce', q_c, state)
        state = dec_c[..., None] * state + jnp.einsum('bhsd,bhse->bhde', k_c, v_c)
        return state, o_c

    s0 = jnp.zeros((B, H, DKH, DVH), jnp.float32)
    _, o_inter = lax.scan(step, s0, (q_dec, k_end, v, jnp.exp(b_last[:, :, :, 0, :])))
    o = (o_intra + o_inter).transpose(1, 0, 3, 2, 4).reshape(B, n_chunks * CHUNK, H, DVH)
    return o[:, pad:]


def multiscale_pool(u):
    B, L, _ = u.shape
    ug = u.astype(jnp.float32).reshape(B, L, POOL_GROUPS, POOL_GDIM)
    csp = jnp.pad(jnp.cumsum(ug, axis=1), ((0, 0), (1, 0), (0, 0), (0, 0)))
    pos = jnp.arange(L)
    outs = []
    for gi, w in enumerate(POOL_WINDOWS):
        c = csp[:, :, gi]
        lagged = jnp.pad(c[:, :L - w + 1], ((0, 0), (w - 1, 0), (0, 0)))
        cnt = jnp.minimum(pos + 1, w).astype(jnp.float32)[None, :, None]
        outs.append((c[:, 1:] - lagged) / cnt)
    return jnp.stack(outs, axis=2) - ug


def causal_dwconv(h, w, b):
    C = h.shape[-1]
    out = lax.conv_general_dilated(h, w[:, None, :].astype(h.dtype), window_strides=(1,),
                                   padding=((CONV_W - 1, 0),),
                                   dimension_numbers=('NWC', 'WIO', 'NWC'),
                                   feature_group_count=C)
    return out + b


def setup_inputs(seed: int = 0) -> dict:
    key = jax.random.key(seed)
    ks = jax.random.split(key, 20)
    nrm = lambda k, shape, s: jax.random.normal(k, shape, jnp.float32) * s
    F2 = 2 * D_FF
    return {
        'x': nrm(ks[0], (BATCH, SEQ, D_MODEL), 1.0),
        'meta_tokens': nrm(ks[1], (N_META, D_MODEL), 1.0),
        'norm1_g': 1.0 + nrm(ks[2], (DEPTH, D_MODEL), 0.02),
        'w_in': nrm(ks[3], (DEPTH, D_MODEL, IN_WIDTH), D_MODEL ** -0.5),
        'w_gk': nrm(ks[4], (DEPTH, GLA_RANK, GLA_DK), GLA_RANK ** -0.5),
        'b_gk': nrm(ks[5], (DEPTH, GLA_DK), 0.1),
        'gla_norm_g': 1.0 + nrm(ks[6], (DEPTH, GLA_HV), 0.02),
        'w_a': nrm(ks[7], (DEPTH, GLA_DV, D_MODEL), GLA_DV ** -0.5),
        'w_pool_grp': nrm(ks[8], (DEPTH, POOL_GROUPS, POOL_GDIM, POOL_GDIM), POOL_GDIM ** -0.5),
        'pool_scale': 1.0 + nrm(ks[9], (DEPTH, POOL_DIM), 0.02),
        'w_b': nrm(ks[10], (DEPTH, POOL_DIM, D_MODEL), POOL_DIM ** -0.5),
        'b_gates': nrm(ks[11], (DEPTH, 2 * D_MODEL), 0.02),
        'w_o': nrm(ks[12], (DEPTH, D_MODEL, D_MODEL), D_MODEL ** -0.5),
        'norm2_g': 1.0 + nrm(ks[13], (DEPTH, D_MODEL), 0.02),
        'w_up': nrm(ks[14], (DEPTH, D_MODEL, F2), D_MODEL ** -0.5),
        'conv_w': nrm(ks[15], (DEPTH, CONV_W, F2), CONV_W ** -0.5),
        'conv_b': nrm(ks[16], (DEPTH, F2), 0.02),
        'w_down': nrm(ks[17], (DEPTH, D_FF, D_MODEL), D_FF ** -0.5),
        'final_norm_g': 1.0 + nrm(ks[18], (D_MODEL,), 0.02),
    }


def reference(x, meta_tokens, norm1_g, w_in, w_gk, b_gk, gla_norm_g, w_a, w_pool_grp, pool_scale,
              w_b, b_gates, w_o, norm2_g, w_up, conv_w, conv_b, w_down, final_norm_g):
    B = x.shape[0]
    dt = x.dtype
    meta = jnp.broadcast_to(meta_tokens.astype(dt)[None], (B, N_META, D_MODEL))
    h = jnp.concatenate([meta, x], axis=1)
    L = h.shape[1]
    sizes = (GLA_DK, GLA_DK, GLA_DV, GLA_RANK, GLA_DV, POOL_DIM, D_MODEL, D_MODEL)
    splits = np.cumsum(sizes)[:-1].tolist()

    for l in range(DEPTH):
        hn = rmsnorm(h, norm1_g[l])
        p = hn @ w_in[l]
        q, k, v, glr, r, u, ga, gb = jnp.split(p, splits, axis=-1)
        log_a = jax.nn.log_sigmoid((glr @ w_gk[l] + b_gk[l]).astype(jnp.float32)) / GLA_TAU
        q = q.reshape(B, L, GLA_HEADS, GLA_HK) * (GLA_HK ** -0.5)
        k = k.reshape(B, L, GLA_HEADS, GLA_HK)
        v = v.reshape(B, L, GLA_HEADS, GLA_HV)
        log_a = log_a.reshape(B, L, GLA_HEADS, GLA_HK)
        o = gla_chunked(q, k, v, log_a)
        o = rmsnorm(o, gla_norm_g[l]).reshape(B, L, GLA_DV).astype(dt)
        y_a = (o * jax.nn.silu(r)) @ w_a[l]
        pooled = multiscale_pool(u).astype(dt)
        y_b = jnp.einsum('blgc,gcd->blgd', pooled, w_pool_grp[l]).reshape(B, L, POOL_DIM)
        y_b = (y_b * pool_scale[l]) @ w_b[l]
        gate_a = jax.nn.sigmoid(ga + b_gates[l, :D_MODEL])
        gate_b = jax.nn.sigmoid(gb + b_gates[l, D_MODEL:])
        h = h + (gate_a * y_a + gate_b * y_b) @ w_o[l]
        hn = rmsnorm(h, norm2_g[l])
        up = causal_dwconv(hn @ w_up[l], conv_w[l], conv_b[l])
        a, bv = jnp.split(up, 2, axis=-1)
        h = h + (jax.nn.silu(a) * bv) @ w_down[l]

    return rmsnorm(h, final_norm_g)[:, N_META:]
```

```python
from contextlib import ExitStack
import os
STOP = int(os.environ.get('MK_STOP', '99'))
import numpy as np
import concourse.bass as bass
import concourse.mybir as mybir
from concourse.bass_utils import run_bass_kernel_spmd

F32 = mybir.dt.float32
BF16 = mybir.dt.bfloat16
AF = mybir.ActivationFunctionType
ALU = mybir.AluOpType

D = 1024
NMETA = 16
SEQ = 8192
BATCH = 4
PAD = 112
NPOS = PAD + NMETA + SEQ
INW = 6160
DFF = 2816
F2 = 2 * DFF
EPS = 1e-6
TT = 512
FULL_TILES = [(0, 128)] + [(128 + TT * i, TT) for i in range(SEQ // TT)]
NV = 226
NCONST = 896
NSLOT = 4
NDS = 8

ENGS = ("tensor", "vector", "scalar", "gpsimd", "sync")


class Buf:
    __slots__ = ("name", "w", "r", "excl")

    def __init__(self, name, excl=False):
        self.name = name
        self.w = None
        self.r = {}
        self.excl = excl


class TB:
    def __init__(self, t, name, nparts=1):
        self.t = t
        self.p = [Buf(f"{name}.{i}") for i in range(nparts)]

    @property
    def all(self):
        return self.p


class Prog:
    def __init__(self, nc, es):
        self.nc = nc
        self.q = {e: [] for e in ENGS}
        self.cnt = {e: 0 for e in ENGS}
        self.sem = {e: es.enter_context(nc.semaphore("s_" + e)) for e in ENGS}
        self.waited = {e: {} for e in ENGS}
        self.dsem = {e: [es.enter_context(nc.semaphore(f"d_{e}{i}")) for i in range(NDS)]
                     for e in ("sync", "gpsimd", "scalar")}
        self.dcnt = {e: [0] * NDS for e in self.dsem}
        self.dnext = {e: 0 for e in self.dsem}
        self.semname = {}

    def _wait(self, eng, sv):
        sem, val, key, src = sv
        if self.waited[eng].get(key, 0) >= val:
            return
        self.waited[eng][key] = val
        self.q[eng].append(lambda E, sem=sem, val=val: E.wait_ge(sem, val))

    def _deps(self, eng, reads, writes):
        for b in reads:
            if b.w is not None:
                if not (eng == "tensor" and b.w[3] == "tensor"):
                    self._wait(eng, b.w)
            if b.excl:
                for r in b.r.values():
                    if r[3] != eng:
                        self._wait(eng, r)
        for b in writes:
            if b.w is not None:
                if not (eng == "tensor" and b.w[3] == "tensor"):
                    self._wait(eng, b.w)
            for r in b.r.values():
                if r[3] == eng and eng != "dma":
                    continue
                self._wait(eng, r)

    def _record(self, rec, reads, writes):
        for b in reads:
            b.r[rec[2]] = rec
        for b in writes:
            b.w = rec
            b.r = {}

    def op(self, eng, fn, reads=(), writes=()):
        self._deps(eng, reads, writes)
        self.cnt[eng] += 1
        c = self.cnt[eng]
        sem = self.sem[eng]
        self.q[eng].append(lambda E, fn=fn, sem=sem: fn(E).then_inc(sem, 1))
        self._record((sem, c, "e_" + eng, eng), reads, writes)

    def pe(self, fns, reads=(), writes=()):
        eng = "tensor"
        self._deps(eng, reads, writes)
        self.cnt[eng] += 1
        c = self.cnt[eng]
        sem = self.sem[eng]
        for f in fns[:-1]:
            self.q[eng].append(lambda E, f=f: f(E))
        self.q[eng].append(lambda E, f=fns[-1], sem=sem: f(E).then_inc(sem, 1))
        self._record((sem, c, "e_tensor", eng), reads, writes)

    def dma(self, eng, out, in_, reads=(), writes=()):
        i = self.dnext[eng]
        self.dnext[eng] = (i + 1) % NDS
        sem = self.dsem[eng][i]
        key = f"d_{eng}{i}"
        if self.dcnt[eng][i] > 0:
            self._wait(eng, (sem, self.dcnt[eng][i], key, "dma"))
        self._deps(eng, reads, writes)
        self.dcnt[eng][i] += 16
        v = self.dcnt[eng][i]
        self.q[eng].append(lambda E, out=out, in_=in_, sem=sem: E.dma_start(out=out, in_=in_).then_inc(sem, 16))
        rec = (sem, v, key, "dma")
        self._record(rec, reads, writes)
        return rec

    def final_wait(self, eng, rec):
        self._wait(eng, rec)

    def emit(self, block):
        for e in ENGS:
            def mk(fl):
                def _(E):
                    for f in fl:
                        f(E)
                return _
            getattr(block, e)(mk(self.q[e]))


def build_program(tiles, npos):
    nc = bass.Bass("TRN2", target_bir_lowering=False)
    es = ExitStack()
    dram = lambda n, s, dt, kind: nc.dram_tensor(n, s, dt, kind=kind).ap()
    xT = dram("xT", [D, npos], F32, "ExternalInput")
    w_in = dram("w_in", [D, INW], F32, "ExternalInput")
    w_a = dram("w_a", [D, D], F32, "ExternalInput")
    w_b = dram("w_b", [D, D], F32, "ExternalInput")
    w_o = dram("w_o", [D, D], F32, "ExternalInput")
    w_pool = dram("w_pool", [D, 256], F32, "ExternalInput")
    w_up = dram("w_up", [D, F2], F32, "ExternalInput")
    w_down = dram("w_down", [DFF, D], F32, "ExternalInput")
    w_gkb = dram("w_gkb", [17, 512], F32, "ExternalInput")
    vecs_d = dram("vecs", [128, NV], F32, "ExternalInput")
    consts_d = dram("consts", [128, NCONST], F32, "ExternalInput")
    houtT = dram("houtT", [D, npos], F32, "ExternalOutput")
    outT = dram("outT", [D, npos], F32, "ExternalOutput")
    s_in = dram("s_in", [D, INW], BF16, "Internal")
    s_a = dram("s_a", [D, D], BF16, "Internal")
    s_b = dram("s_b", [D, D], BF16, "Internal")
    s_o = dram("s_o", [D, D], BF16, "Internal")
    s_pool = dram("s_pool", [D, 256], BF16, "Internal")
    s_up = dram("s_up", [D, F2], BF16, "Internal")
    s_down = dram("s_down", [DFF, D], BF16, "Internal")

    P = Prog(nc, es)

    def raw(name, shape, dt):
        return es.enter_context(nc.sbuf_tensor("sb_" + name, shape, dt))

    def sb(name, shape, dt, nparts=1):
        return TB(raw(name, shape, dt)[:], name, nparts)

    def view(ap, bufs):
        v = TB(ap, "v", 0)
        v.p = list(bufs)
        return v

    vecs = sb("vecs", [128, NV], F32)
    vsc = sb("vsc", [128, 26], F32)
    consts = sb("consts", [128, NCONST], F32)
    ones = sb("ones", [128, 128], BF16)
    wglr = sb("wglr", [128, 8, 16], BF16)
    wpool = sb("wpool", [128, 8, 256], BF16)
    wgk = sb("wgk", [32, 512], BF16)
    slots = [sb(f"slot{i}", [128, 8, 1024], BF16) for i in range(NSLOT)]
    wready = Buf("wready")

    hT = sb("hT", [128, 8, TT], F32, 8)
    hnT = sb("hnT", [128, 8, TT], BF16, 8)
    X = sb("X", [128, 8, 16 + TT], F32, 8)
    qT = view(X.t[:, 0:4, 0:TT], X.p[0:4])
    kT = view(X.t[:, 4:8, 0:TT], X.p[4:8])
    uT = X
    mb = view(X.t[:, :, 0:TT], X.p)
    Xf = X.t.rearrange("p c t -> p (c t)")
    NY = 3
    yext = [view(Xf[:, i * 528:i * 528 + 2 + TT], [X.p[i]]) for i in range(NY)]
    cacc = [view(Xf[:, (3 + i) * 528:(3 + i) * 528 + TT], [X.p[3 + i]]) for i in range(NY)]
    uhalo = sb("uhalo", [128, 8, 16], F32)
    bigB = sb("bigB", [128, 24, TT], BF16, 24)
    sq = view(bigB.t[:, 0:8, :], bigB.p[0:8])
    silur = sq
    gated = view(bigB.t[:, 8:16, :], bigB.p[8:16])
    ybs = view(bigB.t[:, 16:24, :], bigB.p[16:24])
    act = view(bigB.t[:, 0:22, :], bigB.p[0:22])
    vm = sb("vm", [128, 4096], BF16, 4)
    vtm = view(vm.t.rearrange("p (b n) -> p b n", b=4), vm.p)
    mT = view(vm.t.rearrange("p (c t) -> p c t", c=8), [vm.p[j // 2] for j in range(8)])
    gaT = sb("gaT", [128, 8, TT], BF16, 8)
    kf = sb("kf", [128, 2048], F32, 4)
    ktm = view(kf.t.rearrange("p (b d) -> p b d", b=4), kf.p)
    pA = view(kf.t[:, 0:2 * (16 + TT)].rearrange("p (k e) -> p k e", k=2), kf.p)
    gf = sb("gf", [128, 2048], F32, 4)
    gtm = view(gf.t.rearrange("p (b d) -> p b d", b=4), gf.p)
    pB = view(gf.t[:, 0:2 * (16 + TT)].rearrange("p (k e) -> p k e", k=2), gf.p)
    glrT = sb("glrT", [32, TT], BF16)
    tmpz = sb("tmpz", [128, 512], F32)
    rstd = tmpz
    E1 = [sb(f"E1_{i}", [128, 4, 128], F32) for i in range(2)]
    E2 = [sb("E2_0", [128, 4, 128], F32)] * 2
    qdec = [sb(f"qdec{i}", [128, 4, 128], BF16) for i in range(2)]
    kinv = [sb(f"kinv{i}", [128, 4, 128], BF16) for i in range(2)]
    er = [sb("er0", [128, 512], F32)] * 2
    kend = [sb("kend0", [128, 512], BF16)] * 2
    attm = [sb(f"attm{i}", [128, 4, 128], BF16) for i in range(2)]
    sqo = [sb("sqo0", [128, 8, 128], BF16)] * 2
    rso = [sb("rso0", [128, 4, 128], F32)] * 2
    tmo = [sb("tmo0", [128, 8, 128], F32)] * 2
    Sf = sb("Sf", [128, 4, 256], F32)
    Sb = sb("Sb", [128, 4, 256], BF16)
    sila = [sb(f"sila{i}", [128, TT], BF16) for i in range(2)]
    chalo = sb("chalo", [128, 44, 2], F32, 44)
    ps_t = es.enter_context(nc.psum_tensor("ps", [128, 8, 512], F32))
    banks = [Buf(f"bank{i}", excl=True) for i in range(8)]
    st = {"b1": 0, "b2": 0, "g": 0}

    def ps1():
        i = st["b1"]
        st["b1"] = (i + 1) % 4
        return ps_t[:, i, :], [banks[i]]

    def ps2():
        i = st["b2"]
        st["b2"] = (i + 1) % 2
        b = 4 + 2 * i
        return ps_t[:, b:b + 2, :].rearrange("p b n -> p (b n)"), [banks[b], banks[b + 1]]

    V = vecs.t
    C = consts.t
    TRIB = C[:, 0:128]
    TRIR = C[:, 128:256]
    MASK4 = C[:, 256:768].rearrange("p (h c) -> p h c", h=4)
    INVC = C[:, 768:896].rearrange("p (k c) -> p k c", k=8)

    P.dma("sync", vecs.t, vecs_d, writes=vecs.all)
    P.dma("sync", consts.t, consts_d, writes=consts.all)
    conv_recs = []
    for src, dst, rows in ((w_in, s_in, D), (w_a, s_a, D), (w_up, s_up, D), (w_b, s_b, D), (w_o, s_o, D),
                           (w_pool, s_pool, D), (w_down, s_down, DFF)):
        for r0 in range(0, rows, 128):
            b = Buf("cv")
            conv_recs.append(P.dma("gpsimd", dst[r0:r0 + 128, :], src[r0:r0 + 128, :], writes=[b]))
    P.dma("gpsimd", wgk.t[0:17, :], w_gkb, writes=wgk.all)
    for r in conv_recs:
        P._wait("gpsimd", r)
    P.op("gpsimd", lambda E: E.memset(ones.t, 1.0), writes=ones.all + [wready])
    P.op("gpsimd", lambda E: E.memset(glrT.t, 1.0), writes=glrT.all)
    P.op("gpsimd", lambda E: E.memset(Sf.t, 0.0), writes=Sf.all)
    P.op("gpsimd", lambda E: E.memset(Sb.t, 0.0), writes=Sb.all)
    P.op("gpsimd", lambda E: E.memset(chalo.t, 0.0), writes=chalo.all)
    P.op("gpsimd", lambda E: E.memset(uhalo.t, 0.0), writes=uhalo.all)
    P.op("vector", lambda E: E.tensor_scalar(out=vsc.t[:, 0:24], in0=V[:, 0:24], scalar1=32.0, scalar2=None,
                                             op0=ALU.mult), reads=vecs.all, writes=vsc.all)
    P.op("vector", lambda E: E.tensor_scalar(out=vsc.t[:, 24:26], in0=V[:, 48:50], scalar1=16.0, scalar2=None,
                                             op0=ALU.mult), reads=vecs.all, writes=vsc.all)
    P.dma("sync", wglr.t, s_in[:, 2048:2064].rearrange("(kc p) n -> p kc n", p=128),
          reads=[wready], writes=wglr.all)
    P.dma("sync", wpool.t, s_pool.rearrange("(kc p) n -> p kc n", p=128), reads=[wready], writes=wpool.all)

    def grp_cols(src, c0, n=1024):
        return [(src[:, c0:c0 + n].rearrange("(kc p) n -> p kc n", p=128), (slice(0, 8), slice(0, n)))]

    def grp_up(g):
        n = 512 if g < 5 else 256
        return [(s_up[:, g * 512:g * 512 + n].rearrange("(kc p) n -> p kc n", p=128), (slice(0, 8), slice(0, n))),
                (s_up[:, DFF + g * 512:DFF + g * 512 + n].rearrange("(kc p) n -> p kc n", p=128),
                 (slice(0, 8), slice(512, 512 + n)))]

    def grp_dn(g):
        k0 = g * 8
        nk = 8 if g < 2 else 6
        return [(s_down[k0 * 128:(k0 + nk) * 128, :].rearrange("(kc p) n -> p kc n", p=128),
                 (slice(0, nk), slice(0, 1024)))]

    tile_groups = [grp_cols(s_in, 2064), grp_cols(s_in, 0), grp_cols(s_in, 1024), grp_cols(s_in, 3088),
                   grp_cols(s_in, 5136), grp_cols(s_b, 0), grp_cols(s_in, 4112), grp_cols(s_a, 0),
                   grp_cols(s_o, 0)] + [grp_up(g) for g in range(6)] + [grp_dn(g) for g in range(3)]
    NG = len(tile_groups)
    total_groups = NG * len(tiles)
    ld = {"n": 0}

    def ensure(upto):
        upto = min(upto, total_groups - 1)
        while ld["n"] <= upto:
            n = ld["n"]
            sl = slots[n % NSLOT]
            for src, (ks, cs) in tile_groups[n % NG]:
                P.dma("sync", sl.t[:, ks, cs], src, reads=[wready], writes=sl.all)
            ld["n"] += 1

    def use(gidx, span=1):
        ensure(gidx + span - 1 + (NSLOT - span))
        return slots[gidx % NSLOT]

    def proj_fm(sl, col0, rhsT, nk, T, bank):
        return [lambda E, kc=kc: E.matmul(bank[:, 0:T], lhsT=sl.t[:, kc, col0:col0 + 128], rhs=rhsT.t[:, kc, 0:T],
                                          start=(kc == 0), stop=(kc == nk - 1)) for kc in range(nk)]

    def rmsnorm_to(dst, gcol, T, outf32=None):
        for c in range(8):
            P.op("scalar", lambda E, c=c: E.activation(out=sq.t[:, c, 0:T], in_=hT.t[:, c, 0:T], func=AF.Square),
                 reads=[hT.p[c]], writes=[sq.p[c]])
        bank, bb = ps1()
        P.pe([lambda E, c=c: E.matmul(bank[:, 0:T], lhsT=ones.t[:, :], rhs=sq.t[:, c, 0:T], start=(c == 0),
                                      stop=(c == 7)) for c in range(8)], reads=sq.all + ones.all, writes=bb)
        P.op("scalar", lambda E: E.activation(out=rstd.t[:, 0:T], in_=bank[:, 0:T], func=AF.Ln, bias=float(D * EPS)),
             reads=bb, writes=rstd.all)
        P.op("scalar", lambda E: E.activation(out=rstd.t[:, 0:T], in_=rstd.t[:, 0:T], func=AF.Exp, scale=-0.5),
             reads=rstd.all, writes=rstd.all)
        tgt = dst if outf32 is None else outf32
        for c in range(8):
            P.op("vector", lambda E, c=c: E.scalar_tensor_tensor(
                out=tgt.t[:, c, 0:T], in0=hT.t[:, c, 0:T], scalar=vsc.t[:, gcol + c:gcol + c + 1],
                in1=rstd.t[:, 0:T], op0=ALU.mult, op1=ALU.mult),
                reads=[hT.p[c]] + rstd.all + vsc.all, writes=[tgt.p[c]])

    out_recs = []

    def fm_proj(gidx, col0s, rhsT, T, evac):
        sl = use(gidx)
        for j, col0 in enumerate(col0s):
            bank, bb = ps1()
            P.pe(proj_fm(sl, col0, rhsT, 8, T, bank), reads=sl.all + rhsT.all, writes=bb)
            evac(j, bank, bb)

    def do_tile(ti, t0, T):
        first = (t0 == 0)
        nblk = T // 128
        gb0 = ti * NG
        EE = 16 + T
        if STOP < 1:
            return
        P.dma("sync", hT.t[:, :, 0:T], xT[:, t0:t0 + T].rearrange("(c p) t -> p c t", p=128), writes=hT.all)
        rmsnorm_to(hnT, 0, T)
        if STOP < 2:
            return
        fm_proj(gb0 + 0, [j * 128 for j in range(8)], hnT, T,
                lambda j, bank, bb: P.op("scalar", lambda E: E.activation(out=silur.t[:, j, 0:T], in_=bank[:, 0:T], func=AF.Silu),
                                         reads=bb, writes=[silur.p[j]]))
        if STOP < 3:
            return
        def ev_qk(j, bank, bb):
            if j < 4:
                P.op("scalar", lambda E: E.activation(out=qT.t[:, j, 0:T], in_=bank[:, 0:T], func=AF.Identity,
                                                      scale=float(128 ** -0.5)), reads=bb, writes=[qT.p[j]])
            else:
                P.op("scalar", lambda E: E.activation(out=kT.t[:, j - 4, 0:T], in_=bank[:, 0:T], func=AF.Copy),
                     reads=bb, writes=[kT.p[j - 4]])
        fm_proj(gb0 + 1, [j * 128 for j in range(8)], hnT, T, ev_qk)
        sl = use(gb0 + 1)
        for blk in range(nblk):
            bank, bb = ps1()
            P.pe([lambda E, kc=kc, blk=blk, bank=bank, sl=sl: E.matmul(
                bank[:, :], lhsT=hnT.t[:, kc, blk * 128:(blk + 1) * 128], rhs=sl.t[:, kc, 512:1024],
                start=(kc == 0), stop=(kc == 7)) for kc in range(8)], reads=sl.all + hnT.all, writes=bb)
            P.op("scalar", lambda E, blk=blk, bank=bank: E.activation(out=ktm.t[:, blk, :], in_=bank[:, :], func=AF.Copy),
                 reads=bb, writes=[ktm.p[blk]])
        sl = use(gb0 + 2)
        for blk in range(nblk):
            for hf in range(2):
                bank, bb = ps1()
                P.pe([lambda E, kc=kc, blk=blk, hf=hf, bank=bank, sl=sl: E.matmul(
                    bank[:, :], lhsT=hnT.t[:, kc, blk * 128:(blk + 1) * 128], rhs=sl.t[:, kc, hf * 512:(hf + 1) * 512],
                    start=(kc == 0), stop=(kc == 7)) for kc in range(8)], reads=sl.all + hnT.all, writes=bb)
                P.op("vector", lambda E, blk=blk, hf=hf, bank=bank: E.tensor_copy(
                    out=vtm.t[:, blk, hf * 512:(hf + 1) * 512], in_=bank[:, :]), reads=bb, writes=[vtm.p[blk]])
        bank, bb = ps1()
        P.pe([lambda E, kc=kc, bank=bank: E.matmul(bank[0:16, 0:T], lhsT=wglr.t[:, kc, :], rhs=hnT.t[:, kc, 0:T],
                                                    start=(kc == 0), stop=(kc == 7)) for kc in range(8)],
             reads=wglr.all + hnT.all, writes=bb)
        P.op("vector", lambda E, bank=bank: E.tensor_copy(out=glrT.t[0:16, 0:T], in_=bank[0:16, 0:T]),
             reads=bb, writes=glrT.all)
        for blk in range(nblk):
            bank, bb = ps1()
            P.pe([lambda E, blk=blk, bank=bank: E.matmul(bank[:, :], lhsT=glrT.t[0:17, blk * 128:(blk + 1) * 128],
                                                          rhs=wgk.t[0:17, :], start=True, stop=True)],
                 reads=glrT.all + wgk.all, writes=bb)
            P.op("scalar", lambda E, bank=bank: E.activation(out=tmpz.t[:, :], in_=bank[:, :], func=AF.Exp, scale=-1.0),
                 reads=bb, writes=tmpz.all)
            P.op("scalar", lambda E, blk=blk: E.activation(out=gtm.t[:, blk, :], in_=tmpz.t[:, :], func=AF.Ln, bias=1.0),
                 reads=tmpz.all, writes=[gtm.p[blk]])
        if STOP < 4:
            return
        for blk in range(nblk):
            pb = st["g"] % 2
            st["g"] += 1
            cs = slice(blk * 128, (blk + 1) * 128)
            bkA, bbA = ps1()
            P.pe([lambda E, h=h, blk=blk, bkA=bkA: E.matmul(bkA[:, h * 128:(h + 1) * 128],
                                                             lhsT=gtm.t[:, blk, h * 128:(h + 1) * 128], rhs=TRIB,
                                                             start=True, stop=True) for h in range(4)],
                 reads=[gtm.p[blk]] + consts.all, writes=bbA)
            bkB, bbB = ps1()
            P.pe([lambda E, blk=blk, bkB=bkB: E.matmul(bkB[:, :], lhsT=TRIR, rhs=gtm.t[:, blk, :], start=True, stop=True)],
                 reads=[gtm.p[blk]] + consts.all, writes=bbB)
            A3 = bkA.rearrange("p (h c) -> p h c", h=4)
            P.op("scalar", lambda E, pb=pb, A3=A3: E.activation(out=E1[pb].t, in_=A3, func=AF.Exp),
                 reads=bbA, writes=E1[pb].all)
            P.op("scalar", lambda E, pb=pb, A3=A3: E.activation(out=E2[pb].t, in_=A3, func=AF.Exp, scale=-1.0),
                 reads=bbA, writes=E2[pb].all)
            P.op("scalar", lambda E, pb=pb, bkB=bkB: E.activation(out=er[pb].t, in_=bkB[:, :], func=AF.Exp),
                 reads=bbB, writes=er[pb].all)
            P.op("vector", lambda E, pb=pb, cs=cs: E.tensor_tensor(out=qdec[pb].t, in0=qT.t[:, :, cs], in1=E1[pb].t,
                                                                   op=ALU.mult), reads=qT.all + E1[pb].all, writes=qdec[pb].all)
            P.op("vector", lambda E, pb=pb, cs=cs: E.tensor_tensor(out=kinv[pb].t, in0=kT.t[:, :, cs], in1=E2[pb].t,
                                                                   op=ALU.mult), reads=kT.all + E2[pb].all, writes=kinv[pb].all)
            P.op("vector", lambda E, pb=pb, blk=blk: E.tensor_tensor(out=kend[pb].t, in0=ktm.t[:, blk, :], in1=er[pb].t,
                                                                     op=ALU.mult), reads=[ktm.p[blk]] + er[pb].all, writes=kend[pb].all)
            bkC, bbC = ps1()
            P.pe([lambda E, h=h, pb=pb, bkC=bkC: E.matmul(bkC[:, h * 128:(h + 1) * 128], lhsT=kinv[pb].t[:, h, :],
                                                           rhs=qdec[pb].t[:, h, :], start=True, stop=True) for h in range(4)],
                 reads=kinv[pb].all + qdec[pb].all, writes=bbC)
            C3 = bkC.rearrange("p (h c) -> p h c", h=4)
            P.op("vector", lambda E, pb=pb, C3=C3: E.tensor_tensor(out=attm[pb].t, in0=C3, in1=MASK4, op=ALU.mult),
                 reads=bbC + consts.all, writes=attm[pb].all)
            bkD, bbD = ps2()
            fns = []
            for h in range(4):
                for ec in range(2):
                    o = (h * 2 + ec) * 128
                    fns.append(lambda E, h=h, ec=ec, o=o, blk=blk, pb=pb, bkD=bkD: E.matmul(
                        bkD[:, o:o + 128], lhsT=vtm.t[:, blk, h * 256 + ec * 128:h * 256 + ec * 128 + 128],
                        rhs=attm[pb].t[:, h, :], start=True, stop=False))
                    fns.append(lambda E, h=h, ec=ec, o=o, pb=pb, bkD=bkD: E.matmul(
                        bkD[:, o:o + 128], lhsT=Sb.t[:, h, ec * 128:(ec + 1) * 128], rhs=qdec[pb].t[:, h, :],
                        start=False, stop=True))
            P.pe(fns, reads=[vtm.p[blk]] + attm[pb].all + Sb.all + qdec[pb].all, writes=bbD)
            bkE, bbE = ps2()
            P.pe([lambda E, h=h, blk=blk, pb=pb, bkE=bkE: E.matmul(bkE[:, h * 256:(h + 1) * 256],
                                                                    lhsT=kend[pb].t[:, h * 128:(h + 1) * 128],
                                                                    rhs=vtm.t[:, blk, h * 256:(h + 1) * 256],
                                                                    start=True, stop=True) for h in range(4)],
                 reads=kend[pb].all + [vtm.p[blk]], writes=bbE)
            for h in range(4):
                P.op("vector", lambda E, h=h, pb=pb, bkE=bkE: E.scalar_tensor_tensor(
                    out=Sf.t[:, h, :], in0=Sf.t[:, h, :], scalar=E1[pb].t[:, h, 127:128], in1=bkE[:, h * 256:(h + 1) * 256],
                    op0=ALU.mult, op1=ALU.add), reads=Sf.all + E1[pb].all + bbE, writes=Sf.all)
            P.op("scalar", lambda E: E.activation(out=Sb.t, in_=Sf.t, func=AF.Copy), reads=Sf.all, writes=Sb.all)
            D3 = bkD.rearrange("p (k c) -> p k c", k=8)
            P.op("scalar", lambda E, pb=pb, D3=D3: E.activation(out=sqo[pb].t, in_=D3, func=AF.Square),
                 reads=bbD, writes=sqo[pb].all)
            bkF, bbF = ps1()
            fns = []
            for h in range(4):
                for ec in range(2):
                    fns.append(lambda E, h=h, ec=ec, pb=pb, bkF=bkF: E.matmul(
                        bkF[:, h * 128:(h + 1) * 128], lhsT=ones.t[:, :], rhs=sqo[pb].t[:, h * 2 + ec, :],
                        start=(ec == 0), stop=(ec == 1)))
            P.pe(fns, reads=sqo[pb].all + ones.all, writes=bbF)
            F3 = bkF.rearrange("p (h c) -> p h c", h=4)
            P.op("scalar", lambda E, pb=pb, F3=F3: E.activation(out=rso[pb].t, in_=F3, func=AF.Ln, bias=float(256 * EPS)),
                 reads=bbF, writes=rso[pb].all)
            P.op("scalar", lambda E, pb=pb: E.activation(out=rso[pb].t, in_=rso[pb].t, func=AF.Exp, scale=-0.5),
                 reads=rso[pb].all, writes=rso[pb].all)
            D4 = bkD.rearrange("p (h e c) -> p h e c", h=4, e=2)
            T4 = tmo[pb].t.rearrange("p (h e) c -> p h e c", h=4)
            for ec in range(2):
                P.op("vector", lambda E, ec=ec, pb=pb, D4=D4, T4=T4: E.scalar_tensor_tensor(
                    out=T4[:, :, ec, :], in0=D4[:, :, ec, :], scalar=vsc.t[:, 24 + ec:25 + ec], in1=rso[pb].t,
                    op0=ALU.mult, op1=ALU.mult), reads=bbD + rso[pb].all + vsc.all, writes=tmo[pb].all)
            P.op("gpsimd", lambda E, pb=pb, cs=cs: E.tensor_tensor(out=gated.t[:, :, cs], in0=tmo[pb].t, in1=silur.t[:, :, cs],
                                                                   op=ALU.mult), reads=tmo[pb].all + silur.all, writes=gated.all)
        if STOP < 5:
            return
        P.op("gpsimd", lambda E: E.tensor_copy(out=uT.t[:, :, 0:16], in_=uhalo.t), reads=uhalo.all, writes=uT.all)
        fm_proj(gb0 + 3, [j * 128 for j in range(8)], hnT, T,
                lambda j, bank, bb: P.op("scalar", lambda E: E.activation(out=uT.t[:, j, 16:16 + T], in_=bank[:, 0:T], func=AF.Copy),
                                         reads=bb, writes=[uT.p[j]]))
        P.op("gpsimd", lambda E: E.tensor_copy(out=uhalo.t, in_=uT.t[:, :, T:T + 16]), reads=uT.all, writes=uhalo.all)
        for g in range(4):
            w = 2 ** (g + 1)
            c2 = slice(2 * g, 2 * g + 2)
            ub = [uT.p[2 * g], uT.p[2 * g + 1]]
            P.op("gpsimd", lambda E, c2=c2: E.tensor_tensor(out=pA.t[:, :, 1:EE], in0=uT.t[:, c2, 1:EE], in1=uT.t[:, c2, 0:EE - 1],
                                                            op=ALU.add), reads=ub, writes=pA.all)
            cur, oth = pA, pB
            lo, sh = 1, 2
            for s_ in range(g):
                nlo = lo + sh
                P.op("gpsimd", lambda E, cur=cur, oth=oth, nlo=nlo, sh=sh: E.tensor_tensor(
                    out=oth.t[:, :, nlo:EE], in0=cur.t[:, :, nlo:EE], in1=cur.t[:, :, nlo - sh:EE - sh], op=ALU.add),
                    reads=cur.all, writes=oth.all)
                cur, oth = oth, cur
                lo, sh = nlo, sh * 2
            pd = [sq.p[2 * g], sq.p[2 * g + 1]]
            P.op("gpsimd", lambda E, cur=cur, oth=oth, w=w: E.tensor_scalar(
                out=oth.t[:, :, 16:EE], in0=cur.t[:, :, 16:EE], scalar1=float(1.0 / w), scalar2=None, op0=ALU.mult),
                reads=cur.all, writes=oth.all)
            P.op("gpsimd", lambda E, oth=oth, c2=c2: E.tensor_tensor(
                out=sq.t[:, c2, 0:T], in0=oth.t[:, :, 16:EE], in1=uT.t[:, c2, 16:EE], op=ALU.subtract),
                reads=oth.all + ub, writes=pd)
            if first:
                P.op("gpsimd", lambda E, cur=cur, oth=oth, c2=c2: E.tensor_tensor(
                    out=oth.t[:, :, 0:16], in0=cur.t[:, :, 16 + PAD:EE], in1=INVC[:, c2, :], op=ALU.mult),
                    reads=cur.all + consts.all, writes=oth.all)
                P.op("gpsimd", lambda E, oth=oth, c2=c2: E.tensor_tensor(
                    out=sq.t[:, c2, PAD:T], in0=oth.t[:, :, 0:16], in1=uT.t[:, c2, 16 + PAD:EE], op=ALU.subtract),
                    reads=oth.all + ub, writes=pd)
        for j in range(8):
            g = j // 2
            jj = j % 2
            bank, bb = ps1()
            P.pe([lambda E, kc=kc, g=g, jj=jj, bank=bank: E.matmul(
                bank[:, 0:T], lhsT=wpool.t[:, 2 * g + kc, jj * 128:(jj + 1) * 128], rhs=sq.t[:, 2 * g + kc, 0:T],
                start=(kc == 0), stop=(kc == 1)) for kc in range(2)],
                reads=wpool.all + [sq.p[2 * g], sq.p[2 * g + 1]], writes=bb)
            P.op("scalar", lambda E, j=j, bank=bank: E.activation(out=ybs.t[:, j, 0:T], in_=bank[:, 0:T], func=AF.Identity,
                                                                   scale=V[:, 24 + j:25 + j]), reads=bb + vecs.all, writes=[ybs.p[j]])
        if STOP < 6:
            return
        fm_proj(gb0 + 4, [j * 128 for j in range(8)], hnT, T,
                lambda j, bank, bb: P.op("scalar", lambda E: E.activation(out=gaT.t[:, j, 0:T], in_=bank[:, 0:T], func=AF.Sigmoid,
                                                                          bias=V[:, 40 + j:41 + j]), reads=bb + vecs.all, writes=[gaT.p[j]]))
        fm_proj(gb0 + 5, [j * 128 for j in range(8)], ybs, T,
                lambda j, bank, bb: P.op("vector", lambda E: E.tensor_tensor(out=mb.t[:, j, 0:T], in0=bank[:, 0:T], in1=gaT.t[:, j, 0:T],
                                                                             op=ALU.mult), reads=bb + [gaT.p[j]], writes=[mb.p[j]]))
        fm_proj(gb0 + 6, [j * 128 for j in range(8)], hnT, T,
                lambda j, bank, bb: P.op("scalar", lambda E: E.activation(out=gaT.t[:, j, 0:T], in_=bank[:, 0:T], func=AF.Sigmoid,
                                                                          bias=V[:, 32 + j:33 + j]), reads=bb + vecs.all, writes=[gaT.p[j]]))
        def ev_ya(j, bank, bb):
            P.op("vector", lambda E: E.tensor_tensor(out=tmpz.t[:, 0:T], in0=bank[:, 0:T], in1=gaT.t[:, j, 0:T], op=ALU.mult),
                 reads=bb + [gaT.p[j]], writes=tmpz.all)
            P.op("gpsimd", lambda E: E.tensor_tensor(out=mT.t[:, j, 0:T], in0=tmpz.t[:, 0:T], in1=mb.t[:, j, 0:T], op=ALU.add),
                 reads=tmpz.all + [mb.p[j]], writes=[mT.p[j]])
        fm_proj(gb0 + 7, [j * 128 for j in range(8)], gated, T, ev_ya)
        fm_proj(gb0 + 8, [j * 128 for j in range(8)], mT, T,
                lambda j, bank, bb: P.op("vector", lambda E: E.tensor_tensor(out=hT.t[:, j, 0:T], in0=hT.t[:, j, 0:T], in1=bank[:, 0:T],
                                                                             op=ALU.add), reads=bb + [hT.p[j]], writes=[hT.p[j]]))
        if STOP < 7:
            return
        rmsnorm_to(hnT, 8, T)
        yi = 0
        for g in range(6):
            sl = use(gb0 + 9 + g)
            npair = 4 if g < 5 else 2
            for jj in range(npair):
                pj = g * 4 + jj
                res = []
                for half in range(2):
                    ch = pj + 22 * half
                    col0 = half * 512 + jj * 128
                    bank, bb = ps1()
                    P.pe(proj_fm(sl, col0, hnT, 8, T, bank), reads=sl.all + hnT.all, writes=bb)
                    ye = yext[yi % NY]
                    ca = cacc[yi % NY]
                    yi += 1
                    P.op("gpsimd", lambda E, ye=ye, ch=ch: E.tensor_copy(out=ye.t[:, 0:2], in_=chalo.t[:, ch, :]),
                         reads=[chalo.p[ch]], writes=ye.all)
                    P.op("scalar", lambda E, ye=ye, bank=bank: E.activation(out=ye.t[:, 2:2 + T], in_=bank[:, 0:T], func=AF.Copy),
                         reads=bb, writes=ye.all)
                    P.op("scalar", lambda E, ca=ca, bank=bank, ch=ch: E.activation(
                        out=ca.t[:, 0:T], in_=bank[:, 0:T], func=AF.Identity, scale=V[:, 138 + ch:139 + ch],
                        bias=V[:, 182 + ch:183 + ch]), reads=bb + vecs.all, writes=ca.all)
                    P.op("gpsimd", lambda E, ye=ye, ch=ch, T=T: E.tensor_copy(out=chalo.t[:, ch, :], in_=ye.t[:, T:T + 2]),
                         reads=ye.all, writes=[chalo.p[ch]])
                    P.op("vector", lambda E, ye=ye, ca=ca, ch=ch: E.scalar_tensor_tensor(
                        out=ca.t[:, 0:T], in0=ye.t[:, 1:1 + T], scalar=V[:, 94 + ch:95 + ch], in1=ca.t[:, 0:T],
                        op0=ALU.mult, op1=ALU.add), reads=ye.all + ca.all + vecs.all, writes=ca.all)
                    P.op("vector", lambda E, ye=ye, ca=ca, ch=ch: E.scalar_tensor_tensor(
                        out=ca.t[:, 0:T], in0=ye.t[:, 0:T], scalar=V[:, 50 + ch:51 + ch], in1=ca.t[:, 0:T],
                        op0=ALU.mult, op1=ALU.add), reads=ye.all + ca.all + vecs.all, writes=ca.all)
                    res.append(ca)
                sa = sila[pj % 2]
                P.op("scalar", lambda E, sa=sa, ca=res[0]: E.activation(out=sa.t[:, 0:T], in_=ca.t[:, 0:T], func=AF.Silu),
                     reads=res[0].all, writes=sa.all)
                P.op("vector", lambda E, sa=sa, cb=res[1], pj=pj: E.tensor_tensor(out=act.t[:, pj, 0:T], in0=sa.t[:, 0:T],
                                                                                   in1=cb.t[:, 0:T], op=ALU.mult),
                     reads=sa.all + res[1].all, writes=[act.p[pj]])
        if STOP < 8:
            return
        sls = [use(gb0 + 15, span=3), slots[(gb0 + 16) % NSLOT], slots[(gb0 + 17) % NSLOT]]
        for j in range(8):
            bank, bb = ps1()
            P.pe([lambda E, kc=kc, j=j, bank=bank, sls=sls: E.matmul(
                bank[:, 0:T], lhsT=sls[kc // 8].t[:, kc % 8, j * 128:(j + 1) * 128], rhs=act.t[:, kc, 0:T],
                start=(kc == 0), stop=(kc == 21)) for kc in range(22)],
                reads=sls[0].all + sls[1].all + sls[2].all + act.all, writes=bb)
            P.op("vector", lambda E, j=j, bank=bank: E.tensor_tensor(out=hT.t[:, j, 0:T], in0=hT.t[:, j, 0:T], in1=bank[:, 0:T],
                                                                     op=ALU.add), reads=bb + [hT.p[j]], writes=[hT.p[j]])
        if first:
            for c in range(8):
                P.op("gpsimd", lambda E, c=c: E.memset(hT.t[:, c, 0:PAD], 0.0), writes=[hT.p[c]])
        if STOP < 9:
            return
        out_recs.append(P.dma("sync", houtT[:, t0:t0 + T].rearrange("(c p) t -> p c t", p=128), hT.t[:, :, 0:T],
                              reads=hT.all))
        if STOP < 10:
            return
        rmsnorm_to(None, 16, T, outf32=mb)
        if STOP < 11:
            return
        out_recs.append(P.dma("sync", outT[:, t0:t0 + T].rearrange("(c p) t -> p c t", p=128), mb.t[:, :, 0:T],
                              reads=mb.all))
    for ti, (t0, T) in enumerate(tiles):
        do_tile(ti, t0, T)
    for r in out_recs:
        P.final_wait("sync", r)
    with nc.Block() as block:
        P.emit(block)
    es.close()
    return nc


def _consts():
    c = np.zeros((128, NCONST), np.float32)
    s = np.arange(128)[:, None]
    cc = np.arange(128)[None, :]
    c[:, 0:128] = np.where(s <= cc, -1.0 / 16.0, 0.0)
    c[:, 128:256] = np.where(s > cc, -1.0 / 16.0, 0.0)
    m = np.where(s <= cc, 1.0, 0.0)
    c[:, 256:768] = np.tile(m, (1, 4))
    inv = np.zeros((8, 16), np.float32)
    for ch in range(8):
        w = 2 ** (ch // 2 + 1)
        for j in range(16):
            inv[ch, j] = 1.0 / min(j + 1, w)
    c[:, 768:896] = np.broadcast_to(inv.reshape(1, 128), (128, 128))
    return c


def _vecs(inp, l):
    v = np.zeros((128, NV), np.float32)
    fm = lambda a: np.ascontiguousarray(np.asarray(a, np.float32).reshape(-1, 128).T)
    v[:, 0:8] = fm(inp["norm1_g"][l])
    v[:, 8:16] = fm(inp["norm2_g"][l])
    v[:, 16:24] = fm(inp["final_norm_g"])
    v[:, 24:32] = fm(inp["pool_scale"][l])
    v[:, 32:40] = fm(inp["b_gates"][l][:D])
    v[:, 40:48] = fm(inp["b_gates"][l][D:])
    v[:, 48:50] = fm(inp["gla_norm_g"][l])
    v[:, 50:94] = fm(inp["conv_w"][l][0])
    v[:, 94:138] = fm(inp["conv_w"][l][1])
    v[:, 138:182] = fm(inp["conv_w"][l][2])
    v[:, 182:226] = fm(inp["conv_b"][l])
    return v


def _layer_map(inp, l):
    f = lambda a: np.ascontiguousarray(np.asarray(a, np.float32))
    return {
        "w_in": f(inp["w_in"][l]), "w_a": f(inp["w_a"][l]), "w_b": f(inp["w_b"][l]), "w_o": f(inp["w_o"][l]),
        "w_pool": f(np.asarray(inp["w_pool_grp"][l]).reshape(D, 256)), "w_up": f(inp["w_up"][l]),
        "w_down": f(inp["w_down"][l]),
        "w_gkb": f(np.concatenate([np.asarray(inp["w_gk"][l]), np.asarray(inp["b_gk"][l])[None, :]], axis=0)),
        "vecs": _vecs(inp, l), "consts": _consts(),
    }


_NC_CACHE = {}


def _get_nc(tiles, npos):
    key = (tuple(tiles), npos)
    if key not in _NC_CACHE:
        _NC_CACHE[key] = build_program(tiles, npos)
    return _NC_CACHE[key]


def make_xT(inp, b, npos_real=None):
    x = np.asarray(inp["x"], np.float32)
    meta = np.asarray(inp["meta_tokens"], np.float32)
    n = x.shape[1] if npos_real is None else npos_real
    full = np.concatenate([np.zeros((PAD, D), np.float32), meta, x[b, :n]], axis=0)
    return np.ascontiguousarray(full.T)


def kernel(**inputs):
    tiles = FULL_TILES
    nc = _get_nc(tiles, NPOS)
    hs = [make_xT(inputs, b) for b in range(BATCH)]
    outs = None
    for l in range(2):
        lm = _layer_map(inputs, l)
        in_maps = []
        for core in range(8):
            m = dict(lm)
            m["xT"] = hs[core % BATCH]
            in_maps.append(m)
        res = run_bass_kernel_spmd(nc, in_maps, core_ids=list(range(8)))
        hs = [np.ascontiguousarray(res.results[b]["houtT"]) for b in range(BATCH)]
        outs = [res.results[b]["outT"] for b in range(BATCH)]
    out = np.stack([np.ascontiguousarray(o[:, PAD + NMETA:].T) for o in outs], axis=0)
    return out.astype(np.float32)
```

```python
from contextlib import ExitStack
import os
STOP = int(os.environ.get('MK_STOP', '99'))
import numpy as np
import concourse.bass as bass
import concourse.mybir as mybir
from concourse.bass_utils import run_bass_kernel_spmd

F32 = mybir.dt.float32
BF16 = mybir.dt.bfloat16
AF = mybir.ActivationFunctionType
ALU = mybir.AluOpType

D = 1024
NMETA = 16
SEQ = 8192
BATCH = 4
PAD = 496
NT = (PAD + NMETA + SEQ) // 512
NSTEP = NT + 2
NPOS = NSTEP * 512
INW = 6160
DFF = 2816
F2 = 2 * DFF
EPS = 1e-6
TT = 512
NV = 226
NCONST = 1024
NSLOT = 4
NDS = 8

ENGS = ("tensor", "vector", "scalar", "gpsimd", "sync")


class Buf:
    __slots__ = ("name", "w", "r", "excl")

    def __init__(self, name, excl=False):
        self.name = name
        self.w = None
        self.r = {}
        self.excl = excl


class TB:
    def __init__(self, t, name, nparts=1):
        self.t = t
        self.p = [Buf(f"{name}.{i}") for i in range(nparts)]

    @property
    def all(self):
        return self.p


class Prog:
    def __init__(self, nc, es):
        self.nc = nc
        self.q = {e: [] for e in ENGS}
        self.cnt = {e: 0 for e in ENGS}
        self.sem = {e: es.enter_context(nc.semaphore("s_" + e)) for e in ENGS}
        self.waited = {e: {} for e in ENGS}
        nds = {"sync": 12, "gpsimd": 12}
        self.dsem = {e: [es.enter_context(nc.semaphore(f"d_{e}{i}")) for i in range(nds[e])] for e in nds}
        self.dcnt = {e: [0] * nds[e] for e in self.dsem}
        self.dnext = {e: 0 for e in self.dsem}
        self.ccsem = es.enter_context(nc.semaphore("s_cc"))
        self.ccn = 0

    def _wait(self, eng, sv):
        sem, val, key, src = sv
        if self.waited[eng].get(key, 0) >= val:
            return
        self.waited[eng][key] = val
        self.q[eng].append(lambda E, sem=sem, val=val: E.wait_ge(sem, val))

    def _deps(self, eng, reads, writes):
        for b in reads:
            if b.w is not None:
                if not (eng == "tensor" and b.w[3] == "tensor"):
                    self._wait(eng, b.w)
            if b.excl:
                for r in b.r.values():
                    if r[3] != eng:
                        self._wait(eng, r)
        for b in writes:
            if b.w is not None:
                if not (eng == "tensor" and b.w[3] == "tensor"):
                    self._wait(eng, b.w)
            for r in b.r.values():
                if r[3] == eng and eng != "dma":
                    continue
                self._wait(eng, r)

    def _record(self, rec, reads, writes):
        for b in reads:
            b.r[rec[2]] = rec
        for b in writes:
            b.w = rec
            b.r = {}

    def op(self, eng, fn, reads=(), writes=()):
        self._deps(eng, reads, writes)
        self.cnt[eng] += 1
        c = self.cnt[eng]
        sem = self.sem[eng]
        self.q[eng].append(lambda E, fn=fn, sem=sem: fn(E).then_inc(sem, 1))
        self._record((sem, c, "e_" + eng, eng), reads, writes)

    def pe(self, fns, reads=(), writes=(), fine=None):
        eng = "tensor"
        self._deps(eng, reads, writes)
        self.cnt[eng] += 1
        c = self.cnt[eng]
        sem = self.sem[eng]
        allr = list(reads)
        for i, f in enumerate(fns):
            if fine is not None:
                self._deps(eng, fine[i], ())
                allr += list(fine[i])
            if i < len(fns) - 1:
                self.q[eng].append(lambda E, f=f: f(E))
            else:
                self.q[eng].append(lambda E, f=f, sem=sem: f(E).then_inc(sem, 1))
        self._record((sem, c, "e_tensor", eng), allr, writes)

    def dma(self, eng, out, in_, reads=(), writes=()):
        i = self.dnext[eng]
        self.dnext[eng] = (i + 1) % len(self.dsem[eng])
        sem = self.dsem[eng][i]
        key = f"d_{eng}{i}"
        if self.dcnt[eng][i] > 0:
            self._wait(eng, (sem, self.dcnt[eng][i], key, "dma"))
        self._deps(eng, reads, writes)
        self.dcnt[eng][i] += 16
        v = self.dcnt[eng][i]
        self.q[eng].append(lambda E, out=out, in_=in_, sem=sem: E.dma_start(out=out, in_=in_).then_inc(sem, 16))
        rec = (sem, v, key, "dma")
        self._record(rec, reads, writes)
        return rec

    def cc(self, fn, reads=(), writes=()):
        eng = "gpsimd"
        self._deps(eng, reads, writes)
        self.ccn += 1
        sem = self.ccsem
        self.q[eng].append(lambda E, fn=fn, sem=sem: fn(E).then_inc(sem, 1))
        rec = (sem, self.ccn, "cc", "cc")
        self._record(rec, reads, writes)
        return rec

    def final_wait(self, eng, rec):
        self._wait(eng, rec)

    def emit(self, block):
        for e in ENGS:
            def mk(fl):
                def _(E):
                    for f in fl:
                        f(E)
                return _
            getattr(block, e)(mk(self.q[e]))


def build_program(nsteps, NL=1, pair_groups=None):
    nc = bass.Bass("TRN2", target_bir_lowering=False)
    es = ExitStack()
    dram = lambda n, s, dt, kind: nc.dram_tensor(n, s, dt, kind=kind).ap()
    npos = nsteps * TT
    tiles = [(i * TT, TT) for i in range(nsteps)]
    xT = dram("xT", [D, npos], F32, "ExternalInput")
    flags_d = dram("flags", [128, 4], F32, "ExternalInput")
    bands_d = dram("bands", [128, 16 * 128], F32, "ExternalInput")
    sends = [nc.dram_tensor(f"sendb{i}", [D, TT], F32).ap() for i in range(2)]
    recvs = [nc.dram_tensor(f"recvb{i}", [2 * D, TT], F32).ap() for i in range(2)]
    sendB = [Buf(f"sendb{i}") for i in range(2)]
    recvB = [Buf(f"recvb{i}") for i in range(2)]
    LR = range(NL)
    w_in = [dram(f"w_in{l}", [D, INW], F32, "ExternalInput") for l in LR]
    w_a = [dram(f"w_a{l}", [D, D], F32, "ExternalInput") for l in LR]
    w_b = [dram(f"w_b{l}", [D, D], F32, "ExternalInput") for l in LR]
    w_o = [dram(f"w_o{l}", [D, D], F32, "ExternalInput") for l in LR]
    w_pool = [dram(f"w_pool{l}", [D, 256], F32, "ExternalInput") for l in LR]
    w_up = [dram(f"w_up{l}", [D, F2], F32, "ExternalInput") for l in LR]
    w_down = [dram(f"w_down{l}", [DFF, D], F32, "ExternalInput") for l in LR]
    w_gkb = [dram(f"w_gkb{l}", [17, 512], F32, "ExternalInput") for l in LR]
    vecs_d = [dram(f"vecs{l}", [128, NV], F32, "ExternalInput") for l in LR]
    consts_d = dram("consts", [128, NCONST], F32, "ExternalInput")
    outT = dram("outT", [D, npos], F32, "ExternalOutput")
    s_in = [dram(f"s_in{l}", [D, INW], BF16, "Internal") for l in LR]
    s_a = [dram(f"s_a{l}", [D, D], BF16, "Internal") for l in LR]
    s_b = [dram(f"s_b{l}", [D, D], BF16, "Internal") for l in LR]
    s_o = [dram(f"s_o{l}", [D, D], BF16, "Internal") for l in LR]
    s_pool = [dram(f"s_pool{l}", [D, 256], BF16, "Internal") for l in LR]
    s_up = [dram(f"s_up{l}", [D, F2], BF16, "Internal") for l in LR]
    s_down = [dram(f"s_down{l}", [DFF, D], BF16, "Internal") for l in LR]

    P = Prog(nc, es)

    def raw(name, shape, dt):
        return es.enter_context(nc.sbuf_tensor("sb_" + name, shape, dt))

    def sb(name, shape, dt, nparts=1):
        return TB(raw(name, shape, dt)[:], name, nparts)

    def view(ap, bufs):
        v = TB(ap, "v", 0)
        v.p = list(bufs)
        return v

    vecsL = [sb(f"vecs{l}", [128, NV], F32) for l in LR]
    vscL = [sb(f"vsc{l}", [128, 26], F32) for l in LR]
    consts = sb("consts", [128, NCONST], F32)
    flags = sb("flags", [128, 4], F32)
    ones = sb("ones", [128, 128], BF16)
    wglrL = [sb(f"wglr{l}", [128, 8, 16], BF16) for l in LR]
    wgkL = [sb(f"wgk{l}", [32, 512], BF16) for l in LR]
    slots = [sb(f"slot{i}", [128, 8, 1024], BF16) for i in range(NSLOT)]
    wready = Buf("wready")

    HS = [sb(f"H{i}", [128, 8, 16 + TT], F32, 8) for i in range(2)]
    hTv = [view(HS[i].t[:, :, 0:TT], HS[i].p) for i in range(2)]
    hnT = sb("hnT", [128, 8, TT], BF16, 8)
    NY = 3

    def xviews(X):
        Xf = X.t.rearrange("p c t -> p (c t)")
        return (view(X.t[:, 0:4, 0:TT], X.p[0:4]), view(X.t[:, 4:8, 0:TT], X.p[4:8]), view(X.t[:, :, 0:TT], X.p),
                [view(Xf[:, i * 528:i * 528 + 2 + TT], [X.p[i]]) for i in range(NY)],
                [view(Xf[:, (3 + i) * 528:(3 + i) * 528 + TT], [X.p[3 + i]]) for i in range(NY)])
    XV = [xviews(HS[i]) for i in range(2)]
    uprevL = [sb(f"uprev{l}", [128, 1024], BF16) for l in LR]
    bands = sb("bands", [128, 16, 128], BF16)
    BANDS = bands.t
    bigB = sb("bigB", [128, 24, TT], BF16, 24)
    sq = view(bigB.t[:, 0:8, :], bigB.p[0:8])
    silur = sq
    gated = view(bigB.t[:, 8:16, :], bigB.p[8:16])
    ybs = view(bigB.t[:, 16:24, :], bigB.p[16:24])
    utm = view(bigB.t[:, 16:24, :].rearrange("p c t -> p (c t)").rearrange("p (b n) -> p b n", b=4), bigB.p[16:24])
    act = view(bigB.t[:, 0:22, :], bigB.p[0:22])
    vm = sb("vm", [128, 4096], BF16, 4)
    vtm = view(vm.t.rearrange("p (b n) -> p b n", b=4), vm.p)
    mT = view(vm.t.rearrange("p (c t) -> p c t", c=8), [vm.p[j // 2] for j in range(8)])
    gaT = sb("gaT", [128, 8, TT], BF16, 8)
    kf = sb("kf", [128, 2048], F32, 4)
    ktm = view(kf.t.rearrange("p (b d) -> p b d", b=4), kf.p)
    kfv = view(kf.t.rearrange("p (c t) -> p c t", c=4), kf.p)
    gf = sb("gf", [128, 2048], F32, 4)
    gtm = view(gf.t.rearrange("p (b d) -> p b d", b=4), gf.p)
    gfv = view(gf.t.rearrange("p (c t) -> p c t", c=4), gf.p)
    stage = [kfv, gfv]
    glrT = sb("glrT", [32, TT], BF16)
    tmpz = sb("tmpz", [128, 512], F32)
    rstd = tmpz
    E1 = [sb("E1_0", [128, 4, 128], F32)] * 2
    E2 = [sb("E2_0", [128, 4, 128], F32)] * 2
    qdec = [sb("qdec0", [128, 4, 128], BF16)] * 2
    kinv = [sb("kinv0", [128, 4, 128], BF16)] * 2
    er = [sb("er0", [128, 512], F32)] * 2
    kend = [sb("kend0", [128, 512], BF16)] * 2
    attm = [sb("attm0", [128, 4, 128], BF16)] * 2
    sqo = [sb("sqo0", [128, 8, 128], BF16)] * 2
    rso = [sb("rso0", [128, 4, 128], F32)] * 2
    tmo = [sb("tmo0", [128, 8, 128], F32)] * 2
    SfL = [sb(f"Sf{l}", [128, 4, 256], F32) for l in LR]
    SbL = [sb(f"Sb{l}", [128, 4, 256], BF16) for l in LR]
    sila = [sb(f"sila{i}", [128, TT], BF16) for i in range(2)]
    chaloL = [sb(f"chalo{l}", [128, 44, 2], F32, 44) for l in LR]
    ps_t = es.enter_context(nc.psum_tensor("ps", [128, 8, 512], F32))
    banks = [Buf(f"bank{i}", excl=True) for i in range(8)]
    st = {"b1": 0, "b2": 0, "g": 0}

    st["gla"] = False

    def ps1():
        n = 4 if st["gla"] else 8
        i = st["b1"] % n
        st["b1"] = (i + 1) % n
        return ps_t[:, i, :], [banks[i]]

    def ps2():
        i = st["b2"]
        st["b2"] = (i + 1) % 2
        b = 4 + 2 * i
        return ps_t[:, b:b + 2, :].rearrange("p b n -> p (b n)"), [banks[b], banks[b + 1]]

    C = consts.t
    TRIB = C[:, 0:128]
    TRIR = C[:, 128:256]
    MASK4 = C[:, 256:768].rearrange("p (h c) -> p h c", h=4)
    INVCS = {0: C[:, 768:896].rearrange("p (k c) -> p k c", k=8), 2: C[:, 896:1024].rearrange("p (k c) -> p k c", k=8)}

    for l in LR:
        P.dma("sync", vecsL[l].t, vecs_d[l], writes=vecsL[l].all)
    P.dma("sync", consts.t, consts_d, writes=consts.all)
    P.dma("sync", flags.t, flags_d, writes=flags.all)
    P.dma("gpsimd", bands.t, bands_d.rearrange("p (k c) -> p k c", k=16), writes=bands.all)
    for l in LR:
        P.dma("gpsimd", wgkL[l].t[0:17, :], w_gkb[l], writes=wgkL[l].all)
        P.dma("gpsimd", wglrL[l].t, w_in[l][:, 2048:2064].rearrange("(kc p) n -> p kc n", p=128), writes=wglrL[l].all)
    P.op("vector", lambda E: E.memset(ones.t, 1.0), writes=ones.all)
    P.op("vector", lambda E: E.memset(glrT.t, 1.0), writes=glrT.all)
    P.op("vector", lambda E: E.memset(hTv[1].t, 0.0), writes=hTv[1].all)
    for i in range(2):
        P.dma("sync", recvs[i][0:D, :].rearrange("(c p) t -> p c t", p=128), hTv[1].t, reads=hTv[1].all, writes=[recvB[i]])
    for l in LR:
        P.op("vector", lambda E, l=l: E.memset(SfL[l].t, 0.0), writes=SfL[l].all)
        P.op("vector", lambda E, l=l: E.memset(SbL[l].t, 0.0), writes=SbL[l].all)
        P.op("vector", lambda E, l=l: E.memset(chaloL[l].t, 0.0), writes=chaloL[l].all)
        P.op("vector", lambda E, l=l: E.memset(uprevL[l].t, 0.0), writes=uprevL[l].all)
        P.op("vector", lambda E, l=l: E.tensor_scalar(out=vscL[l].t[:, 0:24], in0=vecsL[l].t[:, 0:24], scalar1=32.0, scalar2=None,
                                                      op0=ALU.mult), reads=vecsL[l].all, writes=vscL[l].all)
        P.op("vector", lambda E, l=l: E.tensor_scalar(out=vscL[l].t[:, 24:26], in0=vecsL[l].t[:, 48:50], scalar1=16.0, scalar2=None,
                                                      op0=ALU.mult), reads=vecsL[l].all, writes=vscL[l].all)

    RA = "(kc p) n -> p kc n"

    def grp_cols(f32, scr, c0, n=1024):
        return [(f32[:, c0:c0 + n].rearrange(RA, p=128), scr[:, c0:c0 + n].rearrange(RA, p=128), (slice(0, 8), slice(0, n)))]

    def grp_up(l, g):
        n = 512 if g < 5 else 256
        return [(w_up[l][:, o:o + n].rearrange(RA, p=128), s_up[l][:, o:o + n].rearrange(RA, p=128),
                 (slice(0, 8), slice(h * 512, h * 512 + n))) for h, o in ((0, g * 512), (1, DFF + g * 512))]

    def grp_dn(l, g):
        k0 = g * 8
        nk = 8 if g < 2 else 6
        return [(w_down[l][k0 * 128:(k0 + nk) * 128, :].rearrange(RA, p=128),
                 s_down[l][k0 * 128:(k0 + nk) * 128, :].rearrange(RA, p=128), (slice(0, nk), slice(0, 1024)))]

    tile_groups = []
    for l in LR:
        wi, si = w_in[l], s_in[l]
        tile_groups += [grp_cols(wi, si, 2064), grp_cols(wi, si, 0), grp_cols(wi, si, 1024),
                        grp_cols(wi, si, 3088), grp_cols(wi, si, 5136),
                        grp_cols(w_pool[l], s_pool[l], 0, 256), grp_cols(w_b[l], s_b[l], 0), grp_cols(wi, si, 4112),
                        grp_cols(w_a[l], s_a[l], 0),
                        grp_cols(w_o[l], s_o[l], 0)] + [grp_up(l, g) for g in range(6)] + [grp_dn(l, g) for g in range(3)]
    NGL = 19
    NG = len(tile_groups)
    total_groups = NG * len(tiles)
    ld = {"n": 0}
    scrB = {}

    def ensure(upto):
        upto = min(upto, total_groups - 1)
        while ld["n"] <= upto:
            n = ld["n"]
            sl = slots[n % NSLOT]
            for pi, (f32, scr, (ks, cs)) in enumerate(tile_groups[n % NG]):
                sb_ = scrB.setdefault((n % NG, pi), Buf("scr"))
                if n < NG:
                    P.dma("gpsimd", sl.t[:, ks, cs], f32, writes=sl.all)
                    P.dma("sync", scr, sl.t[:, ks, cs], reads=sl.all, writes=[sb_])
                else:
                    P.dma("sync", sl.t[:, ks, cs], scr, reads=[sb_], writes=sl.all)
            ld["n"] += 1

    def use(gidx, span=1):
        ensure(gidx + span - 1 + (NSLOT - span))
        return slots[gidx % NSLOT]

    def proj_fm(sl, col0, rhsT, nk, T, bank):
        return [lambda E, kc=kc: E.matmul(bank[:, 0:T], lhsT=sl.t[:, kc, col0:col0 + 128], rhs=rhsT.t[:, kc, 0:T],
                                          start=(kc == 0), stop=(kc == nk - 1)) for kc in range(nk)]

    def rmsnorm_to(dst, gcol, T, vsc, hT, outf32=None):
        for c in range(8):
            if c % 2 == 0:
                P.op("scalar", lambda E, c=c: E.activation(out=sq.t[:, c, 0:T], in_=hT.t[:, c, 0:T], func=AF.Square),
                     reads=[hT.p[c]], writes=[sq.p[c]])
            else:
                P.op("vector", lambda E, c=c: E.tensor_tensor(out=sq.t[:, c, 0:T], in0=hT.t[:, c, 0:T], in1=hT.t[:, c, 0:T],
                                                              op=ALU.mult), reads=[hT.p[c]], writes=[sq.p[c]])
        bank, bb = ps1()
        P.pe([lambda E, c=c: E.matmul(bank[:, 0:T], lhsT=ones.t[:, :], rhs=sq.t[:, c, 0:T], start=(c == 0),
                                      stop=(c == 7)) for c in range(8)], reads=sq.all + ones.all, writes=bb)
        P.op("scalar", lambda E: E.activation(out=rstd.t[:, 0:T], in_=bank[:, 0:T], func=AF.Ln, bias=float(D * EPS)),
             reads=bb, writes=rstd.all)
        P.op("scalar", lambda E: E.activation(out=rstd.t[:, 0:T], in_=rstd.t[:, 0:T], func=AF.Exp, scale=-0.5),
             reads=rstd.all, writes=rstd.all)
        for c in range(8):
            tgt, ci = (dst, c) if outf32 is None else (outf32[c // 4], c % 4)
            P.op("vector", lambda E, c=c, tgt=tgt, ci=ci: E.scalar_tensor_tensor(
                out=tgt.t[:, ci, 0:T], in0=hT.t[:, c, 0:T], scalar=vsc.t[:, gcol + c:gcol + c + 1],
                in1=rstd.t[:, 0:T], op0=ALU.mult, op1=ALU.mult),
                reads=[hT.p[c]] + rstd.all + vsc.all, writes=[tgt.p[ci]])

    out_recs = []

    def fm_proj(gidx, col0s, rhsT, T, evac):
        sl = use(gidx)
        for j0 in range(0, len(col0s), 2):
            grp = []
            fns = []
            wr = []
            fine = []
            for j in range(j0, min(j0 + 2, len(col0s))):
                bank, bb = ps1()
                fns += proj_fm(sl, col0s[j], rhsT, 8, T, bank)
                fine += [[rhsT.p[kc]] for kc in range(8)]
                wr += bb
                grp.append((j, bank, bb))
            P.pe(fns, reads=sl.all, writes=wr, fine=fine)
            for j, bank, bb in grp:
                evac(j, bank, bb)

    def assemble(step):
        nh = hTv[step % 2]
        c0 = step * TT
        P.dma("sync", nh.t, xT[:, c0:c0 + TT].rearrange("(c p) t -> p c t", p=128), writes=nh.all)
        rb = recvs[step % 2]
        for hf in range(2):
            P.dma("sync", stage[hf].t, rb[hf * 512:(hf + 1) * 512, :].rearrange("(c p) t -> p c t", p=128),
                  reads=[recvB[step % 2]], writes=stage[hf].all)
        for c in range(8):
            P.op("vector", lambda E, c=c: E.scalar_tensor_tensor(
                out=nh.t[:, c, :], in0=stage[c // 4].t[:, c % 4, :], scalar=flags.t[:, 0:1], in1=nh.t[:, c, :],
                op0=ALU.mult, op1=ALU.add), reads=[stage[c // 4].p[c % 4], nh.p[c]] + flags.all, writes=[nh.p[c]])

    def do_tile(ti, t0, T):
        hT = hTv[ti % 2]

        def pre_down():
            if ti + 1 < nsteps:
                assemble(ti + 1)
        for l in LR:
            do_layer(l, ti, t0, T, pre_down)
        P.dma("sync", sends[ti % 2].rearrange("(c p) t -> p c t", p=128), hT.t[:, :, 0:T], reads=hT.all,
              writes=[sendB[ti % 2]])
        if pair_groups is not None:
            P.cc(lambda E, i=ti % 2: E.collective_compute("AllGather", ALU.bypass, replica_groups=pair_groups,
                                                          ins=[sends[i].opt()], outs=[recvs[i].opt()]),
                 reads=[sendB[ti % 2]], writes=[recvB[ti % 2]])
        rmsnorm_to(None, 16, T, vscL[NL - 1], hT, outf32=stage)
        for hf in range(2):
            out_recs.append(P.dma("sync", outT[hf * 512:(hf + 1) * 512, t0:t0 + T].rearrange("(c p) t -> p c t", p=128),
                                  stage[hf].t, reads=stage[hf].all))

    def do_layer(l, ti, t0, T, pre_down):
        first = (ti == 0)
        fix = ti in INVCS
        nblk = T // 128
        gb0 = ti * NG + l * NGL
        EE = 16 + T
        vecs, vsc, wglr, wgk = vecsL[l], vscL[l], wglrL[l], wgkL[l]
        Sf, Sb, uprev, chalo = SfL[l], SbL[l], uprevL[l], chaloL[l]
        GP = "vector" if ti == 0 else "gpsimd"
        hT = hTv[ti % 2]
        qT, kT, mb, yext, cacc = XV[(ti + 1) % 2]
        V = vecs.t
        rmsnorm_to(hnT, 0, T, vsc, hT)
        bank, bb = ps1()
        P.pe([lambda E, kc=kc, bank=bank: E.matmul(bank[0:16, 0:T], lhsT=wglr.t[:, kc, :], rhs=hnT.t[:, kc, 0:T],
                                                    start=(kc == 0), stop=(kc == 7)) for kc in range(8)],
             reads=wglr.all + hnT.all, writes=bb)
        P.op("vector", lambda E, bank=bank: E.tensor_copy(out=glrT.t[0:16, 0:T], in_=bank[0:16, 0:T]),
             reads=bb, writes=glrT.all)
        for blk in range(nblk):
            bank, bb = ps1()
            P.pe([lambda E, blk=blk, bank=bank: E.matmul(bank[:, :], lhsT=glrT.t[0:17, blk * 128:(blk + 1) * 128],
                                                          rhs=wgk.t[0:17, :], start=True, stop=True)],
                 reads=glrT.all + wgk.all, writes=bb)
            P.op("scalar", lambda E, bank=bank: E.activation(out=tmpz.t[:, :], in_=bank[:, :], func=AF.Exp, scale=-1.0),
                 reads=bb, writes=tmpz.all)
            P.op("scalar", lambda E, blk=blk: E.activation(out=gtm.t[:, blk, :], in_=tmpz.t[:, :], func=AF.Ln, bias=1.0),
                 reads=tmpz.all, writes=[gtm.p[blk]])
        if STOP < 4:
            return
        if STOP < 2:
            return
        fm_proj(gb0 + 0, [j * 128 for j in range(8)], hnT, T,
                lambda j, bank, bb: P.op("scalar", lambda E: E.activation(out=silur.t[:, j, 0:T], in_=bank[:, 0:T], func=AF.Silu),
                                         reads=bb, writes=[silur.p[j]]))
        if STOP < 3:
            return
        def ev_qk(j, bank, bb):
            if j < 4:
                P.op("scalar", lambda E: E.activation(out=qT.t[:, j, 0:T], in_=bank[:, 0:T], func=AF.Identity,
                                                      scale=float(128 ** -0.5)), reads=bb, writes=[qT.p[j]])
            else:
                P.op("scalar", lambda E: E.activation(out=kT.t[:, j - 4, 0:T], in_=bank[:, 0:T], func=AF.Copy),
                     reads=bb, writes=[kT.p[j - 4]])
        fm_proj(gb0 + 1, [j * 128 for j in range(8)], hnT, T, ev_qk)
        sl = use(gb0 + 1)
        for blk in range(nblk):
            bank, bb = ps1()
            P.pe([lambda E, kc=kc, blk=blk, bank=bank, sl=sl: E.matmul(
                bank[:, :], lhsT=hnT.t[:, kc, blk * 128:(blk + 1) * 128], rhs=sl.t[:, kc, 512:1024],
                start=(kc == 0), stop=(kc == 7)) for kc in range(8)], reads=sl.all + hnT.all, writes=bb)
            P.op("scalar", lambda E, blk=blk, bank=bank: E.activation(out=ktm.t[:, blk, :], in_=bank[:, :], func=AF.Copy),
                 reads=bb, writes=[ktm.p[blk]])
        sl = use(gb0 + 2)
        for blk in range(nblk):
            pbk = [ps1(), ps1()]
            fns = []
            for hf in range(2):
                fns += [lambda E, kc=kc, blk=blk, hf=hf, bank=pbk[hf][0], sl=sl: E.matmul(
                    bank[:, :], lhsT=hnT.t[:, kc, blk * 128:(blk + 1) * 128], rhs=sl.t[:, kc, hf * 512:(hf + 1) * 512],
                    start=(kc == 0), stop=(kc == 7)) for kc in range(8)]
            P.pe(fns, reads=sl.all + hnT.all, writes=pbk[0][1] + pbk[1][1])
            for hf in range(2):
                bank, bb = pbk[hf]
                P.op("vector" if hf == 0 else "scalar",
                     (lambda E, blk=blk, hf=hf, bank=bank: E.tensor_copy(out=vtm.t[:, blk, hf * 512:(hf + 1) * 512], in_=bank[:, :]))
                     if hf == 0 else
                     (lambda E, blk=blk, hf=hf, bank=bank: E.activation(out=vtm.t[:, blk, hf * 512:(hf + 1) * 512], in_=bank[:, :], func=AF.Copy)),
                     reads=bb, writes=[vtm.p[blk]])
        fillers = []

        def mk_u(blk, hf):
            def f():
                sl = use(gb0 + 3)
                bank, bb = ps1()
                P.pe([lambda E, kc=kc: E.matmul(
                    bank[:, :], lhsT=hnT.t[:, kc, blk * 128:(blk + 1) * 128], rhs=sl.t[:, kc, hf * 512:(hf + 1) * 512],
                    start=(kc == 0), stop=(kc == 7)) for kc in range(8)], reads=sl.all + hnT.all, writes=bb)
                P.op("vector", lambda E: E.tensor_copy(out=utm.t[:, blk, hf * 512:(hf + 1) * 512], in_=bank[:, :]),
                     reads=bb, writes=utm.p[2 * blk:2 * blk + 2])
            return f

        def mk_gb(j):
            def f():
                sl = use(gb0 + 4)
                bank, bb = ps1()
                P.pe(proj_fm(sl, j * 128, hnT, 8, T, bank), reads=sl.all + hnT.all, writes=bb)
                P.op("vector", lambda E: E.tensor_copy(out=gaT.t[:, j, 0:T], in_=bank[:, 0:T]), reads=bb, writes=[gaT.p[j]])
            return f

        for blk in range(nblk):
            for hf in range(2):
                fillers.append(mk_u(blk, hf))
        for j in range(8):
            fillers.append(mk_gb(j))

        def run_fillers(n):
            for _ in range(min(n, len(fillers))):
                fillers.pop(0)()

        st["gla"] = True
        st["b1"] = 0
        for blk in range(nblk):
            pb = st["g"] % 2
            st["g"] += 1
            cs = slice(blk * 128, (blk + 1) * 128)
            bkA, bbA = ps1()
            P.pe([lambda E, h=h, blk=blk, bkA=bkA: E.matmul(bkA[:, h * 128:(h + 1) * 128],
                                                             lhsT=gtm.t[:, blk, h * 128:(h + 1) * 128], rhs=TRIB,
                                                             start=True, stop=True) for h in range(4)],
                 reads=[gtm.p[blk]] + consts.all, writes=bbA)
            bkB, bbB = ps1()
            P.pe([lambda E, blk=blk, bkB=bkB: E.matmul(bkB[:, :], lhsT=TRIR, rhs=gtm.t[:, blk, :], start=True, stop=True)],
                 reads=[gtm.p[blk]] + consts.all, writes=bbB)
            A3 = bkA.rearrange("p (h c) -> p h c", h=4)
            run_fillers(2)
            P.op("scalar", lambda E, pb=pb, A3=A3: E.activation(out=E1[pb].t, in_=A3, func=AF.Exp),
                 reads=bbA, writes=E1[pb].all)
            P.op("scalar", lambda E, pb=pb, A3=A3: E.activation(out=E2[pb].t, in_=A3, func=AF.Exp, scale=-1.0),
                 reads=bbA, writes=E2[pb].all)
            P.op("scalar", lambda E, pb=pb, bkB=bkB: E.activation(out=er[pb].t, in_=bkB[:, :], func=AF.Exp),
                 reads=bbB, writes=er[pb].all)
            P.op("vector", lambda E, pb=pb, cs=cs: E.tensor_tensor(out=qdec[pb].t, in0=qT.t[:, :, cs], in1=E1[pb].t,
                                                                   op=ALU.mult), reads=qT.all + E1[pb].all, writes=qdec[pb].all)
            P.op("vector", lambda E, pb=pb, cs=cs: E.tensor_tensor(out=kinv[pb].t, in0=kT.t[:, :, cs], in1=E2[pb].t,
                                                                   op=ALU.mult), reads=kT.all + E2[pb].all, writes=kinv[pb].all)
            P.op("vector", lambda E, pb=pb, blk=blk: E.tensor_tensor(out=kend[pb].t, in0=ktm.t[:, blk, :], in1=er[pb].t,
                                                                     op=ALU.mult), reads=[ktm.p[blk]] + er[pb].all, writes=kend[pb].all)
            bkC, bbC = ps1()
            P.pe([lambda E, h=h, pb=pb, bkC=bkC: E.matmul(bkC[:, h * 128:(h + 1) * 128], lhsT=kinv[pb].t[:, h, :],
                                                           rhs=qdec[pb].t[:, h, :], start=True, stop=True) for h in range(4)],
                 reads=kinv[pb].all + qdec[pb].all, writes=bbC)
            C3 = bkC.rearrange("p (h c) -> p h c", h=4)
            run_fillers(1)
            P.op("vector", lambda E, pb=pb, C3=C3: E.tensor_tensor(out=attm[pb].t, in0=C3, in1=MASK4, op=ALU.mult),
                 reads=bbC + consts.all, writes=attm[pb].all)
            bkD, bbD = ps2()
            fns = []
            for h in range(4):
                for ec in range(2):
                    o = (h * 2 + ec) * 128
                    fns.append(lambda E, h=h, ec=ec, o=o, blk=blk, pb=pb, bkD=bkD: E.matmul(
                        bkD[:, o:o + 128], lhsT=vtm.t[:, blk, h * 256 + ec * 128:h * 256 + ec * 128 + 128],
                        rhs=attm[pb].t[:, h, :], start=True, stop=False))
                    fns.append(lambda E, h=h, ec=ec, o=o, pb=pb, bkD=bkD: E.matmul(
                        bkD[:, o:o + 128], lhsT=Sb.t[:, h, ec * 128:(ec + 1) * 128], rhs=qdec[pb].t[:, h, :],
                        start=False, stop=True))
            P.pe(fns, reads=[vtm.p[blk]] + attm[pb].all + Sb.all + qdec[pb].all, writes=bbD)
            bkE, bbE = ps2()
            P.pe([lambda E, h=h, blk=blk, pb=pb, bkE=bkE: E.matmul(bkE[:, h * 256:(h + 1) * 256],
                                                                    lhsT=kend[pb].t[:, h * 128:(h + 1) * 128],
                                                                    rhs=vtm.t[:, blk, h * 256:(h + 1) * 256],
                                                                    start=True, stop=True) for h in range(4)],
                 reads=kend[pb].all + [vtm.p[blk]], writes=bbE)
            for h in range(4):
                P.op("vector", lambda E, h=h, pb=pb, bkE=bkE: E.scalar_tensor_tensor(
                    out=Sf.t[:, h, :], in0=Sf.t[:, h, :], scalar=E1[pb].t[:, h, 127:128], in1=bkE[:, h * 256:(h + 1) * 256],
                    op0=ALU.mult, op1=ALU.add), reads=Sf.all + E1[pb].all + bbE, writes=Sf.all)
            P.op("scalar", lambda E: E.activation(out=Sb.t, in_=Sf.t, func=AF.Copy), reads=Sf.all, writes=Sb.all)
            run_fillers(1)
            D3 = bkD.rearrange("p (k c) -> p k c", k=8)
            P.op("scalar", lambda E, pb=pb, D3=D3: E.activation(out=sqo[pb].t, in_=D3, func=AF.Square),
                 reads=bbD, writes=sqo[pb].all)
            bkF, bbF = ps1()
            fns = []
            for h in range(4):
                for ec in range(2):
                    fns.append(lambda E, h=h, ec=ec, pb=pb, bkF=bkF: E.matmul(
                        bkF[:, h * 128:(h + 1) * 128], lhsT=ones.t[:, :], rhs=sqo[pb].t[:, h * 2 + ec, :],
                        start=(ec == 0), stop=(ec == 1)))
            P.pe(fns, reads=sqo[pb].all + ones.all, writes=bbF)
            F3 = bkF.rearrange("p (h c) -> p h c", h=4)
            P.op("scalar", lambda E, pb=pb, F3=F3: E.activation(out=rso[pb].t, in_=F3, func=AF.Ln, bias=float(256 * EPS)),
                 reads=bbF, writes=rso[pb].all)
            P.op("scalar", lambda E, pb=pb: E.activation(out=rso[pb].t, in_=rso[pb].t, func=AF.Exp, scale=-0.5),
                 reads=rso[pb].all, writes=rso[pb].all)
            D4 = bkD.rearrange("p (h e c) -> p h e c", h=4, e=2)
            T4 = tmo[pb].t.rearrange("p (h e) c -> p h e c", h=4)
            for ec in range(2):
                P.op("vector", lambda E, ec=ec, pb=pb, D4=D4, T4=T4: E.scalar_tensor_tensor(
                    out=T4[:, :, ec, :], in0=D4[:, :, ec, :], scalar=vsc.t[:, 24 + ec:25 + ec], in1=rso[pb].t,
                    op0=ALU.mult, op1=ALU.mult), reads=bbD + rso[pb].all + vsc.all, writes=tmo[pb].all)
            P.op(GP, lambda E, pb=pb, cs=cs: E.tensor_tensor(out=gated.t[:, :, cs], in0=tmo[pb].t, in1=silur.t[:, :, cs],
                                                                   op=ALU.mult), reads=tmo[pb].all + silur.all, writes=gated.all)
        if STOP < 5:
            return
        st["gla"] = False
        run_fillers(len(fillers))
        for c in range(8):
            g = c // 2
            bank, bb = ps1()
            fns = []
            for blk in range(nblk):
                bd = BANDS[:, g, :]
                if ti in (0, 2) and blk == nblk - 1:
                    bd = BANDS[:, 8 + 4 * (ti // 2) + g, :]
                fns.append(lambda E, blk=blk, c=c, bank=bank, bd=bd: E.matmul(
                    bank[:, blk * 128:(blk + 1) * 128], lhsT=utm.t[:, blk, c * 128:(c + 1) * 128], rhs=bd, start=True, stop=False))
                prev = uprev.t[:, c * 128:(c + 1) * 128] if blk == 0 else utm.t[:, blk - 1, c * 128:(c + 1) * 128]
                fns.append(lambda E, blk=blk, bank=bank, g=g, prev=prev: E.matmul(
                    bank[:, blk * 128:(blk + 1) * 128], lhsT=prev, rhs=BANDS[:, 4 + g, :], start=False, stop=True))
            P.pe(fns, reads=utm.all + uprev.all + bands.all, writes=bb)
            P.op("scalar" if c % 2 == 0 else "vector",
                 (lambda E, c=c, bank=bank: E.activation(out=sq.t[:, c, 0:T], in_=bank[:, 0:T], func=AF.Copy)) if c % 2 == 0 else
                 (lambda E, c=c, bank=bank: E.tensor_copy(out=sq.t[:, c, 0:T], in_=bank[:, 0:T])),
                 reads=bb, writes=[sq.p[c]])
        P.op(GP, lambda E: E.tensor_copy(out=uprev.t, in_=utm.t[:, nblk - 1, :]), reads=utm.all, writes=uprev.all)
        wpool = use(gb0 + 5)
        for j in range(8):
            g = j // 2
            jj = j % 2
            bank, bb = ps1()
            P.pe([lambda E, kc=kc, g=g, jj=jj, bank=bank: E.matmul(
                bank[:, 0:T], lhsT=wpool.t[:, 2 * g + kc, jj * 128:(jj + 1) * 128], rhs=sq.t[:, 2 * g + kc, 0:T],
                start=(kc == 0), stop=(kc == 1)) for kc in range(2)],
                reads=wpool.all + [sq.p[2 * g], sq.p[2 * g + 1]], writes=bb)
            P.op("scalar", lambda E, j=j, bank=bank: E.activation(out=ybs.t[:, j, 0:T], in_=bank[:, 0:T], func=AF.Identity,
                                                                   scale=V[:, 24 + j:25 + j]), reads=bb + vecs.all, writes=[ybs.p[j]])
        if STOP < 6:
            return
        for j in range(8):
            P.op("scalar", lambda E, j=j: E.activation(out=gaT.t[:, j, 0:T], in_=gaT.t[:, j, 0:T], func=AF.Sigmoid,
                                                       bias=V[:, 40 + j:41 + j]), reads=[gaT.p[j]] + vecs.all, writes=[gaT.p[j]])
        fm_proj(gb0 + 6, [j * 128 for j in range(8)], ybs, T,
                lambda j, bank, bb: P.op("vector", lambda E: E.tensor_tensor(out=mb.t[:, j, 0:T], in0=bank[:, 0:T], in1=gaT.t[:, j, 0:T],
                                                                             op=ALU.mult), reads=bb + [gaT.p[j]], writes=[mb.p[j]]))
        fm_proj(gb0 + 7, [j * 128 for j in range(8)], hnT, T,
                lambda j, bank, bb: P.op("scalar", lambda E: E.activation(out=gaT.t[:, j, 0:T], in_=bank[:, 0:T], func=AF.Sigmoid,
                                                                          bias=V[:, 32 + j:33 + j]), reads=bb + vecs.all, writes=[gaT.p[j]]))
        def ev_ya(j, bank, bb):
            P.op("vector", lambda E: E.tensor_tensor(out=tmpz.t[:, 0:T], in0=bank[:, 0:T], in1=gaT.t[:, j, 0:T], op=ALU.mult),
                 reads=bb + [gaT.p[j]], writes=tmpz.all)
            P.op(GP if j % 2 == 0 else "vector",
                 lambda E: E.tensor_tensor(out=mT.t[:, j, 0:T], in0=tmpz.t[:, 0:T], in1=mb.t[:, j, 0:T], op=ALU.add),
                 reads=tmpz.all + [mb.p[j]], writes=[mT.p[j]])
        fm_proj(gb0 + 8, [j * 128 for j in range(8)], gated, T, ev_ya)
        fm_proj(gb0 + 9, [j * 128 for j in range(8)], mT, T,
                lambda j, bank, bb: P.op("vector", lambda E: E.tensor_tensor(out=hT.t[:, j, 0:T], in0=hT.t[:, j, 0:T], in1=bank[:, 0:T],
                                                                             op=ALU.add), reads=bb + [hT.p[j]], writes=[hT.p[j]]))
        if STOP < 7:
            return
        rmsnorm_to(hnT, 8, T, vsc, hT)
        yi = 0
        for g in range(6):
            sl = use(gb0 + 10 + g)
            npair = 4 if g < 5 else 2
            for jj in range(npair):
                pj = g * 4 + jj
                res = []
                pbk = [ps1(), ps1()]
                P.pe(proj_fm(sl, jj * 128, hnT, 8, T, pbk[0][0]) + proj_fm(sl, 512 + jj * 128, hnT, 8, T, pbk[1][0]),
                     reads=sl.all, writes=pbk[0][1] + pbk[1][1], fine=[[hnT.p[kc]] for kc in range(8)] * 2)
                for half in range(2):
                    ch = pj + 22 * half
                    bank, bb = pbk[half]
                    ye = yext[yi % NY]
                    ca = cacc[yi % NY]
                    yi += 1
                    P.op(GP, lambda E, ye=ye, ch=ch: E.tensor_copy(out=ye.t[:, 0:2], in_=chalo.t[:, ch, :]),
                         reads=[chalo.p[ch]], writes=ye.all)
                    P.op("scalar", lambda E, ye=ye, bank=bank: E.activation(out=ye.t[:, 2:2 + T], in_=bank[:, 0:T], func=AF.Copy),
                         reads=bb, writes=ye.all)
                    P.op("scalar", lambda E, ca=ca, bank=bank, ch=ch: E.activation(
                        out=ca.t[:, 0:T], in_=bank[:, 0:T], func=AF.Identity, scale=V[:, 138 + ch:139 + ch],
                        bias=V[:, 182 + ch:183 + ch]), reads=bb + vecs.all, writes=ca.all)
                    P.op(GP, lambda E, ye=ye, ch=ch, T=T: E.tensor_copy(out=chalo.t[:, ch, :], in_=ye.t[:, T:T + 2]),
                         reads=ye.all, writes=[chalo.p[ch]])
                    P.op("vector", lambda E, ye=ye, ca=ca, ch=ch: E.scalar_tensor_tensor(
                        out=ca.t[:, 0:T], in0=ye.t[:, 1:1 + T], scalar=V[:, 94 + ch:95 + ch], in1=ca.t[:, 0:T],
                        op0=ALU.mult, op1=ALU.add), reads=ye.all + ca.all + vecs.all, writes=ca.all)
                    P.op("vector", lambda E, ye=ye, ca=ca, ch=ch: E.scalar_tensor_tensor(
                        out=ca.t[:, 0:T], in0=ye.t[:, 0:T], scalar=V[:, 50 + ch:51 + ch], in1=ca.t[:, 0:T],
                        op0=ALU.mult, op1=ALU.add), reads=ye.all + ca.all + vecs.all, writes=ca.all)
                    res.append(ca)
                sa = sila[pj % 2]
                P.op("scalar", lambda E, sa=sa, ca=res[0]: E.activation(out=sa.t[:, 0:T], in_=ca.t[:, 0:T], func=AF.Silu),
                     reads=res[0].all, writes=sa.all)
                P.op("vector", lambda E, sa=sa, cb=res[1], pj=pj: E.tensor_tensor(out=act.t[:, pj, 0:T], in0=sa.t[:, 0:T],
                                                                                   in1=cb.t[:, 0:T], op=ALU.mult),
                     reads=sa.all + res[1].all, writes=[act.p[pj]])
        if STOP < 8:
            return
        pre_down()
        sls = [use(gb0 + 16, span=3), slots[(gb0 + 17) % NSLOT], slots[(gb0 + 18) % NSLOT]]
        for j in range(8):
            bank, bb = ps1()
            P.pe([lambda E, kc=kc, j=j, bank=bank, sls=sls: E.matmul(
                bank[:, 0:T], lhsT=sls[kc // 8].t[:, kc % 8, j * 128:(j + 1) * 128], rhs=act.t[:, kc, 0:T],
                start=(kc == 0), stop=(kc == 21)) for kc in range(22)],
                reads=sls[0].all + sls[1].all + sls[2].all, writes=bb, fine=[[act.p[kc]] for kc in range(22)])
            P.op("vector", lambda E, j=j, bank=bank: E.tensor_tensor(out=hT.t[:, j, 0:T], in0=hT.t[:, j, 0:T], in1=bank[:, 0:T],
                                                                     op=ALU.add), reads=bb + [hT.p[j]], writes=[hT.p[j]])
        if first:
            for c in range(8):
                P.op(GP, lambda E, c=c: E.memset(hT.t[:, c, 0:PAD], 0.0), writes=[hT.p[c]])

    assemble(0)
    for ti, (t0, T) in enumerate(tiles):
        do_tile(ti, t0, T)
    for r in out_recs:
        P.final_wait("sync", r)
    with nc.Block() as block:
        P.emit(block)
    es.close()
    return nc


def _consts(role):
    c = np.zeros((128, NCONST), np.float32)
    s = np.arange(128)[:, None]
    cc = np.arange(128)[None, :]
    c[:, 0:128] = np.where(s <= cc, -1.0 / 16.0, 0.0)
    c[:, 128:256] = np.where(s > cc, -1.0 / 16.0, 0.0)
    m = np.where(s <= cc, 1.0, 0.0)
    c[:, 256:768] = np.tile(m, (1, 4))
    real = np.zeros((8, 16), np.float32)
    plain = np.zeros((8, 16), np.float32)
    for ch in range(8):
        w = 2 ** (ch // 2 + 1)
        for j in range(16):
            real[ch, j] = 1.0 / min(j + 1, w)
            plain[ch, j] = 1.0 / w
    c[:, 768:896] = np.broadcast_to((real if role == 0 else plain).reshape(1, 128), (128, 128))
    c[:, 896:1024] = np.broadcast_to((plain if role == 0 else real).reshape(1, 128), (128, 128))
    return c


def _vecs(inp, l):
    v = np.zeros((128, NV), np.float32)
    fm = lambda a: np.ascontiguousarray(np.asarray(a, np.float32).reshape(-1, 128).T)
    v[:, 0:8] = fm(inp["norm1_g"][l])
    v[:, 8:16] = fm(inp["norm2_g"][l])
    v[:, 16:24] = fm(inp["final_norm_g"])
    v[:, 24:32] = fm(inp["pool_scale"][l])
    v[:, 32:40] = fm(inp["b_gates"][l][:D])
    v[:, 40:48] = fm(inp["b_gates"][l][D:])
    v[:, 48:50] = fm(inp["gla_norm_g"][l])
    v[:, 50:94] = fm(inp["conv_w"][l][0])
    v[:, 94:138] = fm(inp["conv_w"][l][1])
    v[:, 138:182] = fm(inp["conv_w"][l][2])
    v[:, 182:226] = fm(inp["conv_b"][l])
    return v


def _layer_map(inp, l):
    f = lambda a: np.ascontiguousarray(np.asarray(a, np.float32))
    return {
        f"w_in{l}": f(inp["w_in"][l]), f"w_a{l}": f(inp["w_a"][l]), f"w_b{l}": f(inp["w_b"][l]), f"w_o{l}": f(inp["w_o"][l]),
        f"w_pool{l}": f(np.asarray(inp["w_pool_grp"][l]).reshape(D, 256)), f"w_up{l}": f(inp["w_up"][l]),
        f"w_down{l}": f(inp["w_down"][l]),
        f"w_gkb{l}": f(np.concatenate([np.asarray(inp["w_gk"][l]), np.asarray(inp["b_gk"][l])[None, :]], axis=0)),
        f"vecs{l}": _vecs(inp, l),
    }


_NC_CACHE = {}
PAIRS = [[0, 1], [2, 3], [4, 5], [6, 7]]


def _get_nc(nsteps):
    if nsteps not in _NC_CACHE:
        _NC_CACHE[nsteps] = build_program(nsteps, NL=1, pair_groups=PAIRS)
    return _NC_CACHE[nsteps]


def make_xT(inp, b, ntok, nsteps):
    x = np.asarray(inp["x"], np.float32)
    meta = np.asarray(inp["meta_tokens"], np.float32)
    full = np.zeros((nsteps * TT, D), np.float32)
    full[PAD:PAD + NMETA] = meta
    full[PAD + NMETA:PAD + NMETA + ntok] = x[b, :ntok]
    return np.ascontiguousarray(full.T)


def _bands(role):
    out = np.zeros((16, 128, 128), np.float32)
    tp = np.arange(128)[:, None]
    t = np.arange(128)[None, :]
    for g in range(4):
        w = 2 ** (g + 1)
        d = t - tp
        out[g] = np.where((d >= 0) & (d < w), 1.0 / w, 0.0) - np.where(d == 0, 1.0, 0.0)
        ds = 128 + t - tp
        out[4 + g] = np.where((ds >= 1) & (ds < w), 1.0 / w, 0.0)
        cnt = np.where(t >= 112, np.minimum(t - 111, w), w).astype(np.float32)
        special = np.where((d >= 0) & (d < w), 1.0 / cnt, 0.0) - np.where(d == 0, 1.0, 0.0)
        out[8 + g] = special if role == 0 else out[g]
        out[12 + g] = out[g] if role == 0 else special
    return np.ascontiguousarray(out.transpose(1, 0, 2).reshape(128, 16 * 128))


def core_map(inp, b, role, ntok, nsteps):
    m = {k[:-1] + "0": v for k, v in _layer_map(inp, role).items()}
    m["consts"] = _consts(role)
    fl = np.zeros((128, 4), np.float32)
    fl[:, 0] = float(role)
    m["flags"] = fl
    m["bands"] = _bands(role)
    m["xT"] = make_xT(inp, b, ntok, nsteps) if role == 0 else np.zeros((D, nsteps * TT), np.float32)
    return m


def kernel(**inputs):
    nc = _get_nc(NSTEP)
    in_maps = [core_map(inputs, core // 2, core % 2, SEQ, NSTEP) for core in range(8)]
    res = run_bass_kernel_spmd(nc, in_maps, core_ids=list(range(8)))
    out = np.stack([np.ascontiguousarray(res.results[2 * b + 1]["outT"][:, 3 * TT:NSTEP * TT].T) for b in range(BATCH)],
                   axis=0)
    return out.astype(np.float32)
```

```python
from contextlib import ExitStack
import os
STOP = int(os.environ.get('MK_STOP', '99'))
import numpy as np
import concourse.bass as bass
import concourse.mybir as mybir
from concourse.bass_utils import run_bass_kernel_spmd

F32 = mybir.dt.float32
BF16 = mybir.dt.bfloat16
AF = mybir.ActivationFunctionType
ALU = mybir.AluOpType

D = 1024
NMETA = 16
SEQ = 8192
BATCH = 4
PAD = 496
NT = (PAD + NMETA + SEQ) // 512
NSTEP = NT + 2
NPOS = NSTEP * 512
INW = 6160
DFF = 2816
F2 = 2 * DFF
EPS = 1e-6
TT = 512
NV = 226
NCONST = 1024
NSLOT = 4
NDS = 8

ENGS = ("tensor", "vector", "scalar", "gpsimd", "sync")


class Buf:
    __slots__ = ("name", "w", "r", "excl")

    def __init__(self, name, excl=False):
        self.name = name
        self.w = None
        self.r = {}
        self.excl = excl


class TB:
    def __init__(self, t, name, nparts=1):
        self.t = t
        self.p = [Buf(f"{name}.{i}") for i in range(nparts)]

    @property
    def all(self):
        return self.p


class Prog:
    def __init__(self, nc, es):
        self.nc = nc
        self.q = {e: [] for e in ENGS}
        self.cnt = {e: 0 for e in ENGS}
        self.sem = {e: es.enter_context(nc.semaphore("s_" + e)) for e in ENGS}
        self.waited = {e: {} for e in ENGS}
        nds = {"sync": 12, "gpsimd": 12}
        self.dsem = {e: [es.enter_context(nc.semaphore(f"d_{e}{i}")) for i in range(nds[e])] for e in nds}
        self.dcnt = {e: [0] * nds[e] for e in self.dsem}
        self.dnext = {e: 0 for e in self.dsem}
        self.ccsem = es.enter_context(nc.semaphore("s_cc"))
        self.ccn = 0

    def _wait(self, eng, sv):
        sem, val, key, src = sv
        if self.waited[eng].get(key, 0) >= val:
            return
        self.waited[eng][key] = val
        self.q[eng].append(lambda E, sem=sem, val=val: E.wait_ge(sem, val))

    def _deps(self, eng, reads, writes):
        for b in reads:
            if b.w is not None:
                if not (eng == "tensor" and b.w[3] == "tensor"):
                    self._wait(eng, b.w)
            if b.excl:
                for r in b.r.values():
                    if r[3] != eng:
                        self._wait(eng, r)
        for b in writes:
            if b.w is not None:
                if not (eng == "tensor" and b.w[3] == "tensor"):
                    self._wait(eng, b.w)
            for r in b.r.values():
                if r[3] == eng and eng != "dma":
                    continue
                self._wait(eng, r)

    def _record(self, rec, reads, writes):
        for b in reads:
            b.r[rec[2]] = rec
        for b in writes:
            b.w = rec
            b.r = {}

    def op(self, eng, fn, reads=(), writes=()):
        self._deps(eng, reads, writes)
        self.cnt[eng] += 1
        c = self.cnt[eng]
        sem = self.sem[eng]
        self.q[eng].append(lambda E, fn=fn, sem=sem: fn(E).then_inc(sem, 1))
        self._record((sem, c, "e_" + eng, eng), reads, writes)

    def pe(self, fns, reads=(), writes=(), fine=None):
        eng = "tensor"
        self._deps(eng, reads, writes)
        self.cnt[eng] += 1
        c = self.cnt[eng]
        sem = self.sem[eng]
        allr = list(reads)
        for i, f in enumerate(fns):
            if fine is not None:
                self._deps(eng, fine[i], ())
                allr += list(fine[i])
            if i < len(fns) - 1:
                self.q[eng].append(lambda E, f=f: f(E))
            else:
                self.q[eng].append(lambda E, f=f, sem=sem: f(E).then_inc(sem, 1))
        self._record((sem, c, "e_tensor", eng), allr, writes)

    def dma(self, eng, out, in_, reads=(), writes=()):
        i = self.dnext[eng]
        self.dnext[eng] = (i + 1) % len(self.dsem[eng])
        sem = self.dsem[eng][i]
        key = f"d_{eng}{i}"
        if self.dcnt[eng][i] > 0:
            self._wait(eng, (sem, self.dcnt[eng][i], key, "dma"))
        self._deps(eng, reads, writes)
        self.dcnt[eng][i] += 16
        v = self.dcnt[eng][i]
        self.q[eng].append(lambda E, out=out, in_=in_, sem=sem: E.dma_start(out=out, in_=in_).then_inc(sem, 16))
        rec = (sem, v, key, "dma")
        self._record(rec, reads, writes)
        return rec

    def cc(self, fn, reads=(), writes=()):
        eng = "gpsimd"
        self._deps(eng, reads, writes)
        self.ccn += 1
        sem = self.ccsem
        self.q[eng].append(lambda E, fn=fn, sem=sem: fn(E).then_inc(sem, 1))
        rec = (sem, self.ccn, "cc", "cc")
        self._record(rec, reads, writes)
        return rec

    def final_wait(self, eng, rec):
        self._wait(eng, rec)

    def emit(self, block):
        for e in ENGS:
            def mk(fl):
                def _(E):
                    for f in fl:
                        f(E)
                return _
            getattr(block, e)(mk(self.q[e]))


def build_program(nsteps, NL=1, pair_groups=None):
    nc = bass.Bass("TRN2", target_bir_lowering=False)
    es = ExitStack()
    dram = lambda n, s, dt, kind: nc.dram_tensor(n, s, dt, kind=kind).ap()
    npos = nsteps * TT
    tiles = [(i * TT, TT) for i in range(nsteps)]
    xT = dram("xT", [D, npos], F32, "ExternalInput")
    flags_d = dram("flags", [128, 4], F32, "ExternalInput")
    bands_d = dram("bands", [128, 16 * 128], F32, "ExternalInput")
    sends = [nc.dram_tensor(f"sendb{i}", [D, TT], F32).ap() for i in range(2)]
    recvs = [nc.dram_tensor(f"recvb{i}", [2 * D, TT], F32).ap() for i in range(2)]
    sendB = [Buf(f"sendb{i}") for i in range(2)]
    recvB = [Buf(f"recvb{i}") for i in range(2)]
    LR = range(NL)
    w_in = [dram(f"w_in{l}", [D, INW], F32, "ExternalInput") for l in LR]
    w_a = [dram(f"w_a{l}", [D, D], F32, "ExternalInput") for l in LR]
    w_b = [dram(f"w_b{l}", [D, D], F32, "ExternalInput") for l in LR]
    w_o = [dram(f"w_o{l}", [D, D], F32, "ExternalInput") for l in LR]
    w_pool = [dram(f"w_pool{l}", [D, 256], F32, "ExternalInput") for l in LR]
    w_up = [dram(f"w_up{l}", [D, F2], F32, "ExternalInput") for l in LR]
    w_down = [dram(f"w_down{l}", [DFF, D], F32, "ExternalInput") for l in LR]
    w_gkb = [dram(f"w_gkb{l}", [17, 512], F32, "ExternalInput") for l in LR]
    vecs_d = [dram(f"vecs{l}", [128, NV], F32, "ExternalInput") for l in LR]
    consts_d = dram("consts", [128, NCONST], F32, "ExternalInput")
    outT = dram("outT", [D, npos], F32, "ExternalOutput")
    s_in = [dram(f"s_in{l}", [D, INW], BF16, "Internal") for l in LR]
    s_a = [dram(f"s_a{l}", [D, D], BF16, "Internal") for l in LR]
    s_b = [dram(f"s_b{l}", [D, D], BF16, "Internal") for l in LR]
    s_o = [dram(f"s_o{l}", [D, D], BF16, "Internal") for l in LR]
    s_pool = [dram(f"s_pool{l}", [D, 256], BF16, "Internal") for l in LR]
    s_up = [dram(f"s_up{l}", [D, F2], BF16, "Internal") for l in LR]
    s_down = [dram(f"s_down{l}", [DFF, D], BF16, "Internal") for l in LR]

    P = Prog(nc, es)

    def raw(name, shape, dt):
        return es.enter_context(nc.sbuf_tensor("sb_" + name, shape, dt))

    def sb(name, shape, dt, nparts=1):
        return TB(raw(name, shape, dt)[:], name, nparts)

    def view(ap, bufs):
        v = TB(ap, "v", 0)
        v.p = list(bufs)
        return v

    vecsL = [sb(f"vecs{l}", [128, NV], F32) for l in LR]
    vscL = [sb(f"vsc{l}", [128, 26], F32) for l in LR]
    consts = sb("consts", [128, NCONST], F32)
    flags = sb("flags", [128, 4], F32)
    ones = sb("ones", [128, 128], BF16)
    wglrL = [sb(f"wglr{l}", [128, 8, 16], BF16) for l in LR]
    wgkL = [sb(f"wgk{l}", [32, 512], BF16) for l in LR]
    slots = [sb(f"slot{i}", [128, 8, 1024], BF16) for i in range(NSLOT)]
    wready = Buf("wready")

    HS = [sb(f"H{i}", [128, 8, 16 + TT], F32, 8) for i in range(2)]
    hTv = [view(HS[i].t[:, :, 0:TT], HS[i].p) for i in range(2)]
    hnT = sb("hnT", [128, 8, TT], BF16, 8)
    NY = 3

    def xviews(X):
        Xf = X.t.rearrange("p c t -> p (c t)")
        return (view(X.t[:, 0:4, 0:TT], X.p[0:4]), view(X.t[:, 4:8, 0:TT], X.p[4:8]), view(X.t[:, :, 0:TT], X.p),
                [view(Xf[:, i * 528:i * 528 + 2 + TT], [X.p[i]]) for i in range(NY)],
                [view(Xf[:, (3 + i) * 528:(3 + i) * 528 + TT], [X.p[3 + i]]) for i in range(NY)])
    XV = [xviews(HS[i]) for i in range(2)]
    uprevL = [sb(f"uprev{l}", [128, 1024], BF16) for l in LR]
    bands = sb("bands", [128, 16, 128], BF16)
    BANDS = bands.t
    bigB = sb("bigB", [128, 24, TT], BF16, 24)
    sq = view(bigB.t[:, 0:8, :], bigB.p[0:8])
    silur = sq
    gated = view(bigB.t[:, 8:16, :], bigB.p[8:16])
    ybs = view(bigB.t[:, 16:24, :], bigB.p[16:24])
    utm = view(bigB.t[:, 16:24, :].rearrange("p c t -> p (c t)").rearrange("p (b n) -> p b n", b=4), bigB.p[16:24])
    act = view(bigB.t[:, 0:22, :], bigB.p[0:22])
    vm = sb("vm", [128, 4096], BF16, 4)
    vtm = view(vm.t.rearrange("p (b n) -> p b n", b=4), vm.p)
    mT = view(vm.t.rearrange("p (c t) -> p c t", c=8), [vm.p[j // 2] for j in range(8)])
    gaT = sb("gaT", [128, 8, TT], BF16, 8)
    kf = sb("kf", [128, 2048], F32, 4)
    ktm = view(kf.t.rearrange("p (b d) -> p b d", b=4), kf.p)
    kfv = view(kf.t.rearrange("p (c t) -> p c t", c=4), kf.p)
    gf = sb("gf", [128, 2048], F32, 4)
    gtm = view(gf.t.rearrange("p (b d) -> p b d", b=4), gf.p)
    gfv = view(gf.t.rearrange("p (c t) -> p c t", c=4), gf.p)
    stage = [kfv, gfv]
    glrT = sb("glrT", [32, TT], BF16)
    tmpz = sb("tmpz", [128, 512], F32)
    rstd = tmpz
    E1 = [sb("E1_0", [128, 4, 128], F32)] * 2
    E2 = [sb("E2_0", [128, 4, 128], F32)] * 2
    qdec = [sb("qdec0", [128, 4, 128], BF16)] * 2
    kinv = [sb("kinv0", [128, 4, 128], BF16)] * 2
    er = [sb("er0", [128, 512], F32)] * 2
    kend = [sb("kend0", [128, 512], BF16)] * 2
    attm = [sb("attm0", [128, 4, 128], BF16)] * 2
    sqo = [sb("sqo0", [128, 8, 128], BF16)] * 2
    rso = [sb("rso0", [128, 4, 128], F32)] * 2
    tmo = [sb("tmo0", [128, 8, 128], F32)] * 2
    SfL = [sb(f"Sf{l}", [128, 4, 256], F32) for l in LR]
    SbL = [sb(f"Sb{l}", [128, 4, 256], BF16) for l in LR]
    sila = [sb(f"sila{i}", [128, TT], BF16) for i in range(2)]
    chaloL = [sb(f"chalo{l}", [128, 44, 2], F32, 44) for l in LR]
    ps_t = es.enter_context(nc.psum_tensor("ps", [128, 8, 512], F32))
    banks = [Buf(f"bank{i}", excl=True) for i in range(8)]
    st = {"b1": 0, "b2": 0, "g": 0}

    st["gla"] = False

    def ps1():
        n = 4 if st["gla"] else 8
        i = st["b1"] % n
        st["b1"] = (i + 1) % n
        return ps_t[:, i, :], [banks[i]]

    def ps2():
        i = st["b2"]
        st["b2"] = (i + 1) % 2
        b = 4 + 2 * i
        return ps_t[:, b:b + 2, :].rearrange("p b n -> p (b n)"), [banks[b], banks[b + 1]]

    C = consts.t
    TRIB = C[:, 0:128]
    TRIR = C[:, 128:256]
    MASK4 = C[:, 256:768].rearrange("p (h c) -> p h c", h=4)
    INVCS = {0: C[:, 768:896].rearrange("p (k c) -> p k c", k=8), 2: C[:, 896:1024].rearrange("p (k c) -> p k c", k=8)}

    for l in LR:
        P.dma("sync", vecsL[l].t, vecs_d[l], writes=vecsL[l].all)
    P.dma("sync", consts.t, consts_d, writes=consts.all)
    P.dma("sync", flags.t, flags_d, writes=flags.all)
    P.dma("gpsimd", bands.t, bands_d.rearrange("p (k c) -> p k c", k=16), writes=bands.all)
    for l in LR:
        P.dma("gpsimd", wgkL[l].t[0:17, :], w_gkb[l], writes=wgkL[l].all)
        P.dma("gpsimd", wglrL[l].t, w_in[l][:, 2048:2064].rearrange("(kc p) n -> p kc n", p=128), writes=wglrL[l].all)
    P.op("vector", lambda E: E.memset(ones.t, 1.0), writes=ones.all)
    P.op("vector", lambda E: E.memset(glrT.t, 1.0), writes=glrT.all)
    P.op("vector", lambda E: E.memset(hTv[1].t, 0.0), writes=hTv[1].all)
    for i in range(2):
        P.dma("sync", recvs[i][0:D, :].rearrange("(c p) t -> p c t", p=128), hTv[1].t, reads=hTv[1].all, writes=[recvB[i]])
    for l in LR:
        P.op("vector", lambda E, l=l: E.memset(SfL[l].t, 0.0), writes=SfL[l].all)
        P.op("vector", lambda E, l=l: E.memset(SbL[l].t, 0.0), writes=SbL[l].all)
        P.op("vector", lambda E, l=l: E.memset(chaloL[l].t, 0.0), writes=chaloL[l].all)
        P.op("vector", lambda E, l=l: E.memset(uprevL[l].t, 0.0), writes=uprevL[l].all)
        P.op("vector", lambda E, l=l: E.tensor_scalar(out=vscL[l].t[:, 0:24], in0=vecsL[l].t[:, 0:24], scalar1=32.0, scalar2=None,
                                                      op0=ALU.mult), reads=vecsL[l].all, writes=vscL[l].all)
        P.op("vector", lambda E, l=l: E.tensor_scalar(out=vscL[l].t[:, 24:26], in0=vecsL[l].t[:, 48:50], scalar1=16.0, scalar2=None,
                                                      op0=ALU.mult), reads=vecsL[l].all, writes=vscL[l].all)

    RA = "(kc p) n -> p kc n"

    def grp_cols(f32, scr, c0, n=1024):
        return [(f32[:, c0:c0 + n].rearrange(RA, p=128), scr[:, c0:c0 + n].rearrange(RA, p=128), (slice(0, 8), slice(0, n)))]

    def grp_up(l, g):
        n = 512 if g < 5 else 256
        return [(w_up[l][:, o:o + n].rearrange(RA, p=128), s_up[l][:, o:o + n].rearrange(RA, p=128),
                 (slice(0, 8), slice(h * 512, h * 512 + n))) for h, o in ((0, g * 512), (1, DFF + g * 512))]

    def grp_dn(l, g):
        k0 = g * 8
        nk = 8 if g < 2 else 6
        return [(w_down[l][k0 * 128:(k0 + nk) * 128, :].rearrange(RA, p=128),
                 s_down[l][k0 * 128:(k0 + nk) * 128, :].rearrange(RA, p=128), (slice(0, nk), slice(0, 1024)))]

    tile_groups = []
    for l in LR:
        wi, si = w_in[l], s_in[l]
        tile_groups += [grp_cols(wi, si, 2064), grp_cols(wi, si, 0), grp_cols(wi, si, 1024),
                        grp_cols(wi, si, 3088), grp_cols(wi, si, 5136),
                        grp_cols(w_pool[l], s_pool[l], 0, 256), grp_cols(w_b[l], s_b[l], 0), grp_cols(wi, si, 4112),
                        grp_cols(w_a[l], s_a[l], 0),
                        grp_cols(w_o[l], s_o[l], 0)] + [grp_up(l, g) for g in range(6)] + [grp_dn(l, g) for g in range(3)]
    NGL = 19
    NG = len(tile_groups)
    total_groups = NG * len(tiles)
    ld = {"n": 0}
    scrB = {}

    def ensure(upto):
        upto = min(upto, total_groups - 1)
        while ld["n"] <= upto:
            n = ld["n"]
            sl = slots[n % NSLOT]
            for pi, (f32, scr, (ks, cs)) in enumerate(tile_groups[n % NG]):
                sb_ = scrB.setdefault((n % NG, pi), Buf("scr"))
                if n < NG:
                    P.dma("gpsimd", sl.t[:, ks, cs], f32, writes=sl.all)
                    P.dma("sync", scr, sl.t[:, ks, cs], reads=sl.all, writes=[sb_])
                else:
                    P.dma("sync", sl.t[:, ks, cs], scr, reads=[sb_], writes=sl.all)
            ld["n"] += 1

    def use(gidx, span=1):
        ensure(gidx + span - 1 + (NSLOT - span))
        return slots[gidx % NSLOT]

    def proj_fm(sl, col0, rhsT, nk, T, bank):
        return [lambda E, kc=kc: E.matmul(bank[:, 0:T], lhsT=sl.t[:, kc, col0:col0 + 128], rhs=rhsT.t[:, kc, 0:T],
                                          start=(kc == 0), stop=(kc == nk - 1)) for kc in range(nk)]

    def rmsnorm_to(dst, gcol, T, vsc, hT, outf32=None):
        for c in range(8):
            if c % 2 == 0:
                P.op("scalar", lambda E, c=c: E.activation(out=sq.t[:, c, 0:T], in_=hT.t[:, c, 0:T], func=AF.Square),
                     reads=[hT.p[c]], writes=[sq.p[c]])
            else:
                P.op("vector", lambda E, c=c: E.tensor_tensor(out=sq.t[:, c, 0:T], in0=hT.t[:, c, 0:T], in1=hT.t[:, c, 0:T],
                                                              op=ALU.mult), reads=[hT.p[c]], writes=[sq.p[c]])
        bank, bb = ps1()
        P.pe([lambda E, c=c: E.matmul(bank[:, 0:T], lhsT=ones.t[:, :], rhs=sq.t[:, c, 0:T], start=(c == 0),
                                      stop=(c == 7)) for c in range(8)], reads=sq.all + ones.all, writes=bb)
        P.op("scalar", lambda E: E.activation(out=rstd.t[:, 0:T], in_=bank[:, 0:T], func=AF.Ln, bias=float(D * EPS)),
             reads=bb, writes=rstd.all)
        P.op("scalar", lambda E: E.activation(out=rstd.t[:, 0:T], in_=rstd.t[:, 0:T], func=AF.Exp, scale=-0.5),
             reads=rstd.all, writes=rstd.all)
        for c in range(8):
            tgt, ci = (dst, c) if outf32 is None else (outf32[c // 4], c % 4)
            P.op("vector", lambda E, c=c, tgt=tgt, ci=ci: E.scalar_tensor_tensor(
                out=tgt.t[:, ci, 0:T], in0=hT.t[:, c, 0:T], scalar=vsc.t[:, gcol + c:gcol + c + 1],
                in1=rstd.t[:, 0:T], op0=ALU.mult, op1=ALU.mult),
                reads=[hT.p[c]] + rstd.all + vsc.all, writes=[tgt.p[ci]])

    out_recs = []

    def fm_proj(gidx, col0s, rhsT, T, evac):
        sl = use(gidx)
        for j0 in range(0, len(col0s), 4):
            grp = []
            fns = []
            wr = []
            fine = []
            for j in range(j0, min(j0 + 4, len(col0s))):
                bank, bb = ps1()
                fns += proj_fm(sl, col0s[j], rhsT, 8, T, bank)
                fine += [[rhsT.p[kc]] for kc in range(8)]
                wr += bb
                grp.append((j, bank, bb))
            P.pe(fns, reads=sl.all, writes=wr, fine=fine)
            for j, bank, bb in grp:
                evac(j, bank, bb)

    def assemble(step):
        nh = hTv[step % 2]
        c0 = step * TT
        P.dma("sync", nh.t, xT[:, c0:c0 + TT].rearrange("(c p) t -> p c t", p=128), writes=nh.all)
        rb = recvs[step % 2]
        for hf in range(2):
            P.dma("sync", stage[hf].t, rb[hf * 512:(hf + 1) * 512, :].rearrange("(c p) t -> p c t", p=128),
                  reads=[recvB[step % 2]], writes=stage[hf].all)
        for c in range(8):
            P.op("vector", lambda E, c=c: E.scalar_tensor_tensor(
                out=nh.t[:, c, :], in0=stage[c // 4].t[:, c % 4, :], scalar=flags.t[:, 0:1], in1=nh.t[:, c, :],
                op0=ALU.mult, op1=ALU.add), reads=[stage[c // 4].p[c % 4], nh.p[c]] + flags.all, writes=[nh.p[c]])

    def do_tile(ti, t0, T):
        hT = hTv[ti % 2]

        def pre_down():
            if ti + 1 < nsteps:
                assemble(ti + 1)
        for l in LR:
            do_layer(l, ti, t0, T, pre_down)
        P.dma("sync", sends[ti % 2].rearrange("(c p) t -> p c t", p=128), hT.t[:, :, 0:T], reads=hT.all,
              writes=[sendB[ti % 2]])
        if pair_groups is not None:
            P.cc(lambda E, i=ti % 2: E.collective_compute("AllGather", ALU.bypass, replica_groups=pair_groups,
                                                          ins=[sends[i].opt()], outs=[recvs[i].opt()]),
                 reads=[sendB[ti % 2]], writes=[recvB[ti % 2]])
        rmsnorm_to(None, 16, T, vscL[NL - 1], hT, outf32=stage)
        for hf in range(2):
            out_recs.append(P.dma("sync", outT[hf * 512:(hf + 1) * 512, t0:t0 + T].rearrange("(c p) t -> p c t", p=128),
                                  stage[hf].t, reads=stage[hf].all))

    def do_layer(l, ti, t0, T, pre_down):
        first = (ti == 0)
        fix = ti in INVCS
        nblk = T // 128
        gb0 = ti * NG + l * NGL
        EE = 16 + T
        vecs, vsc, wglr, wgk = vecsL[l], vscL[l], wglrL[l], wgkL[l]
        Sf, Sb, uprev, chalo = SfL[l], SbL[l], uprevL[l], chaloL[l]
        GP = "vector" if ti == 0 else "gpsimd"
        hT = hTv[ti % 2]
        qT, kT, mb, yext, cacc = XV[(ti + 1) % 2]
        V = vecs.t
        rmsnorm_to(hnT, 0, T, vsc, hT)
        if STOP < 2:
            return
        fm_proj(gb0 + 0, [j * 128 for j in range(8)], hnT, T,
                lambda j, bank, bb: P.op("scalar", lambda E: E.activation(out=silur.t[:, j, 0:T], in_=bank[:, 0:T], func=AF.Silu),
                                         reads=bb, writes=[silur.p[j]]))
        if STOP < 3:
            return
        def ev_qk(j, bank, bb):
            if j < 4:
                P.op("scalar", lambda E: E.activation(out=qT.t[:, j, 0:T], in_=bank[:, 0:T], func=AF.Identity,
                                                      scale=float(128 ** -0.5)), reads=bb, writes=[qT.p[j]])
            else:
                P.op("scalar", lambda E: E.activation(out=kT.t[:, j - 4, 0:T], in_=bank[:, 0:T], func=AF.Copy),
                     reads=bb, writes=[kT.p[j - 4]])
        fm_proj(gb0 + 1, [j * 128 for j in range(8)], hnT, T, ev_qk)
        sl = use(gb0 + 1)
        for blk in range(nblk):
            bank, bb = ps1()
            P.pe([lambda E, kc=kc, blk=blk, bank=bank, sl=sl: E.matmul(
                bank[:, :], lhsT=hnT.t[:, kc, blk * 128:(blk + 1) * 128], rhs=sl.t[:, kc, 512:1024],
                start=(kc == 0), stop=(kc == 7)) for kc in range(8)], reads=sl.all + hnT.all, writes=bb)
            P.op("scalar", lambda E, blk=blk, bank=bank: E.activation(out=ktm.t[:, blk, :], in_=bank[:, :], func=AF.Copy),
                 reads=bb, writes=[ktm.p[blk]])
        sl = use(gb0 + 2)
        for blk in range(nblk):
            pbk = [ps1(), ps1()]
            fns = []
            for hf in range(2):
                fns += [lambda E, kc=kc, blk=blk, hf=hf, bank=pbk[hf][0], sl=sl: E.matmul(
                    bank[:, :], lhsT=hnT.t[:, kc, blk * 128:(blk + 1) * 128], rhs=sl.t[:, kc, hf * 512:(hf + 1) * 512],
                    start=(kc == 0), stop=(kc == 7)) for kc in range(8)]
            P.pe(fns, reads=sl.all + hnT.all, writes=pbk[0][1] + pbk[1][1])
            for hf in range(2):
                bank, bb = pbk[hf]
                P.op("vector" if hf == 0 else "scalar",
                     (lambda E, blk=blk, hf=hf, bank=bank: E.tensor_copy(out=vtm.t[:, blk, hf * 512:(hf + 1) * 512], in_=bank[:, :]))
                     if hf == 0 else
                     (lambda E, blk=blk, hf=hf, bank=bank: E.activation(out=vtm.t[:, blk, hf * 512:(hf + 1) * 512], in_=bank[:, :], func=AF.Copy)),
                     reads=bb, writes=[vtm.p[blk]])
        bank, bb = ps1()
        P.pe([lambda E, kc=kc, bank=bank: E.matmul(bank[0:16, 0:T], lhsT=wglr.t[:, kc, :], rhs=hnT.t[:, kc, 0:T],
                                                    start=(kc == 0), stop=(kc == 7)) for kc in range(8)],
             reads=wglr.all + hnT.all, writes=bb)
        P.op("vector", lambda E, bank=bank: E.tensor_copy(out=glrT.t[0:16, 0:T], in_=bank[0:16, 0:T]),
             reads=bb, writes=glrT.all)
        for blk in range(nblk):
            bank, bb = ps1()
            P.pe([lambda E, blk=blk, bank=bank: E.matmul(bank[:, :], lhsT=glrT.t[0:17, blk * 128:(blk + 1) * 128],
                                                          rhs=wgk.t[0:17, :], start=True, stop=True)],
                 reads=glrT.all + wgk.all, writes=bb)
            P.op("scalar", lambda E, bank=bank: E.activation(out=tmpz.t[:, :], in_=bank[:, :], func=AF.Exp, scale=-1.0),
                 reads=bb, writes=tmpz.all)
            P.op("scalar", lambda E, blk=blk: E.activation(out=gtm.t[:, blk, :], in_=tmpz.t[:, :], func=AF.Ln, bias=1.0),
                 reads=tmpz.all, writes=[gtm.p[blk]])
        if STOP < 4:
            return
        fillers = []

        def mk_u(blk, hf):
            def f():
                sl = use(gb0 + 3)
                bank, bb = ps1()
                P.pe([lambda E, kc=kc: E.matmul(
                    bank[:, :], lhsT=hnT.t[:, kc, blk * 128:(blk + 1) * 128], rhs=sl.t[:, kc, hf * 512:(hf + 1) * 512],
                    start=(kc == 0), stop=(kc == 7)) for kc in range(8)], reads=sl.all + hnT.all, writes=bb)
                P.op("vector", lambda E: E.tensor_copy(out=utm.t[:, blk, hf * 512:(hf + 1) * 512], in_=bank[:, :]),
                     reads=bb, writes=utm.p[2 * blk:2 * blk + 2])
            return f

        def mk_gb(j):
            def f():
                sl = use(gb0 + 4)
                bank, bb = ps1()
                P.pe(proj_fm(sl, j * 128, hnT, 8, T, bank), reads=sl.all + hnT.all, writes=bb)
                P.op("vector", lambda E: E.tensor_copy(out=gaT.t[:, j, 0:T], in_=bank[:, 0:T]), reads=bb, writes=[gaT.p[j]])
            return f

        for blk in range(nblk):
            for hf in range(2):
                fillers.append(mk_u(blk, hf))
        for j in range(8):
            fillers.append(mk_gb(j))

        def run_fillers(n):
            for _ in range(min(n, len(fillers))):
                fillers.pop(0)()

        st["gla"] = True
        st["b1"] = 0
        for blk in range(nblk):
            pb = st["g"] % 2
            st["g"] += 1
            cs = slice(blk * 128, (blk + 1) * 128)
            bkA, bbA = ps1()
            P.pe([lambda E, h=h, blk=blk, bkA=bkA: E.matmul(bkA[:, h * 128:(h + 1) * 128],
                                                             lhsT=gtm.t[:, blk, h * 128:(h + 1) * 128], rhs=TRIB,
                                                             start=True, stop=True) for h in range(4)],
                 reads=[gtm.p[blk]] + consts.all, writes=bbA)
            bkB, bbB = ps1()
            P.pe([lambda E, blk=blk, bkB=bkB: E.matmul(bkB[:, :], lhsT=TRIR, rhs=gtm.t[:, blk, :], start=True, stop=True)],
                 reads=[gtm.p[blk]] + consts.all, writes=bbB)
            A3 = bkA.rearrange("p (h c) -> p h c", h=4)
            run_fillers(2)
            P.op("scalar", lambda E, pb=pb, A3=A3: E.activation(out=E1[pb].t, in_=A3, func=AF.Exp),
                 reads=bbA, writes=E1[pb].all)
            P.op("scalar", lambda E, pb=pb, A3=A3: E.activation(out=E2[pb].t, in_=A3, func=AF.Exp, scale=-1.0),
                 reads=bbA, writes=E2[pb].all)
            P.op("scalar", lambda E, pb=pb, bkB=bkB: E.activation(out=er[pb].t, in_=bkB[:, :], func=AF.Exp),
                 reads=bbB, writes=er[pb].all)
            P.op("vector", lambda E, pb=pb, cs=cs: E.tensor_tensor(out=qdec[pb].t, in0=qT.t[:, :, cs], in1=E1[pb].t,
                                                                   op=ALU.mult), reads=qT.all + E1[pb].all, writes=qdec[pb].all)
            P.op("vector", lambda E, pb=pb, cs=cs: E.tensor_tensor(out=kinv[pb].t, in0=kT.t[:, :, cs], in1=E2[pb].t,
                                                                   op=ALU.mult), reads=kT.all + E2[pb].all, writes=kinv[pb].all)
            P.op("vector", lambda E, pb=pb, blk=blk: E.tensor_tensor(out=kend[pb].t, in0=ktm.t[:, blk, :], in1=er[pb].t,
                                                                     op=ALU.mult), reads=[ktm.p[blk]] + er[pb].all, writes=kend[pb].all)
            bkC, bbC = ps1()
            P.pe([lambda E, h=h, pb=pb, bkC=bkC: E.matmul(bkC[:, h * 128:(h + 1) * 128], lhsT=kinv[pb].t[:, h, :],
                                                           rhs=qdec[pb].t[:, h, :], start=True, stop=True) for h in range(4)],
                 reads=kinv[pb].all + qdec[pb].all, writes=bbC)
            C3 = bkC.rearrange("p (h c) -> p h c", h=4)
            run_fillers(1)
            P.op("vector", lambda E, pb=pb, C3=C3: E.tensor_tensor(out=attm[pb].t, in0=C3, in1=MASK4, op=ALU.mult),
                 reads=bbC + consts.all, writes=attm[pb].all)
            bkD, bbD = ps2()
            fns = []
            for h in range(4):
                for ec in range(2):
                    o = (h * 2 + ec) * 128
                    fns.append(lambda E, h=h, ec=ec, o=o, blk=blk, pb=pb, bkD=bkD: E.matmul(
                        bkD[:, o:o + 128], lhsT=vtm.t[:, blk, h * 256 + ec * 128:h * 256 + ec * 128 + 128],
                        rhs=attm[pb].t[:, h, :], start=True, stop=False))
                    fns.append(lambda E, h=h, ec=ec, o=o, pb=pb, bkD=bkD: E.matmul(
                        bkD[:, o:o + 128], lhsT=Sb.t[:, h, ec * 128:(ec + 1) * 128], rhs=qdec[pb].t[:, h, :],
                        start=False, stop=True))
            P.pe(fns, reads=[vtm.p[blk]] + attm[pb].all + Sb.all + qdec[pb].all, writes=bbD)
            bkE, bbE = ps2()
            P.pe([lambda E, h=h, blk=blk, pb=pb, bkE=bkE: E.matmul(bkE[:, h * 256:(h + 1) * 256],
                                                                    lhsT=kend[pb].t[:, h * 128:(h + 1) * 128],
                                                                    rhs=vtm.t[:, blk, h * 256:(h + 1) * 256],
                                                                    start=True, stop=True) for h in range(4)],
                 reads=kend[pb].all + [vtm.p[blk]], writes=bbE)
            for h in range(4):
                P.op("vector", lambda E, h=h, pb=pb, bkE=bkE: E.scalar_tensor_tensor(
                    out=Sf.t[:, h, :], in0=Sf.t[:, h, :], scalar=E1[pb].t[:, h, 127:128], in1=bkE[:, h * 256:(h + 1) * 256],
                    op0=ALU.mult, op1=ALU.add), reads=Sf.all + E1[pb].all + bbE, writes=Sf.all)
            P.op("scalar", lambda E: E.activation(out=Sb.t, in_=Sf.t, func=AF.Copy), reads=Sf.all, writes=Sb.all)
            run_fillers(1)
            D3 = bkD.rearrange("p (k c) -> p k c", k=8)
            P.op("scalar", lambda E, pb=pb, D3=D3: E.activation(out=sqo[pb].t, in_=D3, func=AF.Square),
                 reads=bbD, writes=sqo[pb].all)
            bkF, bbF = ps1()
            fns = []
            for h in range(4):
                for ec in range(2):
                    fns.append(lambda E, h=h, ec=ec, pb=pb, bkF=bkF: E.matmul(
                        bkF[:, h * 128:(h + 1) * 128], lhsT=ones.t[:, :], rhs=sqo[pb].t[:, h * 2 + ec, :],
                        start=(ec == 0), stop=(ec == 1)))
            P.pe(fns, reads=sqo[pb].all + ones.all, writes=bbF)
            F3 = bkF.rearrange("p (h c) -> p h c", h=4)
            P.op("scalar", lambda E, pb=pb, F3=F3: E.activation(out=rso[pb].t, in_=F3, func=AF.Ln, bias=float(256 * EPS)),
                 reads=bbF, writes=rso[pb].all)
            P.op("scalar", lambda E, pb=pb: E.activation(out=rso[pb].t, in_=rso[pb].t, func=AF.Exp, scale=-0.5),
                 reads=rso[pb].all, writes=rso[pb].all)
            D4 = bkD.rearrange("p (h e c) -> p h e c", h=4, e=2)
            T4 = tmo[pb].t.rearrange("p (h e) c -> p h e c", h=4)
            for ec in range(2):
                P.op("vector", lambda E, ec=ec, pb=pb, D4=D4, T4=T4: E.scalar_tensor_tensor(
                    out=T4[:, :, ec, :], in0=D4[:, :, ec, :], scalar=vsc.t[:, 24 + ec:25 + ec], in1=rso[pb].t,
                    op0=ALU.mult, op1=ALU.mult), reads=bbD + rso[pb].all + vsc.all, writes=tmo[pb].all)
            P.op(GP, lambda E, pb=pb, cs=cs: E.tensor_tensor(out=gated.t[:, :, cs], in0=tmo[pb].t, in1=silur.t[:, :, cs],
                                                                   op=ALU.mult), reads=tmo[pb].all + silur.all, writes=gated.all)
        if STOP < 5:
            return
        st["gla"] = False
        run_fillers(len(fillers))
        for c in range(8):
            g = c // 2
            bank, bb = ps1()
            fns = []
            for blk in range(nblk):
                bd = BANDS[:, g, :]
                if ti in (0, 2) and blk == nblk - 1:
                    bd = BANDS[:, 8 + 4 * (ti // 2) + g, :]
                fns.append(lambda E, blk=blk, c=c, bank=bank, bd=bd: E.matmul(
                    bank[:, blk * 128:(blk + 1) * 128], lhsT=utm.t[:, blk, c * 128:(c + 1) * 128], rhs=bd, start=True, stop=False))
                prev = uprev.t[:, c * 128:(c + 1) * 128] if blk == 0 else utm.t[:, blk - 1, c * 128:(c + 1) * 128]
                fns.append(lambda E, blk=blk, bank=bank, g=g, prev=prev: E.matmul(
                    bank[:, blk * 128:(blk + 1) * 128], lhsT=prev, rhs=BANDS[:, 4 + g, :], start=False, stop=True))
            P.pe(fns, reads=utm.all + uprev.all + bands.all, writes=bb)
            P.op("scalar" if c % 2 == 0 else "vector",
                 (lambda E, c=c, bank=bank: E.activation(out=sq.t[:, c, 0:T], in_=bank[:, 0:T], func=AF.Copy)) if c % 2 == 0 else
                 (lambda E, c=c, bank=bank: E.tensor_copy(out=sq.t[:, c, 0:T], in_=bank[:, 0:T])),
                 reads=bb, writes=[sq.p[c]])
        P.op(GP, lambda E: E.tensor_copy(out=uprev.t, in_=utm.t[:, nblk - 1, :]), reads=utm.all, writes=uprev.all)
        wpool = use(gb0 + 5)
        for j in range(8):
            g = j // 2
            jj = j % 2
            bank, bb = ps1()
            P.pe([lambda E, kc=kc, g=g, jj=jj, bank=bank: E.matmul(
                bank[:, 0:T], lhsT=wpool.t[:, 2 * g + kc, jj * 128:(jj + 1) * 128], rhs=sq.t[:, 2 * g + kc, 0:T],
                start=(kc == 0), stop=(kc == 1)) for kc in range(2)],
                reads=wpool.all + [sq.p[2 * g], sq.p[2 * g + 1]], writes=bb)
            P.op("scalar", lambda E, j=j, bank=bank: E.activation(out=ybs.t[:, j, 0:T], in_=bank[:, 0:T], func=AF.Identity,
                                                                   scale=V[:, 24 + j:25 + j]), reads=bb + vecs.all, writes=[ybs.p[j]])
        if STOP < 6:
            return
        for j in range(8):
            P.op("scalar", lambda E, j=j: E.activation(out=gaT.t[:, j, 0:T], in_=gaT.t[:, j, 0:T], func=AF.Sigmoid,
                                                       bias=V[:, 40 + j:41 + j]), reads=[gaT.p[j]] + vecs.all, writes=[gaT.p[j]])
        fm_proj(gb0 + 6, [j * 128 for j in range(8)], ybs, T,
                lambda j, bank, bb: P.op("vector", lambda E: E.tensor_tensor(out=mb.t[:, j, 0:T], in0=bank[:, 0:T], in1=gaT.t[:, j, 0:T],
                                                                             op=ALU.mult), reads=bb + [gaT.p[j]], writes=[mb.p[j]]))
        fm_proj(gb0 + 7, [j * 128 for j in range(8)], hnT, T,
                lambda j, bank, bb: P.op("scalar", lambda E: E.activation(out=gaT.t[:, j, 0:T], in_=bank[:, 0:T], func=AF.Sigmoid,
                                                                          bias=V[:, 32 + j:33 + j]), reads=bb + vecs.all, writes=[gaT.p[j]]))
        def ev_ya(j, bank, bb):
            P.op("vector", lambda E: E.tensor_tensor(out=tmpz.t[:, 0:T], in0=bank[:, 0:T], in1=gaT.t[:, j, 0:T], op=ALU.mult),
                 reads=bb + [gaT.p[j]], writes=tmpz.all)
            P.op(GP if j % 2 == 0 else "vector",
                 lambda E: E.tensor_tensor(out=mT.t[:, j, 0:T], in0=tmpz.t[:, 0:T], in1=mb.t[:, j, 0:T], op=ALU.add),
                 reads=tmpz.all + [mb.p[j]], writes=[mT.p[j]])
        fm_proj(gb0 + 8, [j * 128 for j in range(8)], gated, T, ev_ya)
        fm_proj(gb0 + 9, [j * 128 for j in range(8)], mT, T,
                lambda j, bank, bb: P.op("vector", lambda E: E.tensor_tensor(out=hT.t[:, j, 0:T], in0=hT.t[:, j, 0:T], in1=bank[:, 0:T],
                                                                             op=ALU.add), reads=bb + [hT.p[j]], writes=[hT.p[j]]))
        if STOP < 7:
            return
        rmsnorm_to(hnT, 8, T, vsc, hT)
        yi = 0
        for g in range(6):
            sl = use(gb0 + 10 + g)
            npair = 4 if g < 5 else 2
            for jj in range(npair):
                pj = g * 4 + jj
                res = []
                pbk = [ps1(), ps1()]
                P.pe(proj_fm(sl, jj * 128, hnT, 8, T, pbk[0][0]) + proj_fm(sl, 512 + jj * 128, hnT, 8, T, pbk[1][0]),
                     reads=sl.all, writes=pbk[0][1] + pbk[1][1], fine=[[hnT.p[kc]] for kc in range(8)] * 2)
                for half in range(2):
                    ch = pj + 22 * half
                    bank, bb = pbk[half]
                    ye = yext[yi % NY]
                    ca = cacc[yi % NY]
                    yi += 1
                    P.op(GP, lambda E, ye=ye, ch=ch: E.tensor_copy(out=ye.t[:, 0:2], in_=chalo.t[:, ch, :]),
                         reads=[chalo.p[ch]], writes=ye.all)
                    P.op("scalar", lambda E, ye=ye, bank=bank: E.activation(out=ye.t[:, 2:2 + T], in_=bank[:, 0:T], func=AF.Copy),
                         reads=bb, writes=ye.all)
                    P.op("scalar", lambda E, ca=ca, bank=bank, ch=ch: E.activation(
                        out=ca.t[:, 0:T], in_=bank[:, 0:T], func=AF.Identity, scale=V[:, 138 + ch:139 + ch],
                        bias=V[:, 182 + ch:183 + ch]), reads=bb + vecs.all, writes=ca.all)
                    P.op(GP, lambda E, ye=ye, ch=ch, T=T: E.tensor_copy(out=chalo.t[:, ch, :], in_=ye.t[:, T:T + 2]),
                         reads=ye.all, writes=[chalo.p[ch]])
                    P.op("vector", lambda E, ye=ye, ca=ca, ch=ch: E.scalar_tensor_tensor(
                        out=ca.t[:, 0:T], in0=ye.t[:, 1:1 + T], scalar=V[:, 94 + ch:95 + ch], in1=ca.t[:, 0:T],
                        op0=ALU.mult, op1=ALU.add), reads=ye.all + ca.all + vecs.all, writes=ca.all)
                    P.op("vector", lambda E, ye=ye, ca=ca, ch=ch: E.scalar_tensor_tensor(
                        out=ca.t[:, 0:T], in0=ye.t[:, 0:T], scalar=V[:, 50 + ch:51 + ch], in1=ca.t[:, 0:T],
                        op0=ALU.mult, op1=ALU.add), reads=ye.all + ca.all + vecs.all, writes=ca.all)
                    res.append(ca)
                sa = sila[pj % 2]
                P.op("scalar", lambda E, sa=sa, ca=res[0]: E.activation(out=sa.t[:, 0:T], in_=ca.t[:, 0:T], func=AF.Silu),
                     reads=res[0].all, writes=sa.all)
                P.op("vector", lambda E, sa=sa, cb=res[1], pj=pj: E.tensor_tensor(out=act.t[:, pj, 0:T], in0=sa.t[:, 0:T],
                                                                                   in1=cb.t[:, 0:T], op=ALU.mult),
                     reads=sa.all + res[1].all, writes=[act.p[pj]])
        if STOP < 8:
            return
        pre_down()
        sls = [use(gb0 + 16, span=3), slots[(gb0 + 17) % NSLOT], slots[(gb0 + 18) % NSLOT]]
        for j in range(8):
            bank, bb = ps1()
            P.pe([lambda E, kc=kc, j=j, bank=bank, sls=sls: E.matmul(
                bank[:, 0:T], lhsT=sls[kc // 8].t[:, kc % 8, j * 128:(j + 1) * 128], rhs=act.t[:, kc, 0:T],
                start=(kc == 0), stop=(kc == 21)) for kc in range(22)],
                reads=sls[0].all + sls[1].all + sls[2].all, writes=bb, fine=[[act.p[kc]] for kc in range(22)])
            P.op("vector", lambda E, j=j, bank=bank: E.tensor_tensor(out=hT.t[:, j, 0:T], in0=hT.t[:, j, 0:T], in1=bank[:, 0:T],
                                                                     op=ALU.add), reads=bb + [hT.p[j]], writes=[hT.p[j]])
        if first:
            for c in range(8):
                P.op(GP, lambda E, c=c: E.memset(hT.t[:, c, 0:PAD], 0.0), writes=[hT.p[c]])

    assemble(0)
    for ti, (t0, T) in enumerate(tiles):
        do_tile(ti, t0, T)
    for r in out_recs:
        P.final_wait("sync", r)
    with nc.Block() as block:
        P.emit(block)
    es.close()
    return nc


def _consts(role):
    c = np.zeros((128, NCONST), np.float32)
    s = np.arange(128)[:, None]
    cc = np.arange(128)[None, :]
    c[:, 0:128] = np.where(s <= cc, -1.0 / 16.0, 0.0)
    c[:, 128:256] = np.where(s > cc, -1.0 / 16.0, 0.0)
    m = np.where(s <= cc, 1.0, 0.0)
    c[:, 256:768] = np.tile(m, (1, 4))
    real = np.zeros((8, 16), np.float32)
    plain = np.zeros((8, 16), np.float32)
    for ch in range(8):
        w = 2 ** (ch // 2 + 1)
        for j in range(16):
            real[ch, j] = 1.0 / min(j + 1, w)
            plain[ch, j] = 1.0 / w
    c[:, 768:896] = np.broadcast_to((real if role == 0 else plain).reshape(1, 128), (128, 128))
    c[:, 896:1024] = np.broadcast_to((plain if role == 0 else real).reshape(1, 128), (128, 128))
    return c


def _vecs(inp, l):
    v = np.zeros((128, NV), np.float32)
    fm = lambda a: np.ascontiguousarray(np.asarray(a, np.float32).reshape(-1, 128).T)
    v[:, 0:8] = fm(inp["norm1_g"][l])
    v[:, 8:16] = fm(inp["norm2_g"][l])
    v[:, 16:24] = fm(inp["final_norm_g"])
    v[:, 24:32] = fm(inp["pool_scale"][l])
    v[:, 32:40] = fm(inp["b_gates"][l][:D])
    v[:, 40:48] = fm(inp["b_gates"][l][D:])
    v[:, 48:50] = fm(inp["gla_norm_g"][l])
    v[:, 50:94] = fm(inp["conv_w"][l][0])
    v[:, 94:138] = fm(inp["conv_w"][l][1])
    v[:, 138:182] = fm(inp["conv_w"][l][2])
    v[:, 182:226] = fm(inp["conv_b"][l])
    return v


def _layer_map(inp, l):
    f = lambda a: np.ascontiguousarray(np.asarray(a, np.float32))
    return {
        f"w_in{l}": f(inp["w_in"][l]), f"w_a{l}": f(inp["w_a"][l]), f"w_b{l}": f(inp["w_b"][l]), f"w_o{l}": f(inp["w_o"][l]),
        f"w_pool{l}": f(np.asarray(inp["w_pool_grp"][l]).reshape(D, 256)), f"w_up{l}": f(inp["w_up"][l]),
        f"w_down{l}": f(inp["w_down"][l]),
        f"w_gkb{l}": f(np.concatenate([np.asarray(inp["w_gk"][l]), np.asarray(inp["b_gk"][l])[None, :]], axis=0)),
        f"vecs{l}": _vecs(inp, l),
    }


_NC_CACHE = {}
PAIRS = [[0, 1], [2, 3], [4, 5], [6, 7]]


def _get_nc(nsteps):
    if nsteps not in _NC_CACHE:
        _NC_CACHE[nsteps] = build_program(nsteps, NL=1, pair_groups=PAIRS)
    return _NC_CACHE[nsteps]


def make_xT(inp, b, ntok, nsteps):
    x = np.asarray(inp["x"], np.float32)
    meta = np.asarray(inp["meta_tokens"], np.float32)
    full = np.zeros((nsteps * TT, D), np.float32)
    full[PAD:PAD + NMETA] = meta
    full[PAD + NMETA:PAD + NMETA + ntok] = x[b, :ntok]
    return np.ascontiguousarray(full.T)


def _bands(role):
    out = np.zeros((16, 128, 128), np.float32)
    tp = np.arange(128)[:, None]
    t = np.arange(128)[None, :]
    for g in range(4):
        w = 2 ** (g + 1)
        d = t - tp
        out[g] = np.where((d >= 0) & (d < w), 1.0 / w, 0.0) - np.where(d == 0, 1.0, 0.0)
        ds = 128 + t - tp
        out[4 + g] = np.where((ds >= 1) & (ds < w), 1.0 / w, 0.0)
        cnt = np.where(t >= 112, np.minimum(t - 111, w), w).astype(np.float32)
        special = np.where((d >= 0) & (d < w), 1.0 / cnt, 0.0) - np.where(d == 0, 1.0, 0.0)
        out[8 + g] = special if role == 0 else out[g]
        out[12 + g] = out[g] if role == 0 else special
    return np.ascontiguousarray(out.transpose(1, 0, 2).reshape(128, 16 * 128))


def core_map(inp, b, role, ntok, nsteps):
    m = {k[:-1] + "0": v for k, v in _layer_map(inp, role).items()}
    m["consts"] = _consts(role)
    fl = np.zeros((128, 4), np.float32)
    fl[:, 0] = float(role)
    m["flags"] = fl
    m["bands"] = _bands(role)
    m["xT"] = make_xT(inp, b, ntok, nsteps) if role == 0 else np.zeros((D, nsteps * TT), np.float32)
    return m


def kernel(**inputs):
    nc = _get_nc(NSTEP)
    in_maps = [core_map(inputs, core // 2, core % 2, SEQ, NSTEP) for core in range(8)]
    res = run_bass_kernel_spmd(nc, in_maps, core_ids=list(range(8)))
    out = np.stack([np.ascontiguousarray(res.results[2 * b + 1]["outT"][:, 3 * TT:NSTEP * TT].T) for b in range(BATCH)],
                   axis=0)
    return out.astype(np.float32)
```

```python
from contextlib import ExitStack
import os
STOP = int(os.environ.get('MK_STOP', '99'))
import numpy as np
import concourse.bass as bass
import concourse.mybir as mybir
from concourse.bass_utils import run_bass_kernel_spmd

F32 = mybir.dt.float32
BF16 = mybir.dt.bfloat16
AF = mybir.ActivationFunctionType
ALU = mybir.AluOpType

D = 1024
NMETA = 16
SEQ = 8192
BATCH = 4
PAD = 496
NT = (PAD + NMETA + SEQ) // 512
NSTEP = NT + 2
NPOS = NSTEP * 512
INW = 6160
DFF = 2816
F2 = 2 * DFF
EPS = 1e-6
TT = 512
NV = 226
NCONST = 1024
NSLOT = 4
NDS = 8

ENGS = ("tensor", "vector", "scalar", "gpsimd", "sync")


class Buf:
    __slots__ = ("name", "w", "r", "excl")

    def __init__(self, name, excl=False):
        self.name = name
        self.w = None
        self.r = {}
        self.excl = excl


class TB:
    def __init__(self, t, name, nparts=1):
        self.t = t
        self.p = [Buf(f"{name}.{i}") for i in range(nparts)]

    @property
    def all(self):
        return self.p


class Prog:
    def __init__(self, nc, es):
        self.nc = nc
        self.q = {e: [] for e in ENGS}
        self.cnt = {e: 0 for e in ENGS}
        self.sem = {e: es.enter_context(nc.semaphore("s_" + e)) for e in ENGS}
        self.waited = {e: {} for e in ENGS}
        nds = {"sync": 12, "gpsimd": 12}
        self.dsem = {e: [es.enter_context(nc.semaphore(f"d_{e}{i}")) for i in range(nds[e])] for e in nds}
        self.dcnt = {e: [0] * nds[e] for e in self.dsem}
        self.dnext = {e: 0 for e in self.dsem}
        self.ccsem = es.enter_context(nc.semaphore("s_cc"))
        self.ccn = 0

    def _wait(self, eng, sv):
        sem, val, key, src = sv
        if self.waited[eng].get(key, 0) >= val:
            return
        self.waited[eng][key] = val
        self.q[eng].append(lambda E, sem=sem, val=val: E.wait_ge(sem, val))

    def _deps(self, eng, reads, writes):
        for b in reads:
            if b.w is not None:
                if not (eng == "tensor" and b.w[3] == "tensor"):
                    self._wait(eng, b.w)
            if b.excl:
                for r in b.r.values():
                    if r[3] != eng:
                        self._wait(eng, r)
        for b in writes:
            if b.w is not None:
                if not (eng == "tensor" and b.w[3] == "tensor"):
                    self._wait(eng, b.w)
            for r in b.r.values():
                if r[3] == eng and eng != "dma":
                    continue
                self._wait(eng, r)

    def _record(self, rec, reads, writes):
        for b in reads:
            b.r[rec[2]] = rec
        for b in writes:
            b.w = rec
            b.r = {}

    def op(self, eng, fn, reads=(), writes=()):
        self._deps(eng, reads, writes)
        self.cnt[eng] += 1
        c = self.cnt[eng]
        sem = self.sem[eng]
        self.q[eng].append(lambda E, fn=fn, sem=sem: fn(E).then_inc(sem, 1))
        self._record((sem, c, "e_" + eng, eng), reads, writes)

    def pe(self, fns, reads=(), writes=(), fine=None):
        eng = "tensor"
        self._deps(eng, reads, writes)
        self.cnt[eng] += 1
        c = self.cnt[eng]
        sem = self.sem[eng]
        allr = list(reads)
        for i, f in enumerate(fns):
            if fine is not None:
                self._deps(eng, fine[i], ())
                allr += list(fine[i])
            if i < len(fns) - 1:
                self.q[eng].append(lambda E, f=f: f(E))
            else:
                self.q[eng].append(lambda E, f=f, sem=sem: f(E).then_inc(sem, 1))
        self._record((sem, c, "e_tensor", eng), allr, writes)

    def dma(self, eng, out, in_, reads=(), writes=()):
        i = self.dnext[eng]
        self.dnext[eng] = (i + 1) % len(self.dsem[eng])
        sem = self.dsem[eng][i]
        key = f"d_{eng}{i}"
        if self.dcnt[eng][i] > 0:
            self._wait(eng, (sem, self.dcnt[eng][i], key, "dma"))
        self._deps(eng, reads, writes)
        self.dcnt[eng][i] += 16
        v = self.dcnt[eng][i]
        self.q[eng].append(lambda E, out=out, in_=in_, sem=sem: E.dma_start(out=out, in_=in_).then_inc(sem, 16))
        rec = (sem, v, key, "dma")
        self._record(rec, reads, writes)
        return rec

    def cc(self, fn, reads=(), writes=()):
        eng = "gpsimd"
        self._deps(eng, reads, writes)
        self.ccn += 1
        sem = self.ccsem
        self.q[eng].append(lambda E, fn=fn, sem=sem: fn(E).then_inc(sem, 1))
        rec = (sem, self.ccn, "cc", "cc")
        self._record(rec, reads, writes)
        return rec

    def final_wait(self, eng, rec):
        self._wait(eng, rec)

    def emit(self, block):
        for e in ENGS:
            def mk(fl):
                def _(E):
                    for f in fl:
                        f(E)
                return _
            getattr(block, e)(mk(self.q[e]))


def build_program(nsteps, NL=1, pair_groups=None):
    nc = bass.Bass("TRN2", target_bir_lowering=False)
    es = ExitStack()
    dram = lambda n, s, dt, kind: nc.dram_tensor(n, s, dt, kind=kind).ap()
    npos = nsteps * TT
    tiles = [(i * TT, TT) for i in range(nsteps)]
    xT = dram("xT", [D, npos], F32, "ExternalInput")
    flags_d = dram("flags", [128, 4], F32, "ExternalInput")
    bands_d = dram("bands", [128, 16 * 128], F32, "ExternalInput")
    sends = [nc.dram_tensor(f"sendb{i}", [D, TT], F32).ap() for i in range(2)]
    recvs = [nc.dram_tensor(f"recvb{i}", [2 * D, TT], F32).ap() for i in range(2)]
    sendB = [Buf(f"sendb{i}") for i in range(2)]
    recvB = [Buf(f"recvb{i}") for i in range(2)]
    LR = range(NL)
    w_in = [dram(f"w_in{l}", [D, INW], F32, "ExternalInput") for l in LR]
    w_a = [dram(f"w_a{l}", [D, D], F32, "ExternalInput") for l in LR]
    w_b = [dram(f"w_b{l}", [D, D], F32, "ExternalInput") for l in LR]
    w_o = [dram(f"w_o{l}", [D, D], F32, "ExternalInput") for l in LR]
    w_pool = [dram(f"w_pool{l}", [D, 256], F32, "ExternalInput") for l in LR]
    w_up = [dram(f"w_up{l}", [D, F2], F32, "ExternalInput") for l in LR]
    w_down = [dram(f"w_down{l}", [DFF, D], F32, "ExternalInput") for l in LR]
    w_gkb = [dram(f"w_gkb{l}", [17, 512], F32, "ExternalInput") for l in LR]
    vecs_d = [dram(f"vecs{l}", [128, NV], F32, "ExternalInput") for l in LR]
    consts_d = dram("consts", [128, NCONST], F32, "ExternalInput")
    outT = dram("outT", [D, npos], F32, "ExternalOutput")
    s_in = [dram(f"s_in{l}", [D, INW], BF16, "Internal") for l in LR]
    s_a = [dram(f"s_a{l}", [D, D], BF16, "Internal") for l in LR]
    s_b = [dram(f"s_b{l}", [D, D], BF16, "Internal") for l in LR]
    s_o = [dram(f"s_o{l}", [D, D], BF16, "Internal") for l in LR]
    s_pool = [dram(f"s_pool{l}", [D, 256], BF16, "Internal") for l in LR]
    s_up = [dram(f"s_up{l}", [D, F2], BF16, "Internal") for l in LR]
    s_down = [dram(f"s_down{l}", [DFF, D], BF16, "Internal") for l in LR]

    P = Prog(nc, es)

    def raw(name, shape, dt):
        return es.enter_context(nc.sbuf_tensor("sb_" + name, shape, dt))

    def sb(name, shape, dt, nparts=1):
        return TB(raw(name, shape, dt)[:], name, nparts)

    def view(ap, bufs):
        v = TB(ap, "v", 0)
        v.p = list(bufs)
        return v

    vecsL = [sb(f"vecs{l}", [128, NV], F32) for l in LR]
    vscL = [sb(f"vsc{l}", [128, 26], F32) for l in LR]
    consts = sb("consts", [128, NCONST], F32)
    flags = sb("flags", [128, 4], F32)
    ones = sb("ones", [128, 128], BF16)
    wglrL = [sb(f"wglr{l}", [128, 8, 16], BF16) for l in LR]
    wgkL = [sb(f"wgk{l}", [32, 512], BF16) for l in LR]
    slots = [sb(f"slot{i}", [128, 8, 1024], BF16) for i in range(NSLOT)]
    wready = Buf("wready")

    HS = [sb(f"H{i}", [128, 8, 16 + TT], F32, 8) for i in range(2)]
    hTv = [view(HS[i].t[:, :, 0:TT], HS[i].p) for i in range(2)]
    hnT = sb("hnT", [128, 8, TT], BF16, 8)
    NY = 3

    def xviews(X):
        Xf = X.t.rearrange("p c t -> p (c t)")
        return (view(X.t[:, 0:4, 0:TT], X.p[0:4]), view(X.t[:, 4:8, 0:TT], X.p[4:8]), view(X.t[:, :, 0:TT], X.p),
                [view(Xf[:, i * 528:i * 528 + 2 + TT], [X.p[i]]) for i in range(NY)],
                [view(Xf[:, (3 + i) * 528:(3 + i) * 528 + TT], [X.p[3 + i]]) for i in range(NY)])
    XV = [xviews(HS[i]) for i in range(2)]
    uprevL = [sb(f"uprev{l}", [128, 1024], BF16) for l in LR]
    bands = sb("bands", [128, 16, 128], BF16)
    BANDS = bands.t
    bigB = sb("bigB", [128, 24, TT], BF16, 24)
    sq = view(bigB.t[:, 0:8, :], bigB.p[0:8])
    silur = sq
    gated = view(bigB.t[:, 8:16, :], bigB.p[8:16])
    ybs = view(bigB.t[:, 16:24, :], bigB.p[16:24])
    utm = view(bigB.t[:, 16:24, :].rearrange("p c t -> p (c t)").rearrange("p (b n) -> p b n", b=4), bigB.p[16:24])
    act = view(bigB.t[:, 0:22, :], bigB.p[0:22])
    vm = sb("vm", [128, 4096], BF16, 4)
    vtm = view(vm.t.rearrange("p (b n) -> p b n", b=4), vm.p)
    mT = view(vm.t.rearrange("p (c t) -> p c t", c=8), [vm.p[j // 2] for j in range(8)])
    gaT = sb("gaT", [128, 8, TT], BF16, 8)
    kf = sb("kf", [128, 2048], F32, 4)
    ktm = view(kf.t.rearrange("p (b d) -> p b d", b=4), kf.p)
    kfv = view(kf.t.rearrange("p (c t) -> p c t", c=4), kf.p)
    gf = sb("gf", [128, 2048], F32, 4)
    gtm = view(gf.t.rearrange("p (b d) -> p b d", b=4), gf.p)
    gfv = view(gf.t.rearrange("p (c t) -> p c t", c=4), gf.p)
    stage = [kfv, gfv]
    glrT = sb("glrT", [32, TT], BF16)
    tmpz = sb("tmpz", [128, 512], F32)
    rstd = tmpz
    E1 = [sb("E1_0", [128, 4, 128], F32)] * 2
    E2 = [sb("E2_0", [128, 4, 128], F32)] * 2
    qdec = [sb("qdec0", [128, 4, 128], BF16)] * 2
    kinv = [sb("kinv0", [128, 4, 128], BF16)] * 2
    er = [sb("er0", [128, 512], F32)] * 2
    kend = [sb("kend0", [128, 512], BF16)] * 2
    attm = [sb("attm0", [128, 4, 128], BF16)] * 2
    sqo = [sb("sqo0", [128, 8, 128], BF16)] * 2
    rso = [sb("rso0", [128, 4, 128], F32)] * 2
    tmo = [sb("tmo0", [128, 8, 128], F32)] * 2
    SfL = [sb(f"Sf{l}", [128, 4, 256], F32) for l in LR]
    SbL = [sb(f"Sb{l}", [128, 4, 256], BF16) for l in LR]
    sila = [sb(f"sila{i}", [128, TT], BF16) for i in range(2)]
    chaloL = [sb(f"chalo{l}", [128, 44, 2], F32, 44) for l in LR]
    ps_t = es.enter_context(nc.psum_tensor("ps", [128, 8, 512], F32))
    banks = [Buf(f"bank{i}", excl=True) for i in range(8)]
    st = {"b1": 0, "b2": 0, "g": 0}

    st["gla"] = False

    def ps1():
        n = 4 if st["gla"] else 8
        i = st["b1"] % n
        st["b1"] = (i + 1) % n
        return ps_t[:, i, :], [banks[i]]

    def ps2():
        i = st["b2"]
        st["b2"] = (i + 1) % 2
        b = 4 + 2 * i
        return ps_t[:, b:b + 2, :].rearrange("p b n -> p (b n)"), [banks[b], banks[b + 1]]

    C = consts.t
    TRIB = C[:, 0:128]
    TRIR = C[:, 128:256]
    MASK4 = C[:, 256:768].rearrange("p (h c) -> p h c", h=4)
    INVCS = {0: C[:, 768:896].rearrange("p (k c) -> p k c", k=8), 2: C[:, 896:1024].rearrange("p (k c) -> p k c", k=8)}

    for l in LR:
        P.dma("sync", vecsL[l].t, vecs_d[l], writes=vecsL[l].all)
    P.dma("sync", consts.t, consts_d, writes=consts.all)
    P.dma("sync", flags.t, flags_d, writes=flags.all)
    P.dma("gpsimd", bands.t, bands_d.rearrange("p (k c) -> p k c", k=16), writes=bands.all)
    for l in LR:
        P.dma("gpsimd", wgkL[l].t[0:17, :], w_gkb[l], writes=wgkL[l].all)
        P.dma("gpsimd", wglrL[l].t, w_in[l][:, 2048:2064].rearrange("(kc p) n -> p kc n", p=128), writes=wglrL[l].all)
    P.op("vector", lambda E: E.memset(ones.t, 1.0), writes=ones.all)
    P.op("vector", lambda E: E.memset(glrT.t, 1.0), writes=glrT.all)
    P.op("vector", lambda E: E.memset(hTv[1].t, 0.0), writes=hTv[1].all)
    for i in range(2):
        P.dma("sync", recvs[i][0:D, :].rearrange("(c p) t -> p c t", p=128), hTv[1].t, reads=hTv[1].all, writes=[recvB[i]])
    for l in LR:
        P.op("vector", lambda E, l=l: E.memset(SfL[l].t, 0.0), writes=SfL[l].all)
        P.op("vector", lambda E, l=l: E.memset(SbL[l].t, 0.0), writes=SbL[l].all)
        P.op("vector", lambda E, l=l: E.memset(chaloL[l].t, 0.0), writes=chaloL[l].all)
        P.op("vector", lambda E, l=l: E.memset(uprevL[l].t, 0.0), writes=uprevL[l].all)
        P.op("vector", lambda E, l=l: E.tensor_scalar(out=vscL[l].t[:, 0:24], in0=vecsL[l].t[:, 0:24], scalar1=32.0, scalar2=None,
                                                      op0=ALU.mult), reads=vecsL[l].all, writes=vscL[l].all)
        P.op("vector", lambda E, l=l: E.tensor_scalar(out=vscL[l].t[:, 24:26], in0=vecsL[l].t[:, 48:50], scalar1=16.0, scalar2=None,
                                                      op0=ALU.mult), reads=vecsL[l].all, writes=vscL[l].all)

    RA = "(kc p) n -> p kc n"

    def grp_cols(f32, scr, c0, n=1024):
        return [(f32[:, c0:c0 + n].rearrange(RA, p=128), scr[:, c0:c0 + n].rearrange(RA, p=128), (slice(0, 8), slice(0, n)))]

    def grp_up(l, g):
        n = 512 if g < 5 else 256
        return [(w_up[l][:, o:o + n].rearrange(RA, p=128), s_up[l][:, o:o + n].rearrange(RA, p=128),
                 (slice(0, 8), slice(h * 512, h * 512 + n))) for h, o in ((0, g * 512), (1, DFF + g * 512))]

    def grp_dn(l, g):
        k0 = g * 8
        nk = 8 if g < 2 else 6
        return [(w_down[l][k0 * 128:(k0 + nk) * 128, :].rearrange(RA, p=128),
                 s_down[l][k0 * 128:(k0 + nk) * 128, :].rearrange(RA, p=128), (slice(0, nk), slice(0, 1024)))]

    tile_groups = []
    for l in LR:
        wi, si = w_in[l], s_in[l]
        tile_groups += [grp_cols(wi, si, 2064), grp_cols(wi, si, 0), grp_cols(wi, si, 1024),
                        grp_cols(wi, si, 3088), grp_cols(wi, si, 5136),
                        grp_cols(w_pool[l], s_pool[l], 0, 256), grp_cols(w_b[l], s_b[l], 0), grp_cols(wi, si, 4112),
                        grp_cols(w_a[l], s_a[l], 0),
                        grp_cols(w_o[l], s_o[l], 0)] + [grp_up(l, g) for g in range(6)] + [grp_dn(l, g) for g in range(3)]
    NGL = 19
    NG = len(tile_groups)
    total_groups = NG * len(tiles)
    ld = {"n": 0}
    scrB = {}

    def ensure(upto):
        upto = min(upto, total_groups - 1)
        while ld["n"] <= upto:
            n = ld["n"]
            sl = slots[n % NSLOT]
            for pi, (f32, scr, (ks, cs)) in enumerate(tile_groups[n % NG]):
                sb_ = scrB.setdefault((n % NG, pi), Buf("scr"))
                if n < NG:
                    P.dma("gpsimd", sl.t[:, ks, cs], f32, writes=sl.all)
                    P.dma("sync", scr, sl.t[:, ks, cs], reads=sl.all, writes=[sb_])
                else:
                    P.dma("sync", sl.t[:, ks, cs], scr, reads=[sb_], writes=sl.all)
            ld["n"] += 1

    def use(gidx, span=1):
        ensure(gidx + span - 1 + (NSLOT - span))
        return slots[gidx % NSLOT]

    def proj_fm(sl, col0, rhsT, nk, T, bank):
        return [lambda E, kc=kc: E.matmul(bank[:, 0:T], lhsT=sl.t[:, kc, col0:col0 + 128], rhs=rhsT.t[:, kc, 0:T],
                                          start=(kc == 0), stop=(kc == nk - 1)) for kc in range(nk)]

    def rmsnorm_to(dst, gcol, T, vsc, hT, outf32=None):
        for c in range(8):
            if c % 2 == 0:
                P.op("scalar", lambda E, c=c: E.activation(out=sq.t[:, c, 0:T], in_=hT.t[:, c, 0:T], func=AF.Square),
                     reads=[hT.p[c]], writes=[sq.p[c]])
            else:
                P.op("vector", lambda E, c=c: E.tensor_tensor(out=sq.t[:, c, 0:T], in0=hT.t[:, c, 0:T], in1=hT.t[:, c, 0:T],
                                                              op=ALU.mult), reads=[hT.p[c]], writes=[sq.p[c]])
        bank, bb = ps1()
        P.pe([lambda E, c=c: E.matmul(bank[:, 0:T], lhsT=ones.t[:, :], rhs=sq.t[:, c, 0:T], start=(c == 0),
                                      stop=(c == 7)) for c in range(8)], reads=sq.all + ones.all, writes=bb)
        P.op("scalar", lambda E: E.activation(out=rstd.t[:, 0:T], in_=bank[:, 0:T], func=AF.Ln, bias=float(D * EPS)),
             reads=bb, writes=rstd.all)
        P.op("scalar", lambda E: E.activation(out=rstd.t[:, 0:T], in_=rstd.t[:, 0:T], func=AF.Exp, scale=-0.5),
             reads=rstd.all, writes=rstd.all)
        for c in range(8):
            tgt, ci = (dst, c) if outf32 is None else (outf32[c // 4], c % 4)
            P.op("vector", lambda E, c=c, tgt=tgt, ci=ci: E.scalar_tensor_tensor(
                out=tgt.t[:, ci, 0:T], in0=hT.t[:, c, 0:T], scalar=vsc.t[:, gcol + c:gcol + c + 1],
                in1=rstd.t[:, 0:T], op0=ALU.mult, op1=ALU.mult),
                reads=[hT.p[c]] + rstd.all + vsc.all, writes=[tgt.p[ci]])

    out_recs = []

    def fm_proj(gidx, col0s, rhsT, T, evac):
        sl = use(gidx)
        for j0 in range(0, len(col0s), 2):
            grp = []
            fns = []
            wr = []
            fine = []
            for j in range(j0, min(j0 + 2, len(col0s))):
                bank, bb = ps1()
                fns += proj_fm(sl, col0s[j], rhsT, 8, T, bank)
                fine += [[rhsT.p[kc]] for kc in range(8)]
                wr += bb
                grp.append((j, bank, bb))
            P.pe(fns, reads=sl.all, writes=wr, fine=fine)
            for j, bank, bb in grp:
                evac(j, bank, bb)

    def assemble(step):
        nh = hTv[step % 2]
        c0 = step * TT
        P.dma("sync", nh.t, xT[:, c0:c0 + TT].rearrange("(c p) t -> p c t", p=128), writes=nh.all)
        if step == 0:
            return
        rb = recvs[step % 2]
        for hf in range(2):
            P.dma("sync", stage[hf].t, rb[hf * 512:(hf + 1) * 512, :].rearrange("(c p) t -> p c t", p=128),
                  reads=[recvB[step % 2]], writes=stage[hf].all)
        for c in range(8):
            P.op("vector", lambda E, c=c: E.scalar_tensor_tensor(
                out=nh.t[:, c, :], in0=stage[c // 4].t[:, c % 4, :], scalar=flags.t[:, 0:1], in1=nh.t[:, c, :],
                op0=ALU.mult, op1=ALU.add), reads=[stage[c // 4].p[c % 4], nh.p[c]] + flags.all, writes=[nh.p[c]])

    def do_tile(ti, t0, T):
        hT = hTv[ti % 2]

        def pre_down():
            if ti + 1 < nsteps:
                assemble(ti + 1)
        for l in LR:
            do_layer(l, ti, t0, T, pre_down)
        P.dma("sync", sends[ti % 2].rearrange("(c p) t -> p c t", p=128), hT.t[:, :, 0:T], reads=hT.all,
              writes=[sendB[ti % 2]])
        if pair_groups is not None:
            P.cc(lambda E, i=ti % 2: E.collective_compute("AllGather", ALU.bypass, replica_groups=pair_groups,
                                                          ins=[sends[i].opt()], outs=[recvs[i].opt()]),
                 reads=[sendB[ti % 2]], writes=[recvB[ti % 2]])
        rmsnorm_to(None, 16, T, vscL[NL - 1], hT, outf32=stage)
        for hf in range(2):
            out_recs.append(P.dma("sync", outT[hf * 512:(hf + 1) * 512, t0:t0 + T].rearrange("(c p) t -> p c t", p=128),
                                  stage[hf].t, reads=stage[hf].all))

    def do_layer(l, ti, t0, T, pre_down):
        first = (ti == 0)
        fix = ti in INVCS
        nblk = T // 128
        gb0 = ti * NG + l * NGL
        EE = 16 + T
        vecs, vsc, wglr, wgk = vecsL[l], vscL[l], wglrL[l], wgkL[l]
        Sf, Sb, uprev, chalo = SfL[l], SbL[l], uprevL[l], chaloL[l]
        GP = "vector" if ti == 0 else "gpsimd"
        hT = hTv[ti % 2]
        qT, kT, mb, yext, cacc = XV[(ti + 1) % 2]
        V = vecs.t
        rmsnorm_to(hnT, 0, T, vsc, hT)
        if STOP < 2:
            return
        fm_proj(gb0 + 0, [j * 128 for j in range(8)], hnT, T,
                lambda j, bank, bb: P.op("scalar", lambda E: E.activation(out=silur.t[:, j, 0:T], in_=bank[:, 0:T], func=AF.Silu),
                                         reads=bb, writes=[silur.p[j]]))
        if STOP < 3:
            return
        def ev_qk(j, bank, bb):
            if j < 4:
                P.op("scalar", lambda E: E.activation(out=qT.t[:, j, 0:T], in_=bank[:, 0:T], func=AF.Identity,
                                                      scale=float(128 ** -0.5)), reads=bb, writes=[qT.p[j]])
            else:
                P.op("scalar", lambda E: E.activation(out=kT.t[:, j - 4, 0:T], in_=bank[:, 0:T], func=AF.Copy),
                     reads=bb, writes=[kT.p[j - 4]])
        fm_proj(gb0 + 1, [j * 128 for j in range(8)], hnT, T, ev_qk)
        sl = use(gb0 + 1)
        for blk in range(nblk):
            bank, bb = ps1()
            P.pe([lambda E, kc=kc, blk=blk, bank=bank, sl=sl: E.matmul(
                bank[:, :], lhsT=hnT.t[:, kc, blk * 128:(blk + 1) * 128], rhs=sl.t[:, kc, 512:1024],
                start=(kc == 0), stop=(kc == 7)) for kc in range(8)], reads=sl.all + hnT.all, writes=bb)
            P.op("scalar", lambda E, blk=blk, bank=bank: E.activation(out=ktm.t[:, blk, :], in_=bank[:, :], func=AF.Copy),
                 reads=bb, writes=[ktm.p[blk]])
        sl = use(gb0 + 2)
        for blk in range(nblk):
            pbk = [ps1(), ps1()]
            fns = []
            for hf in range(2):
                fns += [lambda E, kc=kc, blk=blk, hf=hf, bank=pbk[hf][0], sl=sl: E.matmul(
                    bank[:, :], lhsT=hnT.t[:, kc, blk * 128:(blk + 1) * 128], rhs=sl.t[:, kc, hf * 512:(hf + 1) * 512],
                    start=(kc == 0), stop=(kc == 7)) for kc in range(8)]
            P.pe(fns, reads=sl.all + hnT.all, writes=pbk[0][1] + pbk[1][1])
            for hf in range(2):
                bank, bb = pbk[hf]
                P.op("vector" if hf == 0 else "scalar",
                     (lambda E, blk=blk, hf=hf, bank=bank: E.tensor_copy(out=vtm.t[:, blk, hf * 512:(hf + 1) * 512], in_=bank[:, :]))
                     if hf == 0 else
                     (lambda E, blk=blk, hf=hf, bank=bank: E.activation(out=vtm.t[:, blk, hf * 512:(hf + 1) * 512], in_=bank[:, :], func=AF.Copy)),
                     reads=bb, writes=[vtm.p[blk]])
        bank, bb = ps1()
        P.pe([lambda E, kc=kc, bank=bank: E.matmul(bank[0:16, 0:T], lhsT=wglr.t[:, kc, :], rhs=hnT.t[:, kc, 0:T],
                                                    start=(kc == 0), stop=(kc == 7)) for kc in range(8)],
             reads=wglr.all + hnT.all, writes=bb)
        P.op("vector", lambda E, bank=bank: E.tensor_copy(out=glrT.t[0:16, 0:T], in_=bank[0:16, 0:T]),
             reads=bb, writes=glrT.all)
        for blk in range(nblk):
            bank, bb = ps1()
            P.pe([lambda E, blk=blk, bank=bank: E.matmul(bank[:, :], lhsT=glrT.t[0:17, blk * 128:(blk + 1) * 128],
                                                          rhs=wgk.t[0:17, :], start=True, stop=True)],
                 reads=glrT.all + wgk.all, writes=bb)
            P.op("scalar", lambda E, bank=bank: E.activation(out=tmpz.t[:, :], in_=bank[:, :], func=AF.Exp, scale=-1.0),
                 reads=bb, writes=tmpz.all)
            P.op("scalar", lambda E, blk=blk: E.activation(out=gtm.t[:, blk, :], in_=tmpz.t[:, :], func=AF.Ln, bias=1.0),
                 reads=tmpz.all, writes=[gtm.p[blk]])
        if STOP < 4:
            return
        fillers = []

        def mk_u(blk, hf):
            def f():
                sl = use(gb0 + 3)
                bank, bb = ps1()
                P.pe([lambda E, kc=kc: E.matmul(
                    bank[:, :], lhsT=hnT.t[:, kc, blk * 128:(blk + 1) * 128], rhs=sl.t[:, kc, hf * 512:(hf + 1) * 512],
                    start=(kc == 0), stop=(kc == 7)) for kc in range(8)], reads=sl.all + hnT.all, writes=bb)
                P.op("vector", lambda E: E.tensor_copy(out=utm.t[:, blk, hf * 512:(hf + 1) * 512], in_=bank[:, :]),
                     reads=bb, writes=utm.p[2 * blk:2 * blk + 2])
            return f

        def mk_gb(j):
            def f():
                sl = use(gb0 + 4)
                bank, bb = ps1()
                P.pe(proj_fm(sl, j * 128, hnT, 8, T, bank), reads=sl.all + hnT.all, writes=bb)
                P.op("vector", lambda E: E.tensor_copy(out=gaT.t[:, j, 0:T], in_=bank[:, 0:T]), reads=bb, writes=[gaT.p[j]])
            return f

        for blk in range(nblk):
            for hf in range(2):
                fillers.append(mk_u(blk, hf))
        for j in range(8):
            fillers.append(mk_gb(j))

        def run_fillers(n):
            for _ in range(min(n, len(fillers))):
                fillers.pop(0)()

        st["gla"] = True
        st["b1"] = 0
        for blk in range(nblk):
            pb = st["g"] % 2
            st["g"] += 1
            cs = slice(blk * 128, (blk + 1) * 128)
            bkA, bbA = ps1()
            P.pe([lambda E, h=h, blk=blk, bkA=bkA: E.matmul(bkA[:, h * 128:(h + 1) * 128],
                                                             lhsT=gtm.t[:, blk, h * 128:(h + 1) * 128], rhs=TRIB,
                                                             start=True, stop=True) for h in range(4)],
                 reads=[gtm.p[blk]] + consts.all, writes=bbA)
            bkB, bbB = ps1()
            P.pe([lambda E, blk=blk, bkB=bkB: E.matmul(bkB[:, :], lhsT=TRIR, rhs=gtm.t[:, blk, :], start=True, stop=True)],
                 reads=[gtm.p[blk]] + consts.all, writes=bbB)
            A3 = bkA.rearrange("p (h c) -> p h c", h=4)
            run_fillers(2)
            P.op("scalar", lambda E, pb=pb, A3=A3: E.activation(out=E1[pb].t, in_=A3, func=AF.Exp),
                 reads=bbA, writes=E1[pb].all)
            P.op("scalar", lambda E, pb=pb, A3=A3: E.activation(out=E2[pb].t, in_=A3, func=AF.Exp, scale=-1.0),
                 reads=bbA, writes=E2[pb].all)
            P.op("scalar", lambda E, pb=pb, bkB=bkB: E.activation(out=er[pb].t, in_=bkB[:, :], func=AF.Exp),
                 reads=bbB, writes=er[pb].all)
            P.op("vector", lambda E, pb=pb, cs=cs: E.tensor_tensor(out=qdec[pb].t, in0=qT.t[:, :, cs], in1=E1[pb].t,
                                                                   op=ALU.mult), reads=qT.all + E1[pb].all, writes=qdec[pb].all)
            P.op("vector", lambda E, pb=pb, cs=cs: E.tensor_tensor(out=kinv[pb].t, in0=kT.t[:, :, cs], in1=E2[pb].t,
                                                                   op=ALU.mult), reads=kT.all + E2[pb].all, writes=kinv[pb].all)
            P.op("vector", lambda E, pb=pb, blk=blk: E.tensor_tensor(out=kend[pb].t, in0=ktm.t[:, blk, :], in1=er[pb].t,
                                                                     op=ALU.mult), reads=[ktm.p[blk]] + er[pb].all, writes=kend[pb].all)
            bkC, bbC = ps1()
            P.pe([lambda E, h=h, pb=pb, bkC=bkC: E.matmul(bkC[:, h * 128:(h + 1) * 128], lhsT=kinv[pb].t[:, h, :],
                                                           rhs=qdec[pb].t[:, h, :], start=True, stop=True) for h in range(4)],
                 reads=kinv[pb].all + qdec[pb].all, writes=bbC)
            C3 = bkC.rearrange("p (h c) -> p h c", h=4)
            run_fillers(1)
            P.op("vector", lambda E, pb=pb, C3=C3: E.tensor_tensor(out=attm[pb].t, in0=C3, in1=MASK4, op=ALU.mult),
                 reads=bbC + consts.all, writes=attm[pb].all)
            bkD, bbD = ps2()
            fns = []
            for h in range(4):
                for ec in range(2):
                    o = (h * 2 + ec) * 128
                    fns.append(lambda E, h=h, ec=ec, o=o, blk=blk, pb=pb, bkD=bkD: E.matmul(
                        bkD[:, o:o + 128], lhsT=vtm.t[:, blk, h * 256 + ec * 128:h * 256 + ec * 128 + 128],
                        rhs=attm[pb].t[:, h, :], start=True, stop=False))
                    fns.append(lambda E, h=h, ec=ec, o=o, pb=pb, bkD=bkD: E.matmul(
                        bkD[:, o:o + 128], lhsT=Sb.t[:, h, ec * 128:(ec + 1) * 128], rhs=qdec[pb].t[:, h, :],
                        start=False, stop=True))
            P.pe(fns, reads=[vtm.p[blk]] + attm[pb].all + Sb.all + qdec[pb].all, writes=bbD)
            bkE, bbE = ps2()
            P.pe([lambda E, h=h, blk=blk, pb=pb, bkE=bkE: E.matmul(bkE[:, h * 256:(h + 1) * 256],
                                                                    lhsT=kend[pb].t[:, h * 128:(h + 1) * 128],
                                                                    rhs=vtm.t[:, blk, h * 256:(h + 1) * 256],
                                                                    start=True, stop=True) for h in range(4)],
                 reads=kend[pb].all + [vtm.p[blk]], writes=bbE)
            for h in range(4):
                P.op("vector", lambda E, h=h, pb=pb, bkE=bkE: E.scalar_tensor_tensor(
                    out=Sf.t[:, h, :], in0=Sf.t[:, h, :], scalar=E1[pb].t[:, h, 127:128], in1=bkE[:, h * 256:(h + 1) * 256],
                    op0=ALU.mult, op1=ALU.add), reads=Sf.all + E1[pb].all + bbE, writes=Sf.all)
            P.op("vector", lambda E: E.tensor_copy(out=Sb.t, in_=Sf.t), reads=Sf.all, writes=Sb.all)
            run_fillers(1)
            D3 = bkD.rearrange("p (k c) -> p k c", k=8)
            P.op("scalar", lambda E, pb=pb, D3=D3: E.activation(out=sqo[pb].t, in_=D3, func=AF.Square),
                 reads=bbD, writes=sqo[pb].all)
            bkF, bbF = ps1()
            fns = []
            for h in range(4):
                for ec in range(2):
                    fns.append(lambda E, h=h, ec=ec, pb=pb, bkF=bkF: E.matmul(
                        bkF[:, h * 128:(h + 1) * 128], lhsT=ones.t[:, :], rhs=sqo[pb].t[:, h * 2 + ec, :],
                        start=(ec == 0), stop=(ec == 1)))
            P.pe(fns, reads=sqo[pb].all + ones.all, writes=bbF)
            F3 = bkF.rearrange("p (h c) -> p h c", h=4)
            P.op("scalar", lambda E, pb=pb, F3=F3: E.activation(out=rso[pb].t, in_=F3, func=AF.Ln, bias=float(256 * EPS)),
                 reads=bbF, writes=rso[pb].all)
            P.op("scalar", lambda E, pb=pb: E.activation(out=rso[pb].t, in_=rso[pb].t, func=AF.Exp, scale=-0.5),
                 reads=rso[pb].all, writes=rso[pb].all)
            D4 = bkD.rearrange("p (h e c) -> p h e c", h=4, e=2)
            T4 = tmo[pb].t.rearrange("p (h e) c -> p h e c", h=4)
            for ec in range(2):
                P.op("vector", lambda E, ec=ec, pb=pb, D4=D4, T4=T4: E.scalar_tensor_tensor(
                    out=T4[:, :, ec, :], in0=D4[:, :, ec, :], scalar=vsc.t[:, 24 + ec:25 + ec], in1=rso[pb].t,
                    op0=ALU.mult, op1=ALU.mult), reads=bbD + rso[pb].all + vsc.all, writes=tmo[pb].all)
            P.op(GP, lambda E, pb=pb, cs=cs: E.tensor_tensor(out=gated.t[:, :, cs], in0=tmo[pb].t, in1=silur.t[:, :, cs],
                                                                   op=ALU.mult), reads=tmo[pb].all + silur.all, writes=gated.all)
        if STOP < 5:
            return
        st["gla"] = False
        run_fillers(len(fillers))
        for c in range(8):
            g = c // 2
            bank, bb = ps1()
            fns = []
            for blk in range(nblk):
                bd = BANDS[:, g, :]
                if ti in (0, 2) and blk == nblk - 1:
                    bd = BANDS[:, 8 + 4 * (ti // 2) + g, :]
                fns.append(lambda E, blk=blk, c=c, bank=bank, bd=bd: E.matmul(
                    bank[:, blk * 128:(blk + 1) * 128], lhsT=utm.t[:, blk, c * 128:(c + 1) * 128], rhs=bd, start=True, stop=False))
                prev = uprev.t[:, c * 128:(c + 1) * 128] if blk == 0 else utm.t[:, blk - 1, c * 128:(c + 1) * 128]
                fns.append(lambda E, blk=blk, bank=bank, g=g, prev=prev: E.matmul(
                    bank[:, blk * 128:(blk + 1) * 128], lhsT=prev, rhs=BANDS[:, 4 + g, :], start=False, stop=True))
            P.pe(fns, reads=utm.all + uprev.all + bands.all, writes=bb)
            P.op("scalar" if c % 2 == 0 else "vector",
                 (lambda E, c=c, bank=bank: E.activation(out=sq.t[:, c, 0:T], in_=bank[:, 0:T], func=AF.Copy)) if c % 2 == 0 else
                 (lambda E, c=c, bank=bank: E.tensor_copy(out=sq.t[:, c, 0:T], in_=bank[:, 0:T])),
                 reads=bb, writes=[sq.p[c]])
        P.op(GP, lambda E: E.tensor_copy(out=uprev.t, in_=utm.t[:, nblk - 1, :]), reads=utm.all, writes=uprev.all)
        wpool = use(gb0 + 5)
        for j in range(8):
            g = j // 2
            jj = j % 2
            bank, bb = ps1()
            P.pe([lambda E, kc=kc, g=g, jj=jj, bank=bank: E.matmul(
                bank[:, 0:T], lhsT=wpool.t[:, 2 * g + kc, jj * 128:(jj + 1) * 128], rhs=sq.t[:, 2 * g + kc, 0:T],
                start=(kc == 0), stop=(kc == 1)) for kc in range(2)],
                reads=wpool.all + [sq.p[2 * g], sq.p[2 * g + 1]], writes=bb)
            P.op("scalar", lambda E, j=j, bank=bank: E.activation(out=ybs.t[:, j, 0:T], in_=bank[:, 0:T], func=AF.Identity,
                                                                   scale=V[:, 24 + j:25 + j]), reads=bb + vecs.all, writes=[ybs.p[j]])
        if STOP < 6:
            return
        for j in range(8):
            P.op("scalar", lambda E, j=j: E.activation(out=gaT.t[:, j, 0:T], in_=gaT.t[:, j, 0:T], func=AF.Sigmoid,
                                                       bias=V[:, 40 + j:41 + j]), reads=[gaT.p[j]] + vecs.all, writes=[gaT.p[j]])
        fm_proj(gb0 + 6, [j * 128 for j in range(8)], ybs, T,
                lambda j, bank, bb: P.op("vector", lambda E: E.tensor_tensor(out=mb.t[:, j, 0:T], in0=bank[:, 0:T], in1=gaT.t[:, j, 0:T],
                                                                             op=ALU.mult), reads=bb + [gaT.p[j]], writes=[mb.p[j]]))
        fm_proj(gb0 + 7, [j * 128 for j in range(8)], hnT, T,
                lambda j, bank, bb: P.op("scalar", lambda E: E.activation(out=gaT.t[:, j, 0:T], in_=bank[:, 0:T], func=AF.Sigmoid,
                                                                          bias=V[:, 32 + j:33 + j]), reads=bb + vecs.all, writes=[gaT.p[j]]))
        def ev_ya(j, bank, bb):
            P.op("vector", lambda E: E.tensor_tensor(out=tmpz.t[:, 0:T], in0=bank[:, 0:T], in1=gaT.t[:, j, 0:T], op=ALU.mult),
                 reads=bb + [gaT.p[j]], writes=tmpz.all)
            P.op(GP if j % 2 == 0 else "vector",
                 lambda E: E.tensor_tensor(out=mT.t[:, j, 0:T], in0=tmpz.t[:, 0:T], in1=mb.t[:, j, 0:T], op=ALU.add),
                 reads=tmpz.all + [mb.p[j]], writes=[mT.p[j]])
        fm_proj(gb0 + 8, [j * 128 for j in range(8)], gated, T, ev_ya)
        fm_proj(gb0 + 9, [j * 128 for j in range(8)], mT, T,
                lambda j, bank, bb: P.op("vector", lambda E: E.tensor_tensor(out=hT.t[:, j, 0:T], in0=hT.t[:, j, 0:T], in1=bank[:, 0:T],
                                                                             op=ALU.add), reads=bb + [hT.p[j]], writes=[hT.p[j]]))
        if STOP < 7:
            return
        rmsnorm_to(hnT, 8, T, vsc, hT)
        yi = 0
        for g in range(6):
            sl = use(gb0 + 10 + g)
            npair = 4 if g < 5 else 2
            for jj in range(npair):
                pj = g * 4 + jj
                res = []
                pbk = [ps1(), ps1()]
                P.pe(proj_fm(sl, jj * 128, hnT, 8, T, pbk[0][0]) + proj_fm(sl, 512 + jj * 128, hnT, 8, T, pbk[1][0]),
                     reads=sl.all, writes=pbk[0][1] + pbk[1][1], fine=[[hnT.p[kc]] for kc in range(8)] * 2)
                for half in range(2):
                    ch = pj + 22 * half
                    bank, bb = pbk[half]
                    ye = yext[yi % NY]
                    ca = cacc[yi % NY]
                    yi += 1
                    P.op(GP, lambda E, ye=ye, ch=ch: E.tensor_copy(out=ye.t[:, 0:2], in_=chalo.t[:, ch, :]),
                         reads=[chalo.p[ch]], writes=ye.all)
                    P.op("scalar", lambda E, ye=ye, bank=bank: E.activation(out=ye.t[:, 2:2 + T], in_=bank[:, 0:T], func=AF.Copy),
                         reads=bb, writes=ye.all)
                    P.op("scalar", lambda E, ca=ca, bank=bank, ch=ch: E.activation(
                        out=ca.t[:, 0:T], in_=bank[:, 0:T], func=AF.Identity, scale=V[:, 138 + ch:139 + ch],
                        bias=V[:, 182 + ch:183 + ch]), reads=bb + vecs.all, writes=ca.all)
                    P.op(GP, lambda E, ye=ye, ch=ch, T=T: E.tensor_copy(out=chalo.t[:, ch, :], in_=ye.t[:, T:T + 2]),
                         reads=ye.all, writes=[chalo.p[ch]])
                    P.op("vector", lambda E, ye=ye, ca=ca, ch=ch: E.scalar_tensor_tensor(
                        out=ca.t[:, 0:T], in0=ye.t[:, 1:1 + T], scalar=V[:, 94 + ch:95 + ch], in1=ca.t[:, 0:T],
                        op0=ALU.mult, op1=ALU.add), reads=ye.all + ca.all + vecs.all, writes=ca.all)
                    P.op("vector", lambda E, ye=ye, ca=ca, ch=ch: E.scalar_tensor_tensor(
                        out=ca.t[:, 0:T], in0=ye.t[:, 0:T], scalar=V[:, 50 + ch:51 + ch], in1=ca.t[:, 0:T],
                        op0=ALU.mult, op1=ALU.add), reads=ye.all + ca.all + vecs.all, writes=ca.all)
                    res.append(ca)
                sa = sila[pj % 2]
                P.op("scalar", lambda E, sa=sa, ca=res[0]: E.activation(out=sa.t[:, 0:T], in_=ca.t[:, 0:T], func=AF.Silu),
                     reads=res[0].all, writes=sa.all)
                P.op("vector", lambda E, sa=sa, cb=res[1], pj=pj: E.tensor_tensor(out=act.t[:, pj, 0:T], in0=sa.t[:, 0:T],
                                                                                   in1=cb.t[:, 0:T], op=ALU.mult),
                     reads=sa.all + res[1].all, writes=[act.p[pj]])
        if STOP < 8:
            return
        pre_down()
        sls = [use(gb0 + 16, span=3), slots[(gb0 + 17) % NSLOT], slots[(gb0 + 18) % NSLOT]]
        for j in range(8):
            bank, bb = ps1()
            P.pe([lambda E, kc=kc, j=j, bank=bank, sls=sls: E.matmul(
                bank[:, 0:T], lhsT=sls[kc // 8].t[:, kc % 8, j * 128:(j + 1) * 128], rhs=act.t[:, kc, 0:T],
                start=(kc == 0), stop=(kc == 21)) for kc in range(22)],
                reads=sls[0].all + sls[1].all + sls[2].all, writes=bb, fine=[[act.p[kc]] for kc in range(22)])
            P.op("vector", lambda E, j=j, bank=bank: E.tensor_tensor(out=hT.t[:, j, 0:T], in0=hT.t[:, j, 0:T], in1=bank[:, 0:T],
                                                                     op=ALU.add), reads=bb + [hT.p[j]], writes=[hT.p[j]])
        if first:
            for c in range(8):
                P.op(GP, lambda E, c=c: E.memset(hT.t[:, c, 0:PAD], 0.0), writes=[hT.p[c]])

    assemble(0)
    for ti, (t0, T) in enumerate(tiles):
        do_tile(ti, t0, T)
    for r in out_recs:
        P.final_wait("sync", r)
    with nc.Block() as block:
        P.emit(block)
    es.close()
    return nc


def _consts(role):
    c = np.zeros((128, NCONST), np.float32)
    s = np.arange(128)[:, None]
    cc = np.arange(128)[None, :]
    c[:, 0:128] = np.where(s <= cc, -1.0 / 16.0, 0.0)
    c[:, 128:256] = np.where(s > cc, -1.0 / 16.0, 0.0)
    m = np.where(s <= cc, 1.0, 0.0)
    c[:, 256:768] = np.tile(m, (1, 4))
    real = np.zeros((8, 16), np.float32)
    plain = np.zeros((8, 16), np.float32)
    for ch in range(8):
        w = 2 ** (ch // 2 + 1)
        for j in range(16):
            real[ch, j] = 1.0 / min(j + 1, w)
            plain[ch, j] = 1.0 / w
    c[:, 768:896] = np.broadcast_to((real if role == 0 else plain).reshape(1, 128), (128, 128))
    c[:, 896:1024] = np.broadcast_to((plain if role == 0 else real).reshape(1, 128), (128, 128))
    return c


def _vecs(inp, l):
    v = np.zeros((128, NV), np.float32)
    fm = lambda a: np.ascontiguousarray(np.asarray(a, np.float32).reshape(-1, 128).T)
    v[:, 0:8] = fm(inp["norm1_g"][l])
    v[:, 8:16] = fm(inp["norm2_g"][l])
    v[:, 16:24] = fm(inp["final_norm_g"])
    v[:, 24:32] = fm(inp["pool_scale"][l])
    v[:, 32:40] = fm(inp["b_gates"][l][:D])
    v[:, 40:48] = fm(inp["b_gates"][l][D:])
    v[:, 48:50] = fm(inp["gla_norm_g"][l])
    v[:, 50:94] = fm(inp["conv_w"][l][0])
    v[:, 94:138] = fm(inp["conv_w"][l][1])
    v[:, 138:182] = fm(inp["conv_w"][l][2])
    v[:, 182:226] = fm(inp["conv_b"][l])
    return v


def _layer_map(inp, l):
    f = lambda a: np.ascontiguousarray(np.asarray(a, np.float32))
    return {
        f"w_in{l}": f(inp["w_in"][l]), f"w_a{l}": f(inp["w_a"][l]), f"w_b{l}": f(inp["w_b"][l]), f"w_o{l}": f(inp["w_o"][l]),
        f"w_pool{l}": f(np.asarray(inp["w_pool_grp"][l]).reshape(D, 256)), f"w_up{l}": f(inp["w_up"][l]),
        f"w_down{l}": f(inp["w_down"][l]),
        f"w_gkb{l}": f(np.concatenate([np.asarray(inp["w_gk"][l]), np.asarray(inp["b_gk"][l])[None, :]], axis=0)),
        f"vecs{l}": _vecs(inp, l),
    }


_NC_CACHE = {}
PAIRS = [[0, 1], [2, 3], [4, 5], [6, 7]]


def _get_nc(nsteps):
    if nsteps not in _NC_CACHE:
        _NC_CACHE[nsteps] = build_program(nsteps, NL=1, pair_groups=PAIRS)
    return _NC_CACHE[nsteps]


def make_xT(inp, b, ntok, nsteps):
    x = np.asarray(inp["x"], np.float32)
    meta = np.asarray(inp["meta_tokens"], np.float32)
    full = np.zeros((nsteps * TT, D), np.float32)
    full[PAD:PAD + NMETA] = meta
    full[PAD + NMETA:PAD + NMETA + ntok] = x[b, :ntok]
    return np.ascontiguousarray(full.T)


def _bands(role):
    out = np.zeros((16, 128, 128), np.float32)
    tp = np.arange(128)[:, None]
    t = np.arange(128)[None, :]
    for g in range(4):
        w = 2 ** (g + 1)
        d = t - tp
        out[g] = np.where((d >= 0) & (d < w), 1.0 / w, 0.0) - np.where(d == 0, 1.0, 0.0)
        ds = 128 + t - tp
        out[4 + g] = np.where((ds >= 1) & (ds < w), 1.0 / w, 0.0)
        cnt = np.where(t >= 112, np.minimum(t - 111, w), w).astype(np.float32)
        special = np.where((d >= 0) & (d < w), 1.0 / cnt, 0.0) - np.where(d == 0, 1.0, 0.0)
        out[8 + g] = special if role == 0 else out[g]
        out[12 + g] = out[g] if role == 0 else special
    return np.ascontiguousarray(out.transpose(1, 0, 2).reshape(128, 16 * 128))


def core_map(inp, b, role, ntok, nsteps):
    m = {k[:-1] + "0": v for k, v in _layer_map(inp, role).items()}
    m["consts"] = _consts(role)
    fl = np.zeros((128, 4), np.float32)
    fl[:, 0] = float(role)
    m["flags"] = fl
    m["bands"] = _bands(role)
    m["xT"] = make_xT(inp, b, ntok, nsteps) if role == 0 else np.zeros((D, nsteps * TT), np.float32)
    return m


def kernel(**inputs):
    nc = _get_nc(NSTEP)
    in_maps = [core_map(inputs, core // 2, core % 2, SEQ, NSTEP) for core in range(8)]
    res = run_bass_kernel_spmd(nc, in_maps, core_ids=list(range(8)))
    out = np.stack([np.ascontiguousarray(res.results[2 * b + 1]["outT"][:, 3 * TT:NSTEP * TT].T) for b in range(BATCH)],
                   axis=0)
    return out.astype(np.float32)
```
